# Optimizing a Trainium2 kernel written in Bass

```python
import math
import jax, jax.numpy as jnp
from jax import lax
import numpy as np

D_MODEL = 2048
BATCH = 8
SEQ = 4096
DEPTH = 4

CTX_LEN = 256
GRID_W = 64
N_MOD = 9
FFN_DIM = 5632
HEAD_DIM = 128
NA_HEADS = (D_MODEL // 2) // HEAD_DIM
NA_WIN_ROWS = 8
NA_WIN_COLS = 16
NA_QCOLS = 16
DIFF_HEADS = (D_MODEL // 2) // (2 * HEAD_DIM)
DIFF_QBLOCK = 128
ROPE_THETA = 10000.0
SSM_INNER = 2 * D_MODEL
SSM_HEADDIM = 64
SSM_HEADS = SSM_INNER // SSM_HEADDIM
SSM_STATE = 128
SSM_GROUPS = 8
SSM_CONV = 4
SSM_CHUNK = 128
SSM_CONV_CH = SSM_INNER + 2 * SSM_GROUPS * SSM_STATE
SSM_IN_W = SSM_INNER + SSM_CONV_CH + 2 * SSM_HEADS
ATTN_IN_W = 3 * NA_HEADS * HEAD_DIM + 3 * DIFF_HEADS * 2 * HEAD_DIM
N_EVEN = (DEPTH + 1) // 2
N_ODD = DEPTH // 2
NEG_BIG = -1e30

kernel_name = "hybrid_natten_diffattn_ssd_dit_trunk"


def rmsnorm(x, g, eps=1e-6):
    xf = x.astype(jnp.float32)
    y = xf * lax.rsqrt(jnp.mean(xf * xf, axis=-1, keepdims=True) + eps)
    return y.astype(x.dtype) * g


def modulate(x, g, shift, scale):
    return rmsnorm(x, g) * (1 + scale) + shift


def swiglu(h, w1, w3, w2):
    return (jax.nn.silu(h @ w1) * (h @ w3)) @ w2


def axial_rope_tables(seq, dtype):
    quarter = HEAD_DIM // 4
    inv = 1.0 / (ROPE_THETA ** (jnp.arange(quarter, dtype=jnp.float32) / quarter))
    t = jnp.arange(seq)
    row = (t // GRID_W).astype(jnp.float32)[:, None] * inv
    col = (t % GRID_W).astype(jnp.float32)[:, None] * inv
    ang = jnp.concatenate([row, row, col, col], axis=-1)
    return jnp.cos(ang).astype(dtype), jnp.sin(ang).astype(dtype)


def apply_axial_rope(x, cos, sin):
    xr = x.reshape(x.shape[:-1] + (2, 2, HEAD_DIM // 4))
    rot = jnp.stack([-xr[..., 1, :], xr[..., 0, :]], axis=-2).reshape(x.shape)
    return x * cos[:, None, None, :] + rot * sin[:, None, None, :]


def na_static(kr):
    ncb = GRID_W // NA_QCOLS
    span = NA_QCOLS + NA_WIN_COLS
    cb = np.arange(ncb) * NA_QCOLS
    g0 = np.clip(cb - NA_WIN_COLS // 2, 0, GRID_W - span)
    key_cols = g0[:, None] + np.arange(span)[None, :]
    q_cols = cb[:, None] + np.arange(NA_QCOLS)[None, :]
    cs = np.clip(q_cols - NA_WIN_COLS // 2, 0, GRID_W - NA_WIN_COLS)
    kc = key_cols[:, None, :]
    ok = (kc >= cs[:, :, None]) & (kc < cs[:, :, None] + NA_WIN_COLS)
    coff = np.clip(kc - q_cols[:, :, None], -(NA_WIN_COLS - 1), NA_WIN_COLS - 1) + NA_WIN_COLS - 1
    mask = np.broadcast_to(ok[:, :, None, :], (ncb, NA_QCOLS, kr, span)).reshape(ncb, NA_QCOLS, kr * span)
    return key_cols, coff, mask


def neighbourhood_attn(q, k, v, kc, vc, rpb):
    b, s, h, d = q.shape
    rows = s // GRID_W
    kr = min(NA_WIN_ROWS, rows)
    ncb = GRID_W // NA_QCOLS
    span = NA_QCOLS + NA_WIN_COLS
    key_cols, coff, mask = na_static(kr)
    qg = q.reshape(b, rows, GRID_W, h, d)
    kg = k.reshape(b, rows, GRID_W, h, d)
    vg = v.reshape(b, rows, GRID_W, h, d)
    scale = HEAD_DIM ** -0.5
    nw = kr * span

    def row_fn(r):
        qr = lax.dynamic_index_in_dim(qg, r, axis=1, keepdims=False).reshape(b, ncb, NA_QCOLS, h, d)
        rs = jnp.clip(r - kr // 2, 0, rows - kr)
        kw = lax.dynamic_slice_in_dim(kg, rs, kr, axis=1)[:, :, key_cols]
        vw = lax.dynamic_slice_in_dim(vg, rs, kr, axis=1)[:, :, key_cols]
        kw = kw.transpose(0, 2, 1, 3, 4, 5).reshape(b, ncb, nw, h, d)
        vw = vw.transpose(0, 2, 1, 3, 4, 5).reshape(b, ncb, nw, h, d)
        roff = rs + jnp.arange(kr) - r + (NA_WIN_ROWS - 1)
        bias = rpb[:, roff][:, :, coff]
        bias = bias.transpose(0, 2, 3, 1, 4).reshape(h, ncb, NA_QCOLS, nw).astype(jnp.float32)
        s_win = jnp.einsum('bnqhd,bnkhd->bhnqk', qr, kw).astype(jnp.float32) * scale + bias
        s_win = jnp.where(mask, s_win, NEG_BIG)
        s_ctx = jnp.einsum('bnqhd,bkhd->bhnqk', qr, kc).astype(jnp.float32) * scale
        p = jax.nn.softmax(jnp.concatenate([s_win, s_ctx], axis=-1), axis=-1).astype(v.dtype)
        o = (jnp.einsum('bhnqk,bnkhd->bnqhd', p[..., :nw], vw)
             + jnp.einsum('bhnqk,bkhd->bnqhd', p[..., nw:], vc))
        return o.reshape(b, GRID_W, h, d)

    out = lax.map(row_fn, jnp.arange(rows))
    return out.transpose(1, 0, 2, 3, 4).reshape(b, s, h * d)


def context_attn(q, k, v):
    s = jnp.einsum('bqhd,bkhd->bhqk', q, k).astype(jnp.float32) * HEAD_DIM ** -0.5
    p = jax.nn.softmax(s, axis=-1).astype(v.dtype)
    return jnp.einsum('bhqk,bkhd->bqhd', p, v)


def diff_attn(q, k, v, lam):
    s = jnp.einsum('bqhmd,bkhmd->bhmqk', q, k).astype(jnp.float32) * HEAD_DIM ** -0.5
    p = jax.nn.softmax(s, axis=-1)
    pd = (p[:, :, 0] - lam * p[:, :, 1]).astype(v.dtype)
    return jnp.einsum('bhqk,bkhe->bqhe', pd, v)


def hybrid_attention(hx, hc, w_in, w_out, rpb, lam_vecs, subln_g, lam_init, cos, sin, ctx_out):
    b, s, _ = hx.shape
    l = hc.shape[1]
    na_w = NA_HEADS * HEAD_DIM
    df_w = DIFF_HEADS * 2 * HEAD_DIM
    cuts = [na_w, 2 * na_w, 3 * na_w, 3 * na_w + df_w, 3 * na_w + 2 * df_w]

    def split(p):
        bb, n = p.shape[0], p.shape[1]
        aq, ak, av, dq, dk, dv = jnp.split(p, cuts, axis=-1)
        sa = (bb, n, NA_HEADS, HEAD_DIM)
        sd = (bb, n, DIFF_HEADS, 2, HEAD_DIM)
        return (aq.reshape(sa), ak.reshape(sa), av.reshape(sa),
                dq.reshape(sd), dk.reshape(sd), dv.reshape(bb, n, DIFF_HEADS, 2 * HEAD_DIM))

    aqx, akx, avx, dqx, dkx, dvx = split(hx @ w_in)
    aqc, akc, avc, dqc, dkc, dvc = split(hc @ w_in)

    na_x = neighbourhood_attn(aqx, akx, avx, akc, avc, rpb)

    lf = lam_vecs.astype(jnp.float32)
    lam = jnp.exp(jnp.sum(lf[0] * lf[1])) - jnp.exp(jnp.sum(lf[2] * lf[3])) + lam_init
    dqx = apply_axial_rope(dqx, cos, sin)
    dkx = apply_axial_rope(dkx, cos, sin)
    k_all = jnp.concatenate([dkx, dkc], axis=1)
    v_all = jnp.concatenate([dvx, dvc], axis=1)
    qb = dqx.reshape(b, s // DIFF_QBLOCK, DIFF_QBLOCK, DIFF_HEADS, 2, HEAD_DIM).swapaxes(0, 1)
    dfx = lax.map(lambda qq: diff_attn(qq, k_all, v_all, lam), qb)
    dfx = dfx.swapaxes(0, 1).reshape(b, s, DIFF_HEADS, 2 * HEAD_DIM)

    def diff_out(o):
        return (rmsnorm(o, subln_g) * (1 - lam_init)).reshape(o.shape[0], o.shape[1], df_w)

    out_x = jnp.concatenate([na_x, diff_out(dfx)], axis=-1) @ w_out
    if not ctx_out:
        return out_x, None
    na_c = context_attn(aqc, akc, avc).reshape(b, l, na_w)
    dfc = diff_attn(dqc, dkc, dvc, lam)
    out_c = jnp.concatenate([na_c, diff_out(dfc)], axis=-1) @ w_out
    return out_x, out_c


def depthwise_conv(x, w, bias):
    k = w.shape[0]
    left = (k - 1) // 2
    y = lax.conv_general_dilated(x, w[:, None, :], window_strides=(1,), padding=[(left, k - 1 - left)],
                                 dimension_numbers=('NWC', 'WIO', 'NWC'), feature_group_count=x.shape[-1])
    return y + bias


def ssd_chunked(X, a, Bm, Cm, h0, want_y):
    b, l, h, p = X.shape
    g, n = Bm.shape[-2:]
    j = h // g
    T = SSM_CHUNK
    c = l // T
    X = X.reshape(b, c, T, g, j, p)
    a_cs = jnp.cumsum(a.reshape(b, c, T, g, j), axis=2)
    Bc = Bm.astype(jnp.float32).reshape(b, c, T, g, n)
    Cc = Cm.astype(jnp.float32).reshape(b, c, T, g, n)
    decay_to_end = jnp.exp(a_cs[:, :, -1:] - a_cs)
    states = jnp.einsum('bcsgn,bcsgj,bcsgjp->bcgjpn', Bc, decay_to_end, X)
    chunk_decay = jnp.exp(a_cs[:, :, -1])

    def step(hc, inp):
        dec, st = inp
        return dec[..., None, None] * hc + st, hc

    h_fin, h_prev = lax.scan(step, h0.reshape(b, g, j, p, n),
                             (chunk_decay.transpose(1, 0, 2, 3), states.transpose(1, 0, 2, 3, 4, 5)))
    h_fin = h_fin.reshape(b, h, p, n)
    if not want_y:
        return None, h_fin
    h_prev = h_prev.transpose(1, 0, 2, 3, 4, 5)
    seg = a_cs[:, :, :, None] - a_cs[:, :, None, :]
    lower = np.tril(np.ones((T, T), dtype=bool))[None, None, :, :, None, None]
    Lmat = jnp.exp(jnp.where(lower, seg, -jnp.inf))
    cb = jnp.einsum('bclgn,bcsgn->bclsg', Cc, Bc)
    y_diag = jnp.einsum('bclsg,bclsgj,bcsgjp->bclgjp', cb, Lmat, X)
    y_off = jnp.einsum('bclgn,bcgjpn,bclgj->bclgjp', Cc, h_prev, jnp.exp(a_cs))
    return (y_diag + y_off).reshape(b, l, h, p), h_fin


def ssd_direction(xs, bm, cm, dt_raw, h0, a_log, dt_bias, d_skip, want_y):
    dt = jax.nn.softplus(dt_raw.astype(jnp.float32) + dt_bias.astype(jnp.float32))
    a = dt * (-jnp.exp(a_log.astype(jnp.float32)))
    xf = xs.astype(jnp.float32)
    y, h_fin = ssd_chunked(xf * dt[..., None], a, bm, cm, h0, want_y)
    if want_y:
        y = (y + d_skip.astype(jnp.float32)[:, None] * xf).astype(xs.dtype)
    return y, h_fin


def maybe_flip(t, rev):
    return jnp.flip(t, axis=1) if rev else t


def bidir_ssd(hx, hc, w_in, conv_w, conv_b, a_log, dt_bias, d_skip, norm_g, w_out, ctx_out):
    def prep(hh):
        bb, n = hh.shape[:2]
        z, xbc, dt = jnp.split(hh @ w_in, [SSM_INNER, SSM_INNER + SSM_CONV_CH], axis=-1)
        xbc = jax.nn.silu(depthwise_conv(xbc, conv_w, conv_b))
        xs, bm, cm = jnp.split(xbc, [SSM_INNER, SSM_INNER + SSM_GROUPS * SSM_STATE], axis=-1)
        return (z, xs.reshape(bb, n, SSM_HEADS, SSM_HEADDIM),
                bm.reshape(bb, n, SSM_GROUPS, SSM_STATE), cm.reshape(bb, n, SSM_GROUPS, SSM_STATE), dt)

    zx, xx, bx, cx, dtx = prep(hx)
    zc, xc, bc, cc, dtc = prep(hc)
    h0 = jnp.zeros((hx.shape[0], SSM_HEADS, SSM_HEADDIM, SSM_STATE), jnp.float32)
    ys_lat, ys_ctx = [], []
    for d in range(2):
        rev = d == 1
        sl = slice(d * SSM_HEADS, (d + 1) * SSM_HEADS)
        yc, hc_fin = ssd_direction(maybe_flip(xc, rev), maybe_flip(bc, rev), maybe_flip(cc, rev),
                                   maybe_flip(dtc[..., sl], rev), h0, a_log[d], dt_bias[d], d_skip[d], ctx_out)
        yx, _ = ssd_direction(maybe_flip(xx, rev), maybe_flip(bx, rev), maybe_flip(cx, rev),
                              maybe_flip(dtx[..., sl], rev), hc_fin, a_log[d], dt_bias[d], d_skip[d], True)
        ys_lat.append(maybe_flip(yx, rev))
        if ctx_out:
            ys_ctx.append(maybe_flip(yc, rev))

    def finish(ys, z):
        y = ys[0] + ys[1]
        bb, n = y.shape[:2]
        y = y.reshape(bb, n, SSM_INNER) * jax.nn.silu(z)
        y = rmsnorm(y.reshape(bb, n, SSM_GROUPS, SSM_INNER // SSM_GROUPS),
                    norm_g.reshape(SSM_GROUPS, SSM_INNER // SSM_GROUPS)).reshape(bb, n, SSM_INNER)
        return y @ w_out

    out_x = finish(ys_lat, zx)
    if not ctx_out:
        return out_x, None
    return out_x, finish(ys_ctx, zc)


def setup_inputs(seed: int = 0) -> dict:
    key = jax.random.key(seed)
    ks = jax.random.split(key, 24)
    D = D_MODEL
    f32 = jnp.float32

    def nrm(k, shape, s):
        return jax.random.normal(k, shape, f32) * s

    dt = jnp.exp(jax.random.uniform(ks[19], (N_ODD, 2, SSM_HEADS), f32, math.log(1e-3), math.log(1e-1)))
    return {
        "x": nrm(ks[0], (BATCH, SEQ, D), 1.0),
        "c": nrm(ks[1], (BATCH, D), 1.0),
        "ctx": nrm(ks[2], (BATCH, CTX_LEN, D), 1.0),
        "c_ctx": nrm(ks[3], (D,), 1.0),
        "mod_w": nrm(ks[4], (DEPTH, D, N_MOD * D), 0.5 * D ** -0.5),
        "mod_b": nrm(ks[5], (DEPTH, N_MOD * D), 0.02),
        "norm_g": 1.0 + nrm(ks[6], (DEPTH, 3, D), 0.02),
        "ffn_w1": nrm(ks[7], (DEPTH, 2, D, FFN_DIM), D ** -0.5),
        "ffn_w3": nrm(ks[8], (DEPTH, 2, D, FFN_DIM), D ** -0.5),
        "ffn_w2": nrm(ks[9], (DEPTH, 2, FFN_DIM, D), FFN_DIM ** -0.5),
        "attn_w_in": nrm(ks[10], (N_EVEN, D, ATTN_IN_W), D ** -0.5),
        "attn_w_out": nrm(ks[11], (N_EVEN, D, D), D ** -0.5),
        "na_rpb": nrm(ks[12], (N_EVEN, NA_HEADS, 2 * NA_WIN_ROWS - 1, 2 * NA_WIN_COLS - 1), 0.02),
        "diff_lambda": nrm(ks[13], (N_EVEN, 4, HEAD_DIM), 0.1),
        "diff_subln_g": 1.0 + nrm(ks[14], (N_EVEN, 2 * HEAD_DIM), 0.02),
        "ssm_w_in": nrm(ks[15], (N_ODD, D, SSM_IN_W), D ** -0.5),
        "ssm_conv_w": nrm(ks[16], (N_ODD, SSM_CONV, SSM_CONV_CH), SSM_CONV ** -0.5),
        "ssm_conv_b": nrm(ks[17], (N_ODD, SSM_CONV_CH), 0.02),
        "ssm_a_log": jnp.log(jax.random.uniform(ks[18], (N_ODD, 2, SSM_HEADS), f32, 1.0, 16.0)),
        "ssm_dt_bias": dt + jnp.log(-jnp.expm1(-dt)),
        "ssm_d": 1.0 + nrm(ks[20], (N_ODD, 2, SSM_HEADS), 0.02),
        "ssm_norm_g": 1.0 + nrm(ks[21], (N_ODD, SSM_INNER), 0.02),
        "ssm_w_out": nrm(ks[22], (N_ODD, SSM_INNER, D), SSM_INNER ** -0.5),
        "final_norm_g": 1.0 + nrm(ks[23], (D,), 0.02),
    }


def reference(x, c, ctx, c_ctx, mod_w, mod_b, norm_g, ffn_w1, ffn_w3, ffn_w2, attn_w_in, attn_w_out,
              na_rpb, diff_lambda, diff_subln_g, ssm_w_in, ssm_conv_w, ssm_conv_b, ssm_a_log, ssm_dt_bias,
              ssm_d, ssm_norm_g, ssm_w_out, final_norm_g):
    b, s, d_model = x.shape
    cos, sin = axial_rope_tables(s, x.dtype)
    h = ctx
    for layer in range(DEPTH):
        ctx_out = layer < DEPTH - 1
        m = (jax.nn.silu(c) @ mod_w[layer] + mod_b[layer]).reshape(b, 1, N_MOD, d_model)
        mc = (jax.nn.silu(c_ctx) @ mod_w[layer] + mod_b[layer]).reshape(1, 1, N_MOD, d_model)
        w1a, w3a, w2a = ffn_w1[layer, 0], ffn_w3[layer, 0], ffn_w2[layer, 0]
        w1b, w3b, w2b = ffn_w1[layer, 1], ffn_w3[layer, 1], ffn_w2[layer, 1]

        x = x + 0.5 * m[:, :, 2] * swiglu(modulate(x, norm_g[layer, 0], m[:, :, 0], m[:, :, 1]), w1a, w3a, w2a)
        h = h + 0.5 * mc[:, :, 2] * swiglu(modulate(h, norm_g[layer, 0], mc[:, :, 0], mc[:, :, 1]), w1a, w3a, w2a)

        hx = modulate(x, norm_g[layer, 1], m[:, :, 3], m[:, :, 4])
        hc = modulate(h, norm_g[layer, 1], mc[:, :, 3], mc[:, :, 4])
        i = layer // 2
        if layer % 2 == 0:
            lam_init = 0.8 - 0.6 * math.exp(-0.3 * layer)
            ox, oc = hybrid_attention(hx, hc, attn_w_in[i], attn_w_out[i], na_rpb[i], diff_lambda[i],
                                      diff_subln_g[i], lam_init, cos, sin, ctx_out)
        else:
            ox, oc = bidir_ssd(hx, hc, ssm_w_in[i], ssm_conv_w[i], ssm_conv_b[i], ssm_a_log[i],
                               ssm_dt_bias[i], ssm_d[i], ssm_norm_g[i], ssm_w_out[i], ctx_out)
        x = x + m[:, :, 5] * ox

        x = x + 0.5 * m[:, :, 8] * swiglu(modulate(x, norm_g[layer, 2], m[:, :, 6], m[:, :, 7]), w1b, w3b, w2b)
        if ctx_out:
            h = h + mc[:, :, 5] * oc
            h = h + 0.5 * mc[:, :, 8] * swiglu(modulate(h, norm_g[layer, 2], mc[:, :, 6], mc[:, :, 7]), w1b, w3b, w2b)
    return rmsnorm(x, final_norm_g)
```

```python
import math
from contextlib import ExitStack
import numpy as np
import concourse.bass as bass
import concourse.mybir as mybir
from concourse.bass_utils import run_bass_kernel_spmd

F32 = mybir.dt.float32
BF16 = mybir.dt.bfloat16
AF = mybir.ActivationFunctionType
ALU = mybir.AluOpType
AX = mybir.AxisListType

D = 2048
DC = D // 128
CTX = 256
GRID_W = 64
N_MOD = 9
FFN = 5632
FC = FFN // 128
HD = 128
NA_H = 8
DF_H = 4
ATT_W = 6144
SSM_INNER = 4096
SSM_H = 64
SSM_P = 64
SSM_N = 128
SSM_G = 8
SSM_CONV_CH = SSM_INNER + 2 * SSM_G * SSM_N
SSM_IN_W = SSM_INNER + SSM_CONV_CH + 2 * SSM_H
EPS = 1e-6


class Res:
    __slots__ = ("name", "w", "r", "dsem", "dkey")

    def __init__(self, name):
        self.name = name
        self.w = {}
        self.r = {}
        self.dsem = None
        self.dkey = None


class Prog:
    ENG = ("pe", "act", "dve", "pool", "sp")

    def __init__(self, n_dma_sems=80):
        self.nc = bass.Bass("TRN2", target_bir_lowering=False)
        nc = self.nc
        self.es = ExitStack()
        self.eng = {"pe": nc.tensor, "act": nc.scalar, "dve": nc.vector, "pool": nc.gpsimd, "sp": nc.sync}
        self.sems = {}
        self.cnt = {}
        for e in self.ENG:
            self.sems["E" + e] = self.es.enter_context(nc.semaphore("E" + e))
            self.cnt["E" + e] = 0
        self.dfree = []
        for i in range(n_dma_sems):
            k = "D%d" % i
            self.sems[k] = self.es.enter_context(nc.semaphore(k))
            self.cnt[k] = 0
            self.dfree.append(k)
        self.seen = {e: {} for e in self.ENG}
        self.n_ins = 0
        self.n_wait = 0
        self.bg = set()

    def sb(self, stack, name, shape, dt):
        self.n_sb = getattr(self, "n_sb", 0) + 1
        name = "%s_%d" % (name, self.n_sb)
        t = stack.enter_context(self.nc.sbuf_tensor(name, list(shape), dt))
        r = Res(name)
        return t, r

    def dsem_for(self, res):
        if res.dkey is None:
            res.dkey = self.dfree.pop()
        return res.dkey

    def release(self, res_list):
        for r in res_list:
            if r.dkey is not None:
                self.dfree.append(r.dkey)
                r.dkey = None

    @staticmethod
    def _merge(dst, src):
        for k, v in src.items():
            if dst.get(k, 0) < v:
                dst[k] = v

    def _need(self, reads, writes):
        need = {}
        for r in reads:
            self._merge(need, r.w)
        for w in writes:
            self._merge(need, w.w)
            self._merge(need, w.r)
        return need

    def _waits(self, e, need):
        seen = self.seen[e]
        own = "E" + e
        lst = []
        for k, v in need.items():
            if e == "pe" and k == own:
                continue
            if seen.get(k, 0) < v:
                lst.append((k, v))
                seen[k] = v
        return lst

    def _commit(self, key, reads, writes):
        ev = {key: self.cnt[key]}
        for r in reads:
            self._merge(r.r, ev)
        for w in writes:
            w.w = dict(ev)
            w.r = {}

    def op(self, e, fn, reads=(), writes=()):
        need = self._need(reads, writes)
        lst = self._waits(e, need)
        eng = self.eng[e]
        for (k, v) in lst[1:]:
            eng.wait_ge(self.sems[k], v)
            self.n_wait += 1
        ins = fn()
        if lst:
            ins._wait_ge(self.sems[lst[0][0]], lst[0][1])
        key = "E" + e
        self.cnt[key] += 1
        ins.then_inc(self.sems[key], 1)
        self.n_ins += 1
        self._commit(key, reads, writes)
        return ins

    def group(self, e, fns, reads=(), writes=()):
        need = self._need(reads, writes)
        lst = self._waits(e, need)
        eng = self.eng[e]
        for (k, v) in lst[1:]:
            eng.wait_ge(self.sems[k], v)
            self.n_wait += 1
        ins = None
        for i, fn in enumerate(fns):
            ins = fn()
            if i == 0 and lst:
                ins._wait_ge(self.sems[lst[0][0]], lst[0][1])
            self.n_ins += 1
        key = "E" + e
        self.cnt[key] += 1
        ins.then_inc(self.sems[key], 1)
        self._commit(key, reads, writes)

    def dma(self, q, out, in_, reads, writes, owner, **kw):
        need = self._need(reads, writes)
        lst = self._waits(q, need)
        eng = self.eng[q]
        for (k, v) in lst:
            eng.wait_ge(self.sems[k], v)
            self.n_wait += 1
        key = self.dsem_for(owner)
        ins = eng.dma_start(out=out, in_=in_, **kw)
        self.cnt[key] += 16
        ins.then_inc(self.sems[key], 16)
        self.n_ins += 1
        self._commit(key, reads, writes)

    def barrier(self):
        for e in self.ENG:
            seen = self.seen[e]
            for k, v in self.cnt.items():
                if v == 0 or k == "E" + e or k in self.bg:
                    continue
                if seen.get(k, 0) < v:
                    self.eng[e].wait_ge(self.sems[k], v)
                    seen[k] = v
                    self.n_wait += 1

    def finish(self):
        for k, v in self.cnt.items():
            if v and self.seen["sp"].get(k, 0) < v and k != "Esp":
                self.eng["sp"].wait_ge(self.sems[k], v)
                self.seen["sp"][k] = v


class Cfg:
    def __init__(self, seq=4096, depth=4):
        self.S = seq
        self.L = CTX
        self.T = seq + CTX
        self.NT = self.T // 128
        self.depth = depth
        self.rows = seq // GRID_W


def blocks_of(cfg, tb):
    out = []
    t = 0
    while t < cfg.S:
        n = min(tb, cfg.S - t)
        out.append((t, n, 0))
        t += n
    t = cfg.S
    while t < cfg.T:
        n = min(tb, cfg.T - t)
        out.append((t, n, 1))
        t += n
    return out


class Builder:
    def __init__(self, cfg, mixers=True):
        self.cfg = cfg
        self.mixers = mixers
        self.P = Prog()
        self.nc = self.P.nc
        self.dram_in = {}
        self.dres = {}

    def din(self, name, shape, dt=F32):
        t = self.nc.dram_tensor(name, list(shape), dt, kind="ExternalInput").ap()
        self.dram_in[name] = t
        self.dres[name] = [Res(name)]
        return t

    def dscr(self, name, shape, dt, ntiles=1, kind="Internal"):
        t = self.nc.dram_tensor(name, list(shape), dt, kind=kind).ap()
        self.dres[name] = [Res("%s.%d" % (name, i)) for i in range(ntiles)]
        return t

    def tr(self, name, t0=None, n=None):
        rs = self.dres[name]
        if t0 is None or len(rs) == 1:
            return rs
        return rs[t0 // 128:(t0 + n + 127) // 128]

    def build(self):
        cfg, P, nc = self.cfg, self.P, self.nc
        S, L, T, NT = cfg.S, cfg.L, cfg.T, cfg.NT
        depth = cfg.depth
        n_even = (depth + 1) // 2
        n_odd = depth // 2
        x = self.din("x", [S, D])
        ctx = self.din("ctx", [L, D])
        cT = self.din("cT", [128, DC, 2])
        mod_w = self.din("mod_w", [depth, D, N_MOD * D])
        mod_bT = self.din("mod_bT", [depth, 128, N_MOD * DC])
        norm_gT = self.din("norm_gT", [depth, 128, 3 * DC])
        ffn_w1 = self.din("ffn_w1", [depth, 2, D, FFN])
        ffn_w3 = self.din("ffn_w3", [depth, 2, D, FFN])
        ffn_w2 = self.din("ffn_w2", [depth, 2, FFN, D])
        fin_g = self.din("fin_g", [128, D])
        if n_even:
            attn_w_in = self.din("attn_w_in", [n_even, D, ATT_W])
            attn_w_rot = self.din("attn_w_rot", [n_even, D, 2048])
            attn_w_out = self.din("attn_w_out", [n_even, D, D])
            na_bias = self.din("na_bias", [n_even, NA_H, 15, 64, 64])
            na_cmask = self.din("na_cmask", [128, 4, 64])
            lamT = self.din("lamT", [n_even, 128, 4])
            subg = self.din("subg", [n_even, 128, 2])
            ropec = self.din("ropec", [128, T])
            ropes = self.din("ropes", [128, T])
            self.ain = dict(w_in=attn_w_in, w_rot=attn_w_rot, w_out=attn_w_out, na_bias=na_bias,
                            na_cmask=na_cmask, lamT=lamT, subg=subg, ropec=ropec, ropes=ropes)
            self.awb = self.dscr("awb", [D, ATT_W], BF16)
            self.arb = self.dscr("arb", [D, 2048], BF16)
            self.aob = self.dscr("aob", [D, D], BF16)
            self.QKT = self.dscr("QKT", [32, 128, T], BF16)
            self.VTM = self.dscr("VTM", [T, 2048], BF16)
            self.CAT = self.dscr("CAT", [D, T], BF16)
        ident = self.din("ident", [128, 128])
        if n_odd:
            self.sin = dict(
                w_in=self.din("ssm_w_in", [n_odd, D, SSM_IN_W]),
                w_out=self.din("ssm_w_out", [n_odd, SSM_INNER, D]),
                conv_wT=self.din("conv_wT", [n_odd, 128, 48, 4]),
                conv_bT=self.din("conv_bT", [n_odd, 128, 48]),
                a_log=self.din("ssm_a_log", [n_odd, 128, 128]),
                dt_bias=self.din("ssm_dt_bias", [n_odd, 128, 128]),
                d_skip=self.din("ssm_d", [n_odd, 128, 128]),
                norm_gT=self.din("ssm_norm_gT", [n_odd, 128, 32]),
                masks=self.din("ssm_masks", [128, 4, 128]))
            self.swb = self.dscr("swb", [D, SSM_IN_W], BF16)
            self.sob = self.dscr("sob", [SSM_INNER, D], BF16)
            self.SZ = self.dscr("SZ", [T, SSM_INNER], F32)
            self.XBC = self.dscr("XBC", [48, 128, T], F32)
            self.DTR = self.dscr("DTR", [T, 128], F32)
            self.DTA = self.dscr("DTA", [T, 2, 128], F32)
            self.XS = self.dscr("XS", [T, SSM_INNER], F32)
            self.BCT = self.dscr("BCT", [16, 128, T], BF16)
            self.BTM = self.dscr("BTM", [T, SSM_G * SSM_N], BF16)
            self.YF = self.dscr("YF", [T, SSM_INNER], F32)
            self.YT = self.dscr("YT", [SSM_INNER, T], BF16)
        out = self.dscr("out", [S, D], F32, ntiles=1, kind="ExternalOutput")
        self.out = out
        XT = self.dscr("XT", [D, T], F32, ntiles=NT)
        self.XT = XT
        w1b = self.dscr("w1b", [2, D, FFN], BF16)
        w3b = self.dscr("w3b", [2, D, FFN], BF16)
        w2b = self.dscr("w2b", [2, FFN, D], BF16)

        es = P.es
        self.ps = []
        self.psr = []
        for i in range(8):
            t = es.enter_context(nc.psum_tensor("ps%d" % i, [128, 512], F32))
            self.ps.append(t)
            self.psr.append(Res("ps%d" % i))
        self.identf, self.identf_r = P.sb(es, "identf", [128, 128], F32)
        self.identb, self.identb_r = P.sb(es, "identb", [128, 128], BF16)
        self.onesf, self.onesf_r = P.sb(es, "onesf", [128, 128], F32)
        self.onesb, self.onesb_r = P.sb(es, "onesb", [128, 128], BF16)
        self.modt, self.modt_r = P.sb(es, "modt", [128, N_MOD * DC, 2], F32)
        self.gs, self.gs_r = P.sb(es, "gs", [128, 3, DC, 2], F32)
        self.gate, self.gate_r = P.sb(es, "gate", [128, 3, DC, 2], F32)
        self.sc, self.sc_r = P.sb(es, "sc", [128, DC, 2], F32)

        P.dma("sp", self.identf[:], ident[:, :], self.tr("ident"), [self.identf_r], self.identf_r)
        P.op("dve", lambda: nc.vector.tensor_copy(out=self.identb[:], in_=self.identf[:]),
             [self.identf_r], [self.identb_r])
        P.op("dve", lambda: nc.vector.memset(self.onesf[:], 1.0), [], [self.onesf_r])
        P.op("dve", lambda: nc.vector.memset(self.onesb[:], 1.0), [], [self.onesb_r])
        self.eps_t, self.eps_r = P.sb(es, "epsc", [128, 1], F32)
        P.op("dve", lambda: nc.vector.memset(self.eps_t[:], EPS), [], [self.eps_r])
        with ExitStack() as st:
            craw, craw_r = P.sb(st, "craw", [128, DC, 2], F32)
            P.dma("sp", craw[:], cT[:, :, :], self.tr("cT"), [craw_r], craw_r)
            P.op("act", lambda: nc.scalar.activation(out=self.sc[:], in_=craw[:], func=AF.Silu),
                 [craw_r], [self.sc_r])
            P.barrier()
            P.release([craw_r])

        self.stage_in_transpose(x, ctx)
        for layer in range(depth):
            self.stage_wcast_ffn(layer, ffn_w1, ffn_w3, ffn_w2, w1b, w3b, w2b)
            self.stage_mod(layer, mod_w, mod_bT, norm_gT)
            self.stage_ffn(layer, 0, w1b, w3b, w2b)
            if self.mixers:
                ctx_out = layer < depth - 1
                if layer % 2 == 0:
                    self.stage_attn(layer, ctx_out)
                else:
                    self.stage_ssd(layer, ctx_out)
            self.stage_ffn(layer, 1, w1b, w3b, w2b)
        self.stage_final(fin_g)
        P.barrier()
        P.finish()
        return nc

    def stage_in_transpose(self, x, ctx):
        cfg, P, nc = self.cfg, self.P, self.nc
        XTv = self.XT.rearrange("(c p) t -> p c t", p=128)
        with ExitStack() as st:
            NB = 2
            xin = [P.sb(st, "xin%d" % i, [128, D], F32) for i in range(NB)]
            xo = [P.sb(st, "xo%d" % i, [128, DC, 128], F32) for i in range(NB)]
            for i in range(cfg.NT):
                t0 = i * 128
                xi, xi_r = xin[i % NB]
                xoT, xo_r = xo[i % NB]
                src = x[t0:t0 + 128, :] if t0 < cfg.S else ctx[t0 - cfg.S:t0 - cfg.S + 128, :]
                srcres = self.tr("x") if t0 < cfg.S else self.tr("ctx")
                P.dma("sp", xi[:], src, srcres, [xi_r], xi_r)
                for q in range(DC // 4):
                    pb = (i * (DC // 4) + q) % 8
                    pst, psr = self.ps[pb], self.psr[pb]
                    P.group("pe", [
                        (lambda c=c, pst=pst: nc.tensor.transpose(
                            out=pst[:, (c % 4) * 128:(c % 4 + 1) * 128],
                            in_=xi[:, c * 128:(c + 1) * 128], identity=self.identf[:]))
                        for c in range(q * 4, q * 4 + 4)],
                        [xi_r, self.identf_r], [psr])
                    eng = "dve" if q % 2 == 0 else "act"
                    dst = xoT[:, q * 4:(q + 1) * 4, :]
                    srcp = pst[:, :].rearrange("p (c t) -> p c t", c=4)
                    if eng == "dve":
                        P.op("dve", lambda dst=dst, srcp=srcp: nc.vector.tensor_copy(out=dst, in_=srcp),
                             [psr], [xo_r])
                    else:
                        P.op("act", lambda dst=dst, srcp=srcp: nc.scalar.copy(out=dst, in_=srcp),
                             [psr], [xo_r])
                P.dma("sp", XTv[:, :, t0:t0 + 128], xoT[:], [xo_r], self.tr("XT", t0, 128), xo_r)
            P.barrier()
            P.release([r for _, r in xin] + [r for _, r in xo])

    def stage_wcast_ffn(self, layer, w1, w3, w2, w1b, w3b, w2b):
        for f in range(2):
            for (src, dst, K, name) in ((w1, w1b, D, "w1b"), (w3, w3b, D, "w3b"), (w2, w2b, FFN, "w2b")):
                self.wcast(src[layer, f], dst[f], K, name)

    def stage_mod(self, layer, mod_w, mod_bT, norm_gT):
        P, nc = self.P, self.nc
        NB = 512
        nblk = N_MOD * D // NB
        mwv = mod_w[layer].rearrange("(kt p) n -> p kt n", p=128)
        with ExitStack() as st:
            wt = [P.sb(st, "modw%d" % i, [128, DC, NB], F32) for i in range(2)]
            mb, mb_r = P.sb(st, "modb", [128, N_MOD * DC], F32)
            ng, ng_r = P.sb(st, "normg", [128, 3 * DC], F32)
            P.dma("sp", mb[:], mod_bT[layer], self.tr("mod_bT"), [mb_r], mb_r)
            P.dma("sp", ng[:], norm_gT[layer], self.tr("norm_gT"), [ng_r], ng_r)
            for b in range(nblk):
                w, w_r = wt[b % 2]
                for h in range(2):
                    P.dma("sp" if h == 0 else "act", w[:, h * 8:(h + 1) * 8, :],
                          mwv[:, h * 8:(h + 1) * 8, b * NB:(b + 1) * NB],
                          self.tr("mod_w"), [w_r], w_r)
                pb = b % 2
                pst, psr = self.ps[pb], self.psr[pb]
                fns = []
                for j in range(NB // 128):
                    for kt in range(DC):
                        fns.append(lambda j=j, kt=kt, w=w, pst=pst: nc.tensor.matmul(
                            pst[:, 2 * j:2 * j + 2], w[:, kt, j * 128:(j + 1) * 128], self.sc[:, kt, :],
                            start=(kt == 0), stop=(kt == DC - 1)))
                P.group("pe", fns, [w_r, self.sc_r], [psr])
                nj = NB // 128
                P.op("dve", lambda b=b, pst=pst: nc.vector.tensor_tensor(
                    out=self.modt[:, b * nj:(b + 1) * nj, :],
                    in0=pst[:, 0:2 * nj].rearrange("p (j s) -> p j s", s=2),
                    in1=mb[:, b * nj:(b + 1) * nj].unsqueeze(2).to_broadcast([128, nj, 2]),
                    op=ALU.add), [psr, mb_r], [self.modt_r])
            for i in range(3):
                sh = self.modt[:, (3 * i + 0) * DC:(3 * i + 1) * DC, :]
                scl = self.modt[:, (3 * i + 1) * DC:(3 * i + 2) * DC, :]
                gt = self.modt[:, (3 * i + 2) * DC:(3 * i + 3) * DC, :]
                P.op("dve", lambda i=i, scl=scl: nc.vector.scalar_tensor_tensor(
                    out=self.gs[:, i, :, :], in0=scl, scalar=1.0,
                    in1=ng[:, i * DC:(i + 1) * DC].unsqueeze(2).to_broadcast([128, DC, 2]),
                    op0=ALU.add, op1=ALU.mult), [self.modt_r, ng_r], [self.gs_r])
                fac = 1.0 if i == 1 else 0.5
                P.op("dve", lambda i=i, gt=gt, fac=fac: nc.vector.tensor_scalar(
                    out=self.gate[:, i, :, :], in0=gt, scalar1=fac, scalar2=None, op0=ALU.mult),
                    [self.modt_r], [self.gate_r])
            P.barrier()
            P.release([r for _, r in wt] + [mb_r, ng_r])

    def shift_ap(self, i, c, s):
        return self.modt[:, 3 * i * DC + c, s:s + 1]

    def norm_mod(self, st_tiles, i, t0, n, s, hT, hT_r):
        P, nc = self.P, self.nc
        XTv = self.XT.rearrange("(c p) t -> p c t", p=128)
        xs, sq, rstd, tmp = st_tiles
        (rstd_t, rstd_r) = rstd
        pst, psr = self.ps[7], self.psr[7]
        xres = self.tr("XT", t0, n)
        fns = []
        for c in range(DC):
            xt, xt_r = xs[c % len(xs)]
            sqt, sq_r = sq[c % len(sq)]
            P.dma("sp", xt[:, :n], XTv[:, c, t0:t0 + n], xres, [xt_r], xt_r)
            P.op("act", lambda xt=xt, sqt=sqt: nc.scalar.activation(out=sqt[:, :n], in_=xt[:, :n], func=AF.Square),
                 [xt_r], [sq_r])
            P.group("pe", [lambda sqt=sqt, c=c: nc.tensor.matmul(
                pst[:, :n], self.onesf[:], sqt[:, :n], start=(c == 0), stop=(c == DC - 1))],
                [sq_r, self.onesf_r], [psr])
        P.op("act", lambda: nc.scalar.activation(out=rstd_t[:, :n], in_=pst[:, :n], func=AF.Sqrt,
                                                 bias=self.eps_t[:, 0:1], scale=1.0 / D), [psr, self.eps_r], [rstd_r])
        P.op("dve", lambda: nc.vector.reciprocal(out=rstd_t[:, :n], in_=rstd_t[:, :n]), [rstd_r], [rstd_r])
        for c in range(DC):
            xt, xt_r = xs[c % len(xs)]
            tt, tt_r = tmp[c % len(tmp)]
            P.dma("sp", xt[:, :n], XTv[:, c, t0:t0 + n], xres, [xt_r], xt_r)
            P.op("dve", lambda xt=xt, tt=tt, c=c: nc.vector.scalar_tensor_tensor(
                out=tt[:, :n], in0=xt[:, :n], scalar=self.gs[:, i, c, s:s + 1], in1=rstd_t[:, :n],
                op0=ALU.mult, op1=ALU.mult), [xt_r, rstd_r, self.gs_r], [tt_r])
            P.op("act", lambda tt=tt, c=c: nc.scalar.activation(
                out=hT[:, c, :n], in_=tt[:, :n], func=AF.Identity,
                bias=self.shift_ap(i, c, s), scale=1.0), [tt_r, self.modt_r], [hT_r])

    def norm_tiles(self, st, tb):
        P = self.P
        xs = [P.sb(st, "nx%d" % i, [128, tb], F32) for i in range(4)]
        sq = [P.sb(st, "nsq%d" % i, [128, tb], F32) for i in range(2)]
        rstd = P.sb(st, "nrstd", [128, tb], F32)
        tmp = [P.sb(st, "ntmp%d" % i, [128, tb], F32) for i in range(2)]
        return (xs, sq, rstd, tmp), [r for _, r in xs] + [r for _, r in sq] + [rstd[1]] + [r for _, r in tmp]

    def stage_ffn(self, layer, f, w1b, w3b, w2b):
        cfg, P, nc = self.cfg, self.P, self.nc
        TB = 512
        NW = 256
        slot = 0 if f == 0 else 2
        XTv = self.XT.rearrange("(c p) t -> p c t", p=128)
        w1v = w1b[f].rearrange("(kt p) n -> p kt n", p=128)
        w3v = w3b[f].rearrange("(kt p) n -> p kt n", p=128)
        w2v = w2b[f].rearrange("(kt p) n -> p kt n", p=128)
        with ExitStack() as st:
            ntl, ntl_res = self.norm_tiles(st, TB)
            hT, hT_r = P.sb(st, "hT", [128, DC, TB], BF16)
            gT, gT_r = P.sb(st, "gT", [128, FC, TB], BF16)
            wa = [P.sb(st, "wa%d" % i, [128, DC, NW], BF16) for i in range(2)]
            wb = [P.sb(st, "wb%d" % i, [128, DC, NW], BF16) for i in range(2)]
            wd = [P.sb(st, "wd%d" % i, [128, FC, NW], BF16) for i in range(2)]
            sil = [P.sb(st, "sil%d" % i, [128, TB], F32) for i in range(2)]
            xr = [P.sb(st, "xr%d" % i, [128, TB], F32) for i in range(2)]
            yo = [P.sb(st, "yo%d" % i, [128, TB], F32) for i in range(2)]
            it = 0
            it2 = 0
            for (t0, n, s) in blocks_of(cfg, TB):
                self.norm_mod(ntl, slot, t0, n, s, hT, hT_r)
                for fb in range(FFN // NW):
                    a, a_r = wa[fb % 2]
                    b, b_r = wb[fb % 2]
                    P.dma("sp", a[:], w1v[:, :, fb * NW:(fb + 1) * NW], self.tr("w1b"), [a_r], a_r)
                    P.dma("sp", b[:], w3v[:, :, fb * NW:(fb + 1) * NW], self.tr("w3b"), [b_r], b_r)
                    for j in range(NW // 128):
                        fc = fb * (NW // 128) + j
                        pa, pa_r = self.ps[(it % 2) * 2], self.psr[(it % 2) * 2]
                        pb, pb_r = self.ps[(it % 2) * 2 + 1], self.psr[(it % 2) * 2 + 1]
                        sl, sl_r = sil[it % 2]
                        it += 1
                        P.group("pe", [lambda kt=kt, a=a, j=j, pa=pa: nc.tensor.matmul(
                            pa[:, :n], a[:, kt, j * 128:(j + 1) * 128], hT[:, kt, :n],
                            start=(kt == 0), stop=(kt == DC - 1)) for kt in range(DC)],
                            [a_r, hT_r], [pa_r])
                        P.group("pe", [lambda kt=kt, b=b, j=j, pb=pb: nc.tensor.matmul(
                            pb[:, :n], b[:, kt, j * 128:(j + 1) * 128], hT[:, kt, :n],
                            start=(kt == 0), stop=(kt == DC - 1)) for kt in range(DC)],
                            [b_r, hT_r], [pb_r])
                        P.op("act", lambda pa=pa, sl=sl: nc.scalar.activation(out=sl[:, :n], in_=pa[:, :n], func=AF.Silu),
                             [pa_r], [sl_r])
                        P.op("dve", lambda pb=pb, sl=sl, fc=fc: nc.vector.tensor_tensor(
                            out=gT[:, fc, :n], in0=sl[:, :n], in1=pb[:, :n], op=ALU.mult),
                            [sl_r, pb_r], [gT_r])
                for db in range(D // NW):
                    w, w_r = wd[db % 2]
                    for h in range(2):
                        P.dma("sp", w[:, h * 22:(h + 1) * 22, :], w2v[:, h * 22:(h + 1) * 22, db * NW:(db + 1) * NW],
                              self.tr("w2b"), [w_r], w_r)
                    for j in range(NW // 128):
                        dc = db * (NW // 128) + j
                        py, py_r = self.ps[4 + it2 % 2], self.psr[4 + it2 % 2]
                        xrt, xr_r = xr[it2 % 2]
                        yot, yo_r = yo[it2 % 2]
                        it2 += 1
                        P.dma("sp", xrt[:, :n], XTv[:, dc, t0:t0 + n], self.tr("XT", t0, n), [xr_r], xr_r)
                        P.group("pe", [lambda kt=kt, w=w, j=j, py=py: nc.tensor.matmul(
                            py[:, :n], w[:, kt, j * 128:(j + 1) * 128], gT[:, kt, :n],
                            start=(kt == 0), stop=(kt == FC - 1)) for kt in range(FC)],
                            [w_r, gT_r], [py_r])
                        P.op("dve", lambda py=py, xrt=xrt, yot=yot, dc=dc: nc.vector.scalar_tensor_tensor(
                            out=yot[:, :n], in0=py[:, :n], scalar=self.gate[:, slot, dc, s:s + 1], in1=xrt[:, :n],
                            op0=ALU.mult, op1=ALU.add), [py_r, xr_r, self.gate_r], [yo_r])
                        P.dma("sp", XTv[:, dc, t0:t0 + n], yot[:, :n], [yo_r], self.tr("XT", t0, n), yo_r)
            P.barrier()
            P.release(ntl_res + [r for _, r in wa + wb + wd + xr + yo])

    def wcast(self, src2d, dst2d, K, name, rb=256):
        P = self.P
        if not hasattr(self, "wc_r"):
            self.wc_r = Res("wcast")
        for k0 in range(0, K, rb):
            P.dma("pool", dst2d[k0:k0 + rb, :], src2d[k0:k0 + rb, :], [], self.tr(name), self.wc_r)
        P.bg.add(self.wc_r.dkey)

    def lin_tiles(self, st, KT, NW=256, tag="l"):
        return [self.P.sb(st, "%sw%d" % (tag, i), [128, KT, NW], BF16) for i in range(2)]

    def lin_fm(self, wt, hT, hT_r, KT, wv, wname, cols, n, epi, NW=256, pbanks=(0, 1)):
        P, nc = self.P, self.nc
        c0, c1 = cols
        it = 0
        for bi, b0 in enumerate(range(c0, c1, NW)):
            w, w_r = wt[bi % 2]
            hk = KT // 2
            P.dma("sp", w[:, :hk, :], wv[:, :hk, b0:b0 + NW], self.tr(wname), [w_r], w_r)
            P.dma("sp", w[:, hk:, :], wv[:, hk:, b0:b0 + NW], self.tr(wname), [w_r], w_r)
            for j in range(NW // 128):
                pb = pbanks[it % len(pbanks)]
                it += 1
                ps, ps_r = self.ps[pb], self.psr[pb]
                P.group("pe", [lambda kt=kt, w=w, j=j, ps=ps: nc.tensor.matmul(
                    ps[:, :n], w[:, kt, j * 128:(j + 1) * 128], hT[:, kt, :n],
                    start=(kt == 0), stop=(kt == KT - 1)) for kt in range(KT)],
                    [w_r, hT_r], [ps_r])
                epi(ps, ps_r, (b0 - c0) // 128 + j)

    def stage_attn(self, layer, ctx_out):
        cfg, P, nc = self.cfg, self.P, self.nc
        i = layer // 2
        A = self.ain
        self.wcast(A["w_in"][i], self.awb, D, "awb")
        self.wcast(A["w_rot"][i], self.arb, D, "arb")
        self.wcast(A["w_out"][i], self.aob, D, "aob")
        self.stage_attn_inproj(layer, i)
        self.stage_na(layer, i, ctx_out)
        self.stage_diff(layer, i, ctx_out)
        self.stage_outproj(self.CAT, "CAT", DC, self.aob.rearrange("(kt p) n -> p kt n", p=128), "aob", ctx_out)

    def stage_attn_inproj(self, layer, i):
        cfg, P, nc = self.cfg, self.P, self.nc
        A = self.ain
        TB = 512
        wv = self.awb.rearrange("(kt p) n -> p kt n", p=128)
        rv = self.arb.rearrange("(kt p) n -> p kt n", p=128)
        with ExitStack() as st:
            ntl, ntl_res = self.norm_tiles(st, TB)
            hT, hT_r = P.sb(st, "ahT", [128, DC, TB], BF16)
            cosb, cos_r = P.sb(st, "cosb", [128, TB], F32)
            sinb, sin_r = P.sb(st, "sinb", [128, TB], F32)
            qo = [P.sb(st, "qo%d" % k, [128, TB], BF16) for k in range(3)]
            t1 = [P.sb(st, "rt1%d" % k, [128, TB], F32) for k in range(2)]
            t2 = [P.sb(st, "rt2%d" % k, [128, TB], F32) for k in range(2)]
            vw = [P.sb(st, "vw%d" % k, [128, DC, 512], BF16) for k in range(2)]
            vo = [P.sb(st, "vo%d" % k, [128, 512], BF16) for k in range(3)]
            wrot = [P.sb(st, "wrot%d" % k, [128, DC, 256], BF16) for k in range(2)]
            wlin = self.lin_tiles(st, DC, tag="ap")
            rel = [r for _, r in wlin]
            cnt = [0, 0, 0]
            for (t0, n, s) in blocks_of(cfg, TB):
                self.norm_mod(ntl, 1, t0, n, s, hT, hT_r)
                P.dma("sp", cosb[:, :n], A["ropec"][:, t0:t0 + n], self.tr("ropec"), [cos_r], cos_r)
                P.dma("sp", sinb[:, :n], A["ropes"][:, t0:t0 + n], self.tr("ropes"), [sin_r], sin_r)

                def epi_plain(ps, ps_r, j, t0=t0, n=n):
                    q, q_r = qo[cnt[0] % 3]
                    cnt[0] += 1
                    P.op("act", lambda: nc.scalar.copy(out=q[:, :n], in_=ps[:, :n]), [ps_r], [q_r])
                    P.dma("sp", self.QKT[j, :, t0:t0 + n], q[:, :n], [q_r], self.tr("QKT"), q_r)
                self.lin_fm(wlin, hT, hT_r, DC, wv, "awb", (0, 2048), n, epi_plain)

                for (c0, r0, ch0) in ((3072, 0, 16), (4096, 1024, 24)):
                    for bi in range(4):
                        w, w_r = wrot[cnt[1] % 2]
                        P.dma("sp", w[:], rv[:, :, r0 + bi * 256:r0 + (bi + 1) * 256], self.tr("arb"), [w_r], w_r)
                        wq, wq_r = vw[cnt[1] % 2]
                        cnt[1] += 1
                        P.dma("sp", wq[:, :, 0:256], wv[:, :, c0 + bi * 256:c0 + (bi + 1) * 256], self.tr("awb"), [wq_r], wq_r)
                        for j in range(2):
                            pa, pa_r = self.ps[2], self.psr[2]
                            pb, pb_r = self.ps[3], self.psr[3]
                            P.group("pe", [lambda kt=kt, wq=wq, j=j: nc.tensor.matmul(
                                pa[:, :n], wq[:, kt, j * 128:(j + 1) * 128], hT[:, kt, :n],
                                start=(kt == 0), stop=(kt == DC - 1)) for kt in range(DC)], [wq_r, hT_r], [pa_r])
                            P.group("pe", [lambda kt=kt, w=w, j=j: nc.tensor.matmul(
                                pb[:, :n], w[:, kt, j * 128:(j + 1) * 128], hT[:, kt, :n],
                                start=(kt == 0), stop=(kt == DC - 1)) for kt in range(DC)], [w_r, hT_r], [pb_r])
                            a, a_r = t1[cnt[2] % 2]
                            b, b_r = t2[cnt[2] % 2]
                            cnt[2] += 1
                            q, q_r = qo[cnt[0] % 3]
                            cnt[0] += 1
                            P.op("dve", lambda a=a: nc.vector.tensor_tensor(out=a[:, :n], in0=pa[:, :n], in1=cosb[:, :n], op=ALU.mult),
                                 [pa_r, cos_r], [a_r])
                            P.op("dve", lambda b=b: nc.vector.tensor_tensor(out=b[:, :n], in0=pb[:, :n], in1=sinb[:, :n], op=ALU.mult),
                                 [pb_r, sin_r], [b_r])
                            P.op("pool", lambda a=a, b=b, q=q: nc.gpsimd.tensor_tensor(out=q[:, :n], in0=a[:, :n], in1=b[:, :n], op=ALU.add),
                                 [a_r, b_r], [q_r])
                            ch = ch0 + bi * 2 + j
                            P.dma("sp", self.QKT[ch, :, t0:t0 + n], q[:, :n], [q_r], self.tr("QKT"), q_r)

                for (c0, o0) in ((2048, 0), (2560, 512), (5120, 1024), (5632, 1536)):
                    w, w_r = vw[cnt[1] % 2]
                    cnt[1] += 1
                    P.dma("sp", w[:, :8, :], wv[:, :8, c0:c0 + 512], self.tr("awb"), [w_r], w_r)
                    P.dma("sp", w[:, 8:, :], wv[:, 8:, c0:c0 + 512], self.tr("awb"), [w_r], w_r)
                    for tt in range(n // 128):
                        pv, pv_r = self.ps[4 + tt % 2], self.psr[4 + tt % 2]
                        P.group("pe", [lambda kt=kt, w=w, tt=tt, pv=pv: nc.tensor.matmul(
                            pv[:, :], hT[:, kt, tt * 128:(tt + 1) * 128], w[:, kt, :],
                            start=(kt == 0), stop=(kt == DC - 1)) for kt in range(DC)], [w_r, hT_r], [pv_r])
                        v, v_r = vo[cnt[0] % 3]
                        cnt[0] += 1
                        P.op("act", lambda v=v, pv=pv: nc.scalar.copy(out=v[:, :], in_=pv[:, :]), [pv_r], [v_r])
                        P.dma("sp", self.VTM[t0 + tt * 128:t0 + (tt + 1) * 128, o0:o0 + 512], v[:, :], [v_r],
                              self.tr("VTM"), v_r)
            P.barrier()
            P.release(ntl_res + rel + [cos_r, sin_r] + [r for _, r in qo + vw + vo + wrot])

    def stage_na(self, layer, i, ctx_out):
        cfg, P, nc = self.cfg, self.P, self.nc
        A = self.ain
        S, T, NT, rows = cfg.S, cfg.T, cfg.NT, cfg.rows
        kr = 8
        scale = HD ** -0.5
        CATv = self.CAT.rearrange("(c p) t -> c p t", p=128)
        with ExitStack() as st:
            cm, cm_r = P.sb(st, "cmask", [128, 4, 64], F32)
            P.dma("sp", cm[:], A["na_cmask"][:, :, :], self.tr("na_cmask"), [cm_r], cm_r)
            qT, q_r = P.sb(st, "naq", [128, S], BF16)
            kT, k_r = P.sb(st, "nak", [128, T], BF16)
            ve, ve_r = P.sb(st, "nave", [128, NT, 128], BF16)
            vod, vo_r = P.sb(st, "navo", [128, NT - 1, 128], BF16)
            Wm = [P.sb(st, "naW%d" % k, [128, 4, 64], F32) for k in range(8)]
            oT, o_r = P.sb(st, "naoT", [128, T], BF16)
            E = [P.sb(st, "naE%d" % k, [128, 256], F32) for k in range(2)]
            Pt = [P.sb(st, "naP%d" % k, [128, 384], BF16) for k in range(2)]
            rc = [P.sb(st, "narc%d" % k, [128, 64], F32) for k in range(2)]
            Pc, Pc_r = P.sb(st, "naPc", [128, 512], BF16)
            rcc, rcc_r = P.sb(st, "narcc", [128, 256], F32)
            cqT, cqT_r = P.sb(st, "nacq", [128, 256], BF16)
            for h in range(NA_H):
                P.dma("sp", qT[:], self.QKT[h, :, 0:S], self.tr("QKT"), [q_r], q_r)
                P.dma("sp", kT[:], self.QKT[8 + h, :, :], self.tr("QKT"), [k_r], k_r)
                P.dma("sp", ve[:], self.VTM[:, h * 128:(h + 1) * 128].rearrange("(j p) e -> p j e", p=128),
                      self.tr("VTM"), [ve_r], ve_r)
                P.dma("sp", vod[:], self.VTM[64:T - 64, h * 128:(h + 1) * 128].rearrange("(j p) e -> p j e", p=128),
                      self.tr("VTM"), [vo_r], vo_r)
                for dl in range(8):
                    W, W_r = Wm[dl]
                    dr0 = 7 - dl
                    src = A["na_bias"][i, h, dr0:dr0 + 8].rearrange("(a i2) kc qc -> (i2 kc) a qc", i2=2)
                    P.dma("sp", W[:], src, self.tr("na_bias"), [W_r], W_r)
                    P.op("act", lambda W=W: nc.scalar.activation(out=W[:], in_=W[:], func=AF.Exp), [W_r], [W_r])
                    P.op("dve", lambda W=W: nc.vector.tensor_tensor(out=W[:], in0=W[:], in1=cm[:], op=ALU.mult),
                         [W_r, cm_r], [W_r])
                for r in range(rows):
                    rs = min(max(r - kr // 2, 0), rows - kr)
                    dl = r - rs
                    W, W_r = Wm[dl]
                    ps, ps_r = self.ps[r % 2], self.psr[r % 2]
                    po, po_r = self.ps[2 + r % 2], self.psr[2 + r % 2]
                    Et, E_r = E[r % 2]
                    Pp, Pp_r = Pt[r % 2]
                    rct, rc_r = rc[r % 2]
                    qs = qT[:, r * 64:(r + 1) * 64]
                    k0 = rs * 64
                    fns = [lambda a=a: nc.tensor.matmul(ps[:, a * 64:(a + 1) * 64], kT[:, k0 + a * 128:k0 + (a + 1) * 128], qs,
                                                        start=True, stop=True) for a in range(4)]
                    fns += [lambda a=a: nc.tensor.matmul(ps[:, (4 + a) * 64:(5 + a) * 64], kT[:, S + a * 128:S + (a + 1) * 128], qs,
                                                         start=True, stop=True) for a in range(2)]
                    P.group("pe", fns, [k_r, q_r], [ps_r])
                    P.op("act", lambda: nc.scalar.activation(out=Et[:, :], in_=ps[:, 0:256], func=AF.Exp, scale=scale),
                         [ps_r], [E_r])
                    P.op("act", lambda: nc.scalar.activation(out=Pp[:, 256:384], in_=ps[:, 256:384], func=AF.Exp, scale=scale),
                         [ps_r], [Pp_r])
                    P.op("dve", lambda: nc.vector.tensor_tensor(out=Pp[:, 0:256], in0=Et[:, :],
                                                                in1=W[:].rearrange("p a q -> p (a q)"), op=ALU.mult),
                         [E_r, W_r], [Pp_r])
                    if rs % 2 == 0:
                        vt = [ve[:, rs // 2 + a, :] for a in range(4)]
                    else:
                        vt = [vod[:, (rs - 1) // 2 + a, :] for a in range(4)]
                    vt += [ve[:, S // 128 + a, :] for a in range(2)]
                    fns = [lambda a=a: nc.tensor.matmul(po[:, 0:64], vt[a], Pp[:, a * 64:(a + 1) * 64],
                                                        start=(a == 0), stop=(a == 5)) for a in range(6)]
                    fns += [lambda a=a: nc.tensor.matmul(po[:, 64:128], self.onesb[:], Pp[:, a * 64:(a + 1) * 64],
                                                         start=(a == 0), stop=(a == 5)) for a in range(6)]
                    P.group("pe", fns, [Pp_r, ve_r, vo_r, self.onesb_r], [po_r])
                    P.op("dve", lambda: nc.vector.reciprocal(out=rct[:, :], in_=po[:, 64:128]), [po_r], [rc_r])
                    P.op("dve", lambda: nc.vector.tensor_tensor(out=oT[:, r * 64:(r + 1) * 64], in0=po[:, 0:64], in1=rct[:, :],
                                                                op=ALU.mult), [po_r, rc_r], [o_r])
                if ctx_out:
                    ps, ps_r = self.ps[4], self.psr[4]
                    po, po_r = self.ps[5], self.psr[5]
                    P.dma("sp", cqT[:], self.QKT[h, :, S:T], self.tr("QKT"), [cqT_r], cqT_r)
                    P.group("pe", [lambda a=a: nc.tensor.matmul(ps[:, a * 256:(a + 1) * 256], kT[:, S + a * 128:S + (a + 1) * 128],
                                                                cqT[:], start=True, stop=True) for a in range(2)],
                            [k_r, cqT_r], [ps_r])
                    P.op("act", lambda: nc.scalar.activation(out=Pc[:, :], in_=ps[:, :], func=AF.Exp, scale=scale), [ps_r], [Pc_r])
                    fns = [lambda a=a: nc.tensor.matmul(po[:, 0:256], ve[:, S // 128 + a, :], Pc[:, a * 256:(a + 1) * 256],
                                                        start=(a == 0), stop=(a == 1)) for a in range(2)]
                    fns += [lambda a=a: nc.tensor.matmul(po[:, 256:512], self.onesb[:], Pc[:, a * 256:(a + 1) * 256],
                                                         start=(a == 0), stop=(a == 1)) for a in range(2)]
                    P.group("pe", fns, [Pc_r, ve_r, self.onesb_r], [po_r])
                    P.op("dve", lambda: nc.vector.reciprocal(out=rcc[:, :], in_=po[:, 256:512]), [po_r], [rcc_r])
                    P.op("dve", lambda: nc.vector.tensor_tensor(out=oT[:, S:T], in0=po[:, 0:256], in1=rcc[:, :], op=ALU.mult),
                         [po_r, rcc_r], [o_r])
                nst = T if ctx_out else S
                P.dma("sp", CATv[h, :, 0:nst], oT[:, 0:nst], [o_r], self.tr("CAT"), o_r)
            P.barrier()
            P.release([cm_r, q_r, k_r, ve_r, vo_r, o_r, cqT_r] + [r for _, r in Wm])

    def stage_diff(self, layer, i, ctx_out):
        cfg, P, nc = self.cfg, self.P, self.nc
        A = self.ain
        S, T, NT = cfg.S, cfg.T, cfg.NT
        scale = HD ** -0.5
        lam_init = 0.8 - 0.6 * math.exp(-0.3 * layer)
        CATv = self.CAT.rearrange("(c p) t -> c p t", p=128)
        with ExitStack() as st:
            lt, lt_r = P.sb(st, "lamt", [128, 4], F32)
            sg, sg_r = P.sb(st, "subg", [128, 2], F32)
            lp, lp_r = P.sb(st, "lamp", [128, 2], F32)
            nl, nl_r = P.sb(st, "neglam", [128, 1], F32)
            P.dma("sp", lt[:], A["lamT"][i], self.tr("lamT"), [lt_r], lt_r)
            P.dma("sp", sg[:], A["subg"][i], self.tr("subg"), [sg_r], sg_r)
            P.op("dve", lambda: nc.vector.tensor_tensor(out=lp[:, 0:1], in0=lt[:, 0:1], in1=lt[:, 1:2], op=ALU.mult), [lt_r], [lp_r])
            P.op("dve", lambda: nc.vector.tensor_tensor(out=lp[:, 1:2], in0=lt[:, 2:3], in1=lt[:, 3:4], op=ALU.mult), [lt_r], [lp_r])
            ps, ps_r = self.ps[6], self.psr[6]
            P.group("pe", [lambda: nc.tensor.matmul(ps[:, 0:2], self.onesf[:], lp[:, :], start=True, stop=True)],
                    [lp_r, self.onesf_r], [ps_r])
            P.op("act", lambda: nc.scalar.activation(out=lp[:, :], in_=ps[:, 0:2], func=AF.Exp), [ps_r], [lp_r])
            P.op("dve", lambda: nc.vector.scalar_tensor_tensor(out=nl[:, :], in0=lp[:, 1:2], scalar=-lam_init, in1=lp[:, 0:1],
                                                               op0=ALU.add, op1=ALU.subtract), [lp_r], [nl_r])
            P.op("dve", lambda: nc.vector.tensor_scalar(out=sg[:, :], in0=sg[:, :], scalar1=1.0 - lam_init, scalar2=None, op0=ALU.mult),
                 [sg_r], [sg_r])
            qT = [P.sb(st, "dq%d" % m, [128, T], BF16) for m in range(2)]
            kT = [P.sb(st, "dk%d" % m, [128, T], BF16) for m in range(2)]
            v, v_r = P.sb(st, "dv", [128, NT, 256], BF16)
            Em = [[P.sb(st, "dE%d%d" % (m, k), [128, 512], BF16) for k in range(2)] for m in range(2)]
            rec = [P.sb(st, "drec%d" % m, [128, 512], F32) for m in range(2)]
            oa = [P.sb(st, "doa%d" % e, [128, 512], F32) for e in range(2)]
            ob = [P.sb(st, "dob%d" % e, [128, 512], F32) for e in range(2)]
            sq, sq_r = P.sb(st, "dsq", [128, 512], F32)
            rstd, rstd_r = P.sb(st, "drstd", [128, 512], F32)
            oo = [P.sb(st, "doo%d" % e, [128, 512], BF16) for e in range(2)]
            qblocks = [(t0, n, s) for (t0, n, s) in blocks_of(cfg, 512) if s == 0 or ctx_out]
            for h in range(DF_H):
                for m in range(2):
                    P.dma("sp", qT[m][0][:], self.QKT[16 + 2 * h + m, :, :], self.tr("QKT"), [qT[m][1]], qT[m][1])
                    P.dma("sp", kT[m][0][:], self.QKT[24 + 2 * h + m, :, :], self.tr("QKT"), [kT[m][1]], kT[m][1])
                P.dma("sp", v[:], self.VTM[:, 1024 + h * 256:1024 + (h + 1) * 256].rearrange("(j p) e -> p j e", p=128),
                      self.tr("VTM"), [v_r], v_r)
                for (q0, nq, s) in qblocks:
                    ktiles = list(range(NT)) if s == 0 else list(range(S // 128, NT))
                    for ki, kt in enumerate(ktiles):
                        for m in range(2):
                            pss, pss_r = self.ps[6 + m], self.psr[6 + m]
                            Et, E_r = Em[m][ki % 2]
                            P.group("pe", [lambda m=m, kt=kt, pss=pss: nc.tensor.matmul(
                                pss[:, :nq], kT[m][0][:, kt * 128:(kt + 1) * 128], qT[m][0][:, q0:q0 + nq], start=True, stop=True)],
                                [kT[m][1], qT[m][1]], [pss_r])
                            P.op("act", lambda pss=pss, Et=Et: nc.scalar.activation(out=Et[:, :nq], in_=pss[:, :nq], func=AF.Exp, scale=scale),
                                 [pss_r], [E_r])
                            first, last = (ki == 0), (ki == len(ktiles) - 1)
                            P.group("pe", [
                                lambda m=m, kt=kt, Et=Et, first=first, last=last: nc.tensor.matmul(
                                    self.ps[2 * m][:, :nq], v[:, kt, 0:128], Et[:, :nq], start=first, stop=last),
                                lambda m=m, kt=kt, Et=Et, first=first, last=last: nc.tensor.matmul(
                                    self.ps[2 * m + 1][:, :nq], v[:, kt, 128:256], Et[:, :nq], start=first, stop=last),
                                lambda m=m, kt=kt, Et=Et, first=first, last=last: nc.tensor.matmul(
                                    self.ps[4 + m][:, :nq], self.onesb[:], Et[:, :nq], start=first, stop=last)],
                                [E_r, v_r, self.onesb_r], [self.psr[2 * m], self.psr[2 * m + 1], self.psr[4 + m]])
                    P.op("dve", lambda: nc.vector.reciprocal(out=rec[0][0][:, :nq], in_=self.ps[4][:, :nq]), [self.psr[4]], [rec[0][1]])
                    P.op("dve", lambda: nc.vector.reciprocal(out=rec[1][0][:, :nq], in_=self.ps[5][:, :nq]), [self.psr[5]], [rec[1][1]])
                    P.op("dve", lambda: nc.vector.tensor_scalar(out=rec[1][0][:, :nq], in0=rec[1][0][:, :nq], scalar1=nl[:, 0:1],
                                                                scalar2=None, op0=ALU.mult), [rec[1][1], nl_r], [rec[1][1]])
                    for e in range(2):
                        P.op("dve", lambda e=e: nc.vector.tensor_tensor(out=oa[e][0][:, :nq], in0=self.ps[e][:, :nq], in1=rec[0][0][:, :nq],
                                                                        op=ALU.mult), [self.psr[e], rec[0][1]], [oa[e][1]])
                        P.op("dve", lambda e=e: nc.vector.tensor_tensor(out=ob[e][0][:, :nq], in0=self.ps[2 + e][:, :nq], in1=rec[1][0][:, :nq],
                                                                        op=ALU.mult), [self.psr[2 + e], rec[1][1]], [ob[e][1]])
                        P.op("pool", lambda e=e: nc.gpsimd.tensor_tensor(out=oa[e][0][:, :nq], in0=oa[e][0][:, :nq], in1=ob[e][0][:, :nq],
                                                                         op=ALU.add), [oa[e][1], ob[e][1]], [oa[e][1]])
                    pst, pst_r = self.ps[6], self.psr[6]
                    for e in range(2):
                        P.op("act", lambda e=e: nc.scalar.activation(out=sq[:, :nq], in_=oa[e][0][:, :nq], func=AF.Square), [oa[e][1]], [sq_r])
                        P.group("pe", [lambda e=e: nc.tensor.matmul(pst[:, :nq], self.onesf[:], sq[:, :nq], start=(e == 0), stop=(e == 1))],
                                [sq_r, self.onesf_r], [pst_r])
                    P.op("act", lambda: nc.scalar.activation(out=rstd[:, :nq], in_=pst[:, :nq], func=AF.Sqrt, bias=self.eps_t[:, 0:1],
                                                             scale=1.0 / 256), [pst_r, self.eps_r], [rstd_r])
                    P.op("dve", lambda: nc.vector.reciprocal(out=rstd[:, :nq], in_=rstd[:, :nq]), [rstd_r], [rstd_r])
                    for e in range(2):
                        P.op("dve", lambda e=e: nc.vector.scalar_tensor_tensor(
                            out=oo[e][0][:, :nq], in0=oa[e][0][:, :nq], scalar=sg[:, e:e + 1], in1=rstd[:, :nq],
                            op0=ALU.mult, op1=ALU.mult), [oa[e][1], sg_r, rstd_r], [oo[e][1]])
                        P.dma("sp", CATv[8 + 2 * h + e, :, q0:q0 + nq], oo[e][0][:, :nq], [oo[e][1]], self.tr("CAT"), oo[e][1])
            P.barrier()
            P.release([lt_r, sg_r, v_r] + [r for _, r in qT + kT + oo])

    def stage_outproj(self, ACT_T, aname, KT, wv, wname, ctx_out):
        cfg, P, nc = self.cfg, self.P, self.nc
        TB = 512
        XTv = self.XT.rearrange("(c p) t -> p c t", p=128)
        av = ACT_T.rearrange("(c p) t -> p c t", p=128)
        with ExitStack() as st:
            aT, aT_r = P.sb(st, "opa", [128, KT, TB], BF16)
            xr = [P.sb(st, "opx%d" % k, [128, TB], F32) for k in range(2)]
            yo = [P.sb(st, "opy%d" % k, [128, TB], F32) for k in range(2)]
            wlin = self.lin_tiles(st, KT, tag="op")
            rel = [r for _, r in wlin]
            cnt = [0]
            for (t0, n, s) in blocks_of(cfg, TB):
                if s == 1 and not ctx_out:
                    continue
                hk = KT // 2
                P.dma("sp", aT[:, :hk, :n], av[:, :hk, t0:t0 + n], self.tr(aname), [aT_r], aT_r)
                P.dma("sp", aT[:, hk:, :n], av[:, hk:, t0:t0 + n], self.tr(aname), [aT_r], aT_r)

                def epi(ps, ps_r, j, t0=t0, n=n, s=s):
                    xrt, xr_r = xr[cnt[0] % 2]
                    yot, yo_r = yo[cnt[0] % 2]
                    cnt[0] += 1
                    P.dma("sp", xrt[:, :n], XTv[:, j, t0:t0 + n], self.tr("XT", t0, n), [xr_r], xr_r)
                    P.op("dve", lambda: nc.vector.scalar_tensor_tensor(
                        out=yot[:, :n], in0=ps[:, :n], scalar=self.gate[:, 1, j, s:s + 1], in1=xrt[:, :n],
                        op0=ALU.mult, op1=ALU.add), [ps_r, xr_r, self.gate_r], [yo_r])
                    P.dma("sp", XTv[:, j, t0:t0 + n], yot[:, :n], [yo_r], self.tr("XT", t0, n), yo_r)
                self.lin_fm(wlin, aT, aT_r, KT, wv, wname, (0, D), n, epi)
            P.barrier()
            P.release(rel + [aT_r] + [r for _, r in xr + yo])

    def lin_tm(self, wt, hT, hT_r, KT, wv, wname, cols, ntt, epi, pbanks=(4, 5)):
        P, nc = self.P, self.nc
        c0, c1 = cols
        it = 0
        for bi, b0 in enumerate(range(c0, c1, 512)):
            w, w_r = wt[bi % 2]
            hk = KT // 2
            P.dma("sp", w[:, :hk, :], wv[:, :hk, b0:b0 + 512], self.tr(wname), [w_r], w_r)
            P.dma("sp", w[:, hk:, :], wv[:, hk:, b0:b0 + 512], self.tr(wname), [w_r], w_r)
            for tt in range(ntt):
                pb = pbanks[it % len(pbanks)]
                it += 1
                ps, ps_r = self.ps[pb], self.psr[pb]
                P.group("pe", [lambda kt=kt, w=w, tt=tt, ps=ps: nc.tensor.matmul(
                    ps[:, :], hT[:, kt, tt * 128:(tt + 1) * 128], w[:, kt, :],
                    start=(kt == 0), stop=(kt == KT - 1)) for kt in range(KT)], [w_r, hT_r], [ps_r])
                epi(ps, ps_r, bi, tt)

    def stage_ssd(self, layer, ctx_out):
        i = layer // 2
        A = self.sin
        self.wcast(A["w_in"][i], self.swb, D, "swb")
        self.wcast(A["w_out"][i], self.sob, SSM_INNER, "sob")
        self.stage_ssd_inproj(layer, i)
        self.stage_ssd_conv(layer, i)
        self.stage_ssd_dt(layer, i)
        self.stage_ssd_scan(layer, i, 0, ctx_out)
        self.stage_ssd_scan(layer, i, 1, ctx_out)
        self.stage_outproj(self.YT, "YT", SSM_INNER // 128, self.sob.rearrange("(kt p) n -> p kt n", p=128), "sob", ctx_out)

    def stage_ssd_inproj(self, layer, i):
        cfg, P, nc = self.cfg, self.P, self.nc
        TB = 512
        wv = self.swb.rearrange("(kt p) n -> p kt n", p=128)
        with ExitStack() as st:
            ntl, ntl_res = self.norm_tiles(st, TB)
            hT, hT_r = P.sb(st, "shT", [128, DC, TB], BF16)
            wlin = self.lin_tiles(st, DC, tag="si")
            wtm = [P.sb(st, "stm%d" % k, [128, DC, 512], BF16) for k in range(2)]
            wdt, wdt_r = P.sb(st, "swdt", [128, DC, 128], BF16)
            zo = [P.sb(st, "szo%d" % k, [128, 512], F32) for k in range(3)]
            xo = [P.sb(st, "sxo%d" % k, [128, TB], F32) for k in range(3)]
            do = [P.sb(st, "sdo%d" % k, [128, 128], F32) for k in range(2)]
            cnt = [0, 0, 0]
            for (t0, n, s) in blocks_of(cfg, TB):
                self.norm_mod(ntl, 1, t0, n, s, hT, hT_r)

                def epi_z(ps, ps_r, cb, tt, t0=t0):
                    z, z_r = zo[cnt[0] % 3]
                    cnt[0] += 1
                    P.op("act", lambda: nc.scalar.activation(out=z[:, :], in_=ps[:, :], func=AF.Silu), [ps_r], [z_r])
                    P.dma("sp", self.SZ[t0 + tt * 128:t0 + (tt + 1) * 128, cb * 512:(cb + 1) * 512], z[:, :], [z_r],
                          self.tr("SZ"), z_r)
                self.lin_tm(wtm, hT, hT_r, DC, wv, "swb", (0, SSM_INNER), n // 128, epi_z)

                def epi_x(ps, ps_r, j, t0=t0, n=n):
                    xx, x_r = xo[cnt[1] % 3]
                    cnt[1] += 1
                    P.op("act", lambda: nc.scalar.copy(out=xx[:, :n], in_=ps[:, :n]), [ps_r], [x_r])
                    P.dma("sp", self.XBC[j, :, t0:t0 + n], xx[:, :n], [x_r], self.tr("XBC"), x_r)
                self.lin_fm(wlin, hT, hT_r, DC, wv, "swb", (SSM_INNER, SSM_INNER + SSM_CONV_CH), n, epi_x)

                P.dma("sp", wdt[:], wv[:, :, SSM_INNER + SSM_CONV_CH:SSM_IN_W], self.tr("swb"), [wdt_r], wdt_r)
                for tt in range(n // 128):
                    ps, ps_r = self.ps[6], self.psr[6]
                    P.group("pe", [lambda kt=kt, tt=tt: nc.tensor.matmul(
                        ps[:, 0:128], hT[:, kt, tt * 128:(tt + 1) * 128], wdt[:, kt, :],
                        start=(kt == 0), stop=(kt == DC - 1)) for kt in range(DC)], [wdt_r, hT_r], [ps_r])
                    dd, d_r = do[cnt[2] % 2]
                    cnt[2] += 1
                    P.op("dve", lambda dd=dd: nc.vector.tensor_copy(out=dd[:, :], in_=ps[:, 0:128]), [ps_r], [d_r])
                    P.dma("sp", self.DTR[t0 + tt * 128:t0 + (tt + 1) * 128, :], dd[:, :], [d_r], self.tr("DTR"), d_r)
            P.barrier()
            P.release(ntl_res + [r for _, r in wlin + wtm + zo + xo + do] + [wdt_r])

    def stage_ssd_conv(self, layer, i):
        cfg, P, nc = self.cfg, self.P, self.nc
        A = self.sin
        S, T, NT = cfg.S, cfg.T, cfg.NT
        with ExitStack() as st:
            cw, cw_r = P.sb(st, "cw", [128, 48, 4], F32)
            cb, cb_r = P.sb(st, "cb", [128, 48], F32)
            P.dma("sp", cw[:], A["conv_wT"][i], self.tr("conv_wT"), [cw_r], cw_r)
            P.dma("sp", cb[:], A["conv_bT"][i], self.tr("conv_bT"), [cb_r], cb_r)
            xin = [P.sb(st, "cxin%d" % k, [128, T], F32) for k in range(2)]
            acc = [P.sb(st, "cacc%d" % k, [128, T], F32) for k in range(2)]
            sf = [P.sb(st, "csf%d" % k, [128, T], F32) for k in range(2)]
            sbf = [P.sb(st, "csb%d" % k, [128, T], BF16) for k in range(2)]
            tmf = [P.sb(st, "ctmf%d" % k, [128, NT, 128], F32) for k in range(2)]
            tmb = [P.sb(st, "ctmb%d" % k, [128, NT, 128], BF16) for k in range(2)]
            for c in range(48):
                x, x_r = xin[c % 2]
                a, a_r = acc[c % 2]
                P.dma("sp", x[:], self.XBC[c, :, :], self.tr("XBC"), [x_r], x_r)
                for (lo, hi) in ((0, S), (S, T)):
                    P.op("act", lambda lo=lo, hi=hi: nc.scalar.activation(
                        out=a[:, lo:hi], in_=x[:, lo:hi], func=AF.Identity, bias=cb[:, c:c + 1], scale=cw[:, c, 1:2]),
                        [x_r, cw_r, cb_r], [a_r])
                    for (k, dlo, dhi, slo, shi) in ((0, lo + 1, hi, lo, hi - 1), (2, lo, hi - 1, lo + 1, hi), (3, lo, hi - 2, lo + 2, hi)):
                        P.op("dve", lambda k=k, dlo=dlo, dhi=dhi, slo=slo, shi=shi: nc.vector.scalar_tensor_tensor(
                            out=a[:, dlo:dhi], in0=x[:, slo:shi], scalar=cw[:, c, k:k + 1], in1=a[:, dlo:dhi],
                            op0=ALU.mult, op1=ALU.add), [x_r, a_r, cw_r], [a_r])
                if c < 32:
                    s_, s_r = sf[c % 2]
                    tm, tm_r = tmf[c % 2]
                    P.op("act", lambda: nc.scalar.activation(out=s_[:, :], in_=a[:, :], func=AF.Silu), [a_r], [s_r])
                    for q in range((NT + 3) // 4):
                        js = list(range(q * 4, min(q * 4 + 4, NT)))
                        pst, ps_r = self.ps[q % 4], self.psr[q % 4]
                        P.group("pe", [lambda j=j, pst=pst: nc.tensor.transpose(
                            out=pst[:, (j % 4) * 128:(j % 4 + 1) * 128], in_=s_[:, j * 128:(j + 1) * 128], identity=self.identf[:])
                            for j in js], [s_r, self.identf_r], [ps_r])
                        dst = tm[:, js[0]:js[-1] + 1, :]
                        src = pst[:, 0:len(js) * 128].rearrange("p (j e) -> p j e", e=128)
                        if q % 2 == 0:
                            P.op("dve", lambda dst=dst, src=src: nc.vector.tensor_copy(out=dst, in_=src), [ps_r], [tm_r])
                        else:
                            P.op("act", lambda dst=dst, src=src: nc.scalar.copy(out=dst, in_=src), [ps_r], [tm_r])
                    P.dma("sp", self.XS[:, c * 128:(c + 1) * 128].rearrange("(j p) e -> p j e", p=128), tm[:], [tm_r],
                          self.tr("XS"), tm_r)
                else:
                    s_, s_r = sbf[c % 2]
                    P.op("act", lambda: nc.scalar.activation(out=s_[:, :], in_=a[:, :], func=AF.Silu), [a_r], [s_r])
                    P.dma("sp", self.BCT[c - 32, :, :], s_[:, :], [s_r], self.tr("BCT"), s_r)
                    if c < 40:
                        tm, tm_r = tmb[c % 2]
                        for q in range((NT + 3) // 4):
                            js = list(range(q * 4, min(q * 4 + 4, NT)))
                            pst, ps_r = self.ps[4 + q % 4], self.psr[4 + q % 4]
                            pv = pst[:, :].bitcast(BF16)
                            P.group("pe", [lambda j=j, pv=pv: nc.tensor.transpose(
                                out=pv[:, (j % 4) * 128:(j % 4 + 1) * 128], in_=s_[:, j * 128:(j + 1) * 128], identity=self.identb[:])
                                for j in js], [s_r, self.identb_r], [ps_r])
                            dst = tm[:, js[0]:js[-1] + 1, :]
                            src = pv[:, 0:len(js) * 128].rearrange("p (j e) -> p j e", e=128)
                            P.op("dve", lambda dst=dst, src=src: nc.vector.tensor_copy(out=dst, in_=src), [ps_r], [tm_r])
                        g = c - 32
                        P.dma("sp", self.BTM[:, g * 128:(g + 1) * 128].rearrange("(j p) e -> p j e", p=128), tm[:], [tm_r],
                              self.tr("BTM"), tm_r)
            P.barrier()
            P.release([cw_r, cb_r] + [r for _, r in xin + acc + sf + sbf + tmf + tmb])

    def stage_ssd_dt(self, layer, i):
        cfg, P, nc = self.cfg, self.P, self.nc
        A = self.sin
        NT = cfg.NT
        with ExitStack() as st:
            x, x_r = P.sb(st, "dtx", [128, NT, 128], F32)
            ax, ax_r = P.sb(st, "dtax", [128, NT, 128], F32)
            bi, bi_r = P.sb(st, "dtb", [128, 128], F32)
            al, al_r = P.sb(st, "dtal", [128, 128], F32)
            P.dma("sp", x[:], self.DTR.rearrange("(j p) h -> p j h", p=128), self.tr("DTR"), [x_r], x_r)
            P.dma("sp", bi[:], A["dt_bias"][i], self.tr("ssm_dt_bias"), [bi_r], bi_r)
            P.dma("sp", al[:], A["a_log"][i], self.tr("ssm_a_log"), [al_r], al_r)
            bb = bi[:, :].unsqueeze(1).to_broadcast([128, NT, 128])
            P.op("dve", lambda: nc.vector.tensor_tensor(out=x[:], in0=x[:], in1=bb, op=ALU.add), [x_r, bi_r], [x_r])
            P.op("dve", lambda: nc.vector.scalar_tensor_tensor(out=ax[:], in0=x[:], scalar=-1.0, in1=x[:], op0=ALU.mult, op1=ALU.min),
                 [x_r], [ax_r])
            P.op("act", lambda: nc.scalar.activation(out=ax[:], in_=ax[:], func=AF.Exp), [ax_r], [ax_r])
            P.op("act", lambda: nc.scalar.activation(out=ax[:], in_=ax[:], func=AF.Ln, bias=self.onesf[:, 0:1], scale=1.0),
                 [ax_r, self.onesf_r], [ax_r])
            P.op("dve", lambda: nc.vector.scalar_tensor_tensor(out=x[:], in0=x[:], scalar=0.0, in1=ax[:], op0=ALU.max, op1=ALU.add),
                 [x_r, ax_r], [x_r])
            P.op("act", lambda: nc.scalar.activation(out=al[:], in_=al[:], func=AF.Exp), [al_r], [al_r])
            P.op("dve", lambda: nc.vector.scalar_tensor_tensor(out=ax[:], in0=x[:], scalar=-1.0,
                                                               in1=al[:, :].unsqueeze(1).to_broadcast([128, NT, 128]),
                                                               op0=ALU.mult, op1=ALU.mult), [x_r, al_r], [ax_r])
            P.dma("sp", self.DTA[:, 0, :].rearrange("(j p) h -> p j h", p=128), x[:], [x_r], self.tr("DTA"), x_r)
            P.dma("sp", self.DTA[:, 1, :].rearrange("(j p) h -> p j h", p=128), ax[:], [ax_r], self.tr("DTA"), ax_r)
            P.barrier()
            P.release([x_r, ax_r, bi_r, al_r])

    def stage_ssd_scan(self, layer, i, d, ctx_out):
        cfg, P, nc = self.cfg, self.P, self.nc
        A = self.sin
        S, T, NT = cfg.S, cfg.T, cfg.NT
        G = SSM_G
        nlat = S // 128
        if d == 0:
            order = list(range(nlat, NT)) + list(range(nlat))
        else:
            order = list(range(NT - 1, nlat - 1, -1)) + list(range(nlat - 1, -1, -1))
        with ExitStack() as st:
            mk, mk_r = P.sb(st, "smask", [128, 4, 128], F32)
            P.dma("sp", mk[:], A["masks"][:, :, :], self.tr("ssm_masks"), [mk_r], mk_r)
            m_le, m_gt = (mk[:, 0, :], mk[:, 1, :]) if d == 0 else (mk[:, 2, :], mk[:, 3, :])
            xs, xs_r = P.sb(st, "sxs", [128, 64, 64], F32)
            yac, yac_r = P.sb(st, "syac", [128, SSM_INNER], F32)
            xdt, xdt_r = P.sb(st, "sxdt", [128, 64, 64], BF16)
            xds, xds_r = P.sb(st, "sxds", [128, 64, 64], BF16)
            dta, dta_r = P.sb(st, "sdta", [128, 2, 128], F32)
            eq, eq_r = P.sb(st, "seq", [128, 192], F32)
            dd, dd_r = P.sb(st, "sdd", [128, 64], F32)
            btm, btm_r = P.sb(st, "sbtm", [128, G * 128], BF16)
            bct, bct_r = P.sb(st, "sbct", [128, 16, 128], BF16)
            cbm = [P.sb(st, "scbm%d" % k, [128, 128], F32) for k in range(2)]
            Rt = [P.sb(st, "sR%d" % k, [128, 8, 128], F32) for k in range(2)]
            Lh = [P.sb(st, "sLh%d" % k, [128, 8, 128], F32) for k in range(2)]
            Mt = [P.sb(st, "sM%d" % k, [128, 8, 128], BF16) for k in range(2)]
            tmp = [P.sb(st, "stmp%d" % k, [128, 8, 64], F32) for k in range(2)]
            hf, hf_r = P.sb(st, "shf", [128, G, 512], F32)
            hb = [P.sb(st, "shb%d" % g, [128, 512], BF16) for g in range(G)]
            P.op("dve", lambda: nc.vector.memset(hf[:], 0.0), [], [hf_r])
            for g in range(G):
                P.op("pool", lambda g=g: nc.gpsimd.memset(hb[g][0][:], 0.0), [], [hb[g][1]])
            if d == 1:
                yf, yf_r = P.sb(st, "syf", [128, SSM_INNER], F32)
                sz, sz_r = P.sb(st, "ssz", [128, SSM_INNER], F32)
                dsk, dsk_r = P.sb(st, "sdsk", [128, 128], F32)
                ngt, ng_r = P.sb(st, "sng", [128, 32], F32)
                ss, ss_r = P.sb(st, "sss", [128, 8], F32)
                ytr, ytr_r = P.sb(st, "sytr", [128, 32, 128], BF16)
                P.dma("sp", dsk[:], A["d_skip"][i], self.tr("ssm_d"), [dsk_r], dsk_r)
                P.dma("sp", ngt[:], A["norm_gT"][i], self.tr("ssm_norm_gT"), [ng_r], ng_r)
                P.op("dve", lambda: nc.vector.tensor_tensor(out=dsk[:, 0:64], in0=dsk[:, 0:64], in1=dsk[:, 64:128], op=ALU.add),
                     [dsk_r], [dsk_r])
            YTv = self.YT.rearrange("(c p) t -> p c t", p=128)
            for j in order:
                t0 = j * 128
                is_ctx = j >= nlat
                want_y = (not is_ctx) or ctx_out
                P.dma("sp", xs[:].rearrange("p h e -> p (h e)"), self.XS[t0:t0 + 128, :], self.tr("XS"), [xs_r], xs_r)
                P.dma("sp", dta[:], self.DTA[t0:t0 + 128, :, :], self.tr("DTA"), [dta_r], dta_r)
                P.dma("sp", btm[:], self.BTM[t0:t0 + 128, :], self.tr("BTM"), [btm_r], btm_r)
                P.dma("sp", bct[:], self.BCT[:, :, t0:t0 + 128].rearrange("c n t -> n c t"), self.tr("BCT"), [bct_r], bct_r)
                if d == 1 and want_y:
                    P.dma("sp", yf[:], self.YF[t0:t0 + 128, :], self.tr("YF"), [yf_r], yf_r)
                    P.dma("sp", sz[:], self.SZ[t0:t0 + 128, :], self.tr("SZ"), [sz_r], sz_r)
                dt_d = dta[:, 0, d * 64:(d + 1) * 64]
                a_d = dta[:, 1, d * 64:(d + 1) * 64]
                pc, pc_r = self.ps[7], self.psr[7]
                P.group("pe", [
                    lambda: nc.tensor.matmul(pc[:, 0:64], m_le, a_d, start=True, stop=True),
                    lambda: nc.tensor.matmul(pc[:, 64:128], m_gt, a_d, start=True, stop=True),
                    lambda: nc.tensor.matmul(pc[:, 128:192], self.onesf[:], a_d, start=True, stop=True)],
                    [mk_r, dta_r, self.onesf_r], [pc_r])
                P.op("act", lambda: nc.scalar.activation(out=eq[:, :], in_=pc[:, 0:192], func=AF.Exp), [pc_r], [eq_r])
                P.op("dve", lambda: nc.vector.tensor_tensor(out=dd[:, :], in0=dt_d, in1=eq[:, 64:128], op=ALU.mult), [dta_r, eq_r], [dd_r])
                if want_y:
                    P.op("dve", lambda: nc.vector.tensor_tensor(out=xdt[:], in0=xs[:], in1=dt_d.unsqueeze(2).to_broadcast([128, 64, 64]),
                                                                op=ALU.mult), [xs_r, dta_r], [xdt_r])
                P.op("pool", lambda: nc.gpsimd.tensor_tensor(out=xds[:], in0=xs[:], in1=dd[:, :].unsqueeze(2).to_broadcast([128, 64, 64]),
                                                             op=ALU.mult), [xs_r, dd_r], [xds_r])
                for g in range(G):
                    k2 = g % 2
                    if want_y:
                        pcb, pcb_r = self.ps[0], self.psr[0]
                        P.group("pe", [lambda g=g: nc.tensor.matmul(pcb[:, 0:128], bct[:, g, :], bct[:, 8 + g, :], start=True, stop=True)],
                                [bct_r], [pcb_r])
                        cm, cm_r = cbm[k2]
                        P.op("dve", lambda cm=cm: nc.vector.tensor_tensor(out=cm[:, :], in0=pcb[:, 0:128], in1=m_le, op=ALU.mult),
                             [pcb_r, mk_r], [cm_r])
                        R, R_r = Rt[k2]
                        P.op("pool", lambda R=R, g=g: nc.gpsimd.tensor_tensor(
                            out=R[:], in0=a_d[:, g * 8:(g + 1) * 8].unsqueeze(2).to_broadcast([128, 8, 128]),
                            in1=m_le.unsqueeze(1).to_broadcast([128, 8, 128]), op=ALU.mult), [dta_r, mk_r], [R_r])
                        L, L_r = Lh[k2]
                        for hh in range(2):
                            pd, pd_r = self.ps[1 + hh], self.psr[1 + hh]
                            P.group("pe", [lambda R=R, hh=hh, pd=pd: nc.tensor.matmul(
                                pd[:, :], m_gt, R[:, hh * 4:(hh + 1) * 4, :].rearrange("p a l -> p (a l)"), start=True, stop=True)],
                                [mk_r, R_r], [pd_r])
                            P.op("act", lambda L=L, hh=hh, pd=pd: nc.scalar.activation(
                                out=L[:, hh * 4:(hh + 1) * 4, :].rearrange("p a l -> p (a l)"), in_=pd[:, :], func=AF.Exp), [pd_r], [L_r])
                        M, M_r = Mt[k2]
                        P.op("dve", lambda M=M, L=L, cm=cm: nc.vector.tensor_tensor(
                            out=M[:], in0=L[:], in1=cm[:, :].unsqueeze(1).to_broadcast([128, 8, 128]), op=ALU.mult),
                            [L_r, cm_r], [M_r])
                        pyd, pyd_r = self.ps[3], self.psr[3]
                        P.group("pe", [lambda M=M, g=g, jh=jh: nc.tensor.matmul(
                            pyd[:, jh * 64:(jh + 1) * 64], M[:, jh, :], xdt[:, g * 8 + jh, :], start=True, stop=True) for jh in range(8)],
                            [M_r, xdt_r], [pyd_r])
                        pyo, pyo_r = self.ps[4], self.psr[4]
                        P.group("pe", [lambda g=g: nc.tensor.matmul(pyo[:, :], bct[:, 8 + g, :], hb[g][0][:, :], start=True, stop=True)],
                                [bct_r, hb[g][1]], [pyo_r])
                        tp, tp_r = tmp[k2]
                        P.op("dve", lambda tp=tp, g=g: nc.vector.tensor_tensor(
                            out=tp[:], in0=pyo[:, :].rearrange("p (a e) -> p a e", e=64),
                            in1=eq[:, g * 8:(g + 1) * 8].unsqueeze(2).to_broadcast([128, 8, 64]), op=ALU.mult), [pyo_r, eq_r], [tp_r])
                        P.op("dve", lambda tp=tp, g=g: nc.vector.tensor_tensor(
                            out=yac[:, g * 512:(g + 1) * 512], in0=pyd[:, :], in1=tp[:].rearrange("p a e -> p (a e)"), op=ALU.add),
                            [pyd_r, tp_r], [yac_r])
                    pst, pst_r = self.ps[5 + g % 2], self.psr[5 + g % 2]
                    P.group("pe", [lambda g=g, pst=pst: nc.tensor.matmul(
                        pst[:, :], btm[:, g * 128:(g + 1) * 128], xds[:, g * 8:(g + 1) * 8, :].rearrange("p a e -> p (a e)"),
                        start=True, stop=True)], [btm_r, xds_r], [pst_r])
                    hv = hf[:, g, :].rearrange("p (a e) -> p a e", e=64)
                    P.op("pool", lambda g=g, hv=hv: nc.gpsimd.tensor_tensor(
                        out=hv, in0=hv, in1=eq[:, 128 + g * 8:128 + (g + 1) * 8].unsqueeze(2).to_broadcast([128, 8, 64]), op=ALU.mult),
                        [hf_r, eq_r], [hf_r])
                    P.op("dve", lambda g=g, pst=pst: nc.vector.tensor_tensor(out=hf[:, g, :], in0=hf[:, g, :], in1=pst[:, :], op=ALU.add),
                         [hf_r, pst_r], [hf_r])
                    P.op("act", lambda g=g: nc.scalar.copy(out=hb[g][0][:, :], in_=hf[:, g, :]), [hf_r], [hb[g][1]])
                if not want_y:
                    continue
                if d == 0:
                    P.dma("sp", self.YF[t0:t0 + 128, :], yac[:, :], [yac_r], self.tr("YF"), yac_r)
                    continue
                P.op("dve", lambda: nc.vector.tensor_tensor(out=yac[:, :], in0=yac[:, :], in1=yf[:, :], op=ALU.add), [yac_r, yf_r], [yac_r])
                P.op("pool", lambda: nc.gpsimd.tensor_tensor(out=yf[:, :].rearrange("p (h e) -> p h e", e=64), in0=xs[:],
                                                             in1=dsk[:, 0:64].unsqueeze(2).to_broadcast([128, 64, 64]), op=ALU.mult),
                     [xs_r, dsk_r], [yf_r])
                P.op("dve", lambda: nc.vector.tensor_tensor(out=yac[:, :], in0=yac[:, :], in1=yf[:, :], op=ALU.add), [yac_r, yf_r], [yac_r])
                P.op("dve", lambda: nc.vector.tensor_tensor(out=yac[:, :], in0=yac[:, :], in1=sz[:, :], op=ALU.mult), [yac_r, sz_r], [yac_r])
                for g in range(G):
                    P.op("act", lambda g=g: nc.scalar.activation(out=yf[:, g * 512:(g + 1) * 512], in_=yac[:, g * 512:(g + 1) * 512],
                                                                 func=AF.Square, accum_out=ss[:, g:g + 1]), [yac_r], [yf_r, ss_r])
                P.op("act", lambda: nc.scalar.activation(out=ss[:, :], in_=ss[:, :], func=AF.Sqrt, bias=self.eps_t[:, 0:1], scale=1.0 / 512),
                     [ss_r, self.eps_r], [ss_r])
                P.op("dve", lambda: nc.vector.reciprocal(out=ss[:, :], in_=ss[:, :]), [ss_r], [ss_r])
                P.op("dve", lambda: nc.vector.tensor_tensor(out=yac[:, :].rearrange("p (g e) -> p g e", e=512),
                                                            in0=yac[:, :].rearrange("p (g e) -> p g e", e=512),
                                                            in1=ss[:, :].unsqueeze(2).to_broadcast([128, 8, 512]), op=ALU.mult),
                     [yac_r, ss_r], [yac_r])
                for q in range(8):
                    pst, ps_r = self.ps[q % 2], self.psr[q % 2]
                    P.group("pe", [lambda c=c, pst=pst: nc.tensor.transpose(
                        out=pst[:, (c % 4) * 128:(c % 4 + 1) * 128], in_=yac[:, c * 128:(c + 1) * 128], identity=self.identf[:])
                        for c in range(q * 4, q * 4 + 4)], [yac_r, self.identf_r], [ps_r])
                    for c in range(q * 4, q * 4 + 4):
                        P.op("act", lambda c=c, pst=pst: nc.scalar.activation(
                            out=ytr[:, c, :], in_=pst[:, (c % 4) * 128:(c % 4 + 1) * 128], func=AF.Copy, scale=ngt[:, c:c + 1]),
                            [ps_r, ng_r], [ytr_r])
                P.dma("sp", YTv[:, :, t0:t0 + 128], ytr[:], [ytr_r], self.tr("YT"), ytr_r)
            P.barrier()
            rel = [mk_r, xs_r, yac_r, dta_r, btm_r, bct_r]
            if d == 1:
                rel += [yf_r, sz_r, dsk_r, ng_r, ytr_r]
            P.release(rel)

    def stage_final(self, fin_g):
        cfg, P, nc = self.cfg, self.P, self.nc
        XTv = self.XT.rearrange("(c p) t -> p c t", p=128)
        with ExitStack() as st:
            g, g_r = P.sb(st, "fing", [128, D], F32)
            P.dma("sp", g[:], fin_g[:, :], self.tr("fin_g"), [g_r], g_r)
            xin = [P.sb(st, "fx%d" % i, [128, DC, 128], F32) for i in range(2)]
            xtm = [P.sb(st, "ft%d" % i, [128, D], F32) for i in range(2)]
            junk, junk_r = P.sb(st, "fjunk", [128, D], F32)
            ssq = [P.sb(st, "fss%d" % i, [128, 1], F32) for i in range(2)]
            for i in range(cfg.S // 128):
                t0 = i * 128
                xi, xi_r = xin[i % 2]
                xt, xt_r = xtm[i % 2]
                ss, ss_r = ssq[i % 2]
                P.dma("sp", xi[:], XTv[:, :, t0:t0 + 128], self.tr("XT", t0, 128), [xi_r], xi_r)
                for q in range(DC // 4):
                    pb = (i * (DC // 4) + q) % 8
                    pst, psr = self.ps[pb], self.psr[pb]
                    P.group("pe", [
                        (lambda c=c, pst=pst: nc.tensor.transpose(
                            out=pst[:, (c % 4) * 128:(c % 4 + 1) * 128],
                            in_=xi[:, c, :], identity=self.identf[:]))
                        for c in range(q * 4, q * 4 + 4)], [xi_r, self.identf_r], [psr])
                    P.op("dve", lambda q=q, pst=pst: nc.vector.tensor_copy(out=xt[:, q * 512:(q + 1) * 512], in_=pst[:, :]),
                         [psr], [xt_r])
                P.op("act", lambda: nc.scalar.activation(out=junk[:], in_=xt[:], func=AF.Square, accum_out=ss[:, 0:1]),
                     [xt_r], [junk_r, ss_r])
                P.op("act", lambda: nc.scalar.activation(out=ss[:], in_=ss[:], func=AF.Sqrt, bias=self.eps_t[:, 0:1], scale=1.0 / D),
                     [ss_r, self.eps_r], [ss_r])
                P.op("dve", lambda: nc.vector.reciprocal(out=ss[:], in_=ss[:]), [ss_r], [ss_r])
                P.op("dve", lambda: nc.vector.scalar_tensor_tensor(
                    out=xt[:], in0=xt[:], scalar=ss[:, 0:1], in1=g[:], op0=ALU.mult, op1=ALU.mult),
                    [xt_r, ss_r, g_r], [xt_r])
                P.dma("sp", self.out[t0:t0 + 128, :], xt[:], [xt_r], self.tr("out"), xt_r)
            P.barrier()
            P.release([g_r] + [r for _, r in xin + xtm + ssq])


def pmajor(v):
    sh = v.shape[:-1]
    n = v.shape[-1] // 128
    return np.ascontiguousarray(np.swapaxes(v.reshape(sh + (n, 128)), -1, -2))


def make_in_maps(cfg, inputs, n_cores):
    f = lambda a: np.ascontiguousarray(np.asarray(a, dtype=np.float32))
    depth = cfg.depth
    shared = {
        "mod_w": f(inputs["mod_w"][:depth]),
        "mod_bT": pmajor(f(inputs["mod_b"][:depth])),
        "norm_gT": pmajor(f(inputs["norm_g"][:depth]).reshape(depth, 3 * D)),
        "ffn_w1": f(inputs["ffn_w1"][:depth]),
        "ffn_w3": f(inputs["ffn_w3"][:depth]),
        "ffn_w2": f(inputs["ffn_w2"][:depth]),
        "fin_g": np.ascontiguousarray(np.broadcast_to(f(inputs["final_norm_g"])[None, :], (128, D))),
        "ident": np.eye(128, dtype=np.float32),
    }
    n_even = (depth + 1) // 2
    if n_even:
        w_in = f(inputs["attn_w_in"][:n_even])
        d = np.arange(128)
        perm = np.where(d % 64 < 32, d + 32, d - 32)
        sign = np.where(d % 64 < 32, -1.0, 1.0).astype(np.float32)
        cols = (np.arange(8)[:, None] * 128 + perm[None, :]).reshape(-1)
        shared["attn_w_in"] = w_in
        shared["attn_w_rot"] = np.ascontiguousarray(
            np.concatenate([w_in[:, :, 3072:4096][:, :, cols], w_in[:, :, 4096:5120][:, :, cols]], axis=-1))
        shared["attn_w_out"] = f(inputs["attn_w_out"][:n_even])
        rpb = f(inputs["na_rpb"][:n_even])
        kc = np.arange(64)[:, None]
        qc = np.arange(64)[None, :]
        coff = np.clip(kc - qc, -15, 15) + 15
        shared["na_bias"] = np.ascontiguousarray(rpb[:, :, :, coff])
        cs = np.clip(qc - 8, 0, 64 - 16)
        cmask = ((kc >= cs) & (kc < cs + 16)).astype(np.float32)
        shared["na_cmask"] = np.ascontiguousarray(
            np.broadcast_to(cmask[None, :, None, :], (2, 64, 4, 64)).reshape(128, 4, 64))
        shared["lamT"] = np.ascontiguousarray(np.swapaxes(f(inputs["diff_lambda"][:n_even]), 1, 2))
        shared["subg"] = pmajor(f(inputs["diff_subln_g"][:n_even]))
        quarter = 32
        inv = (1.0 / (10000.0 ** (np.arange(quarter, dtype=np.float32) / quarter))).astype(np.float32)
        t = np.arange(cfg.S)
        row = (t // GRID_W).astype(np.float32)[:, None] * inv
        col = (t % GRID_W).astype(np.float32)[:, None] * inv
        ang = np.concatenate([row, row, col, col], axis=-1).astype(np.float32)
        cosT = np.ones((128, cfg.T), np.float32)
        sinT = np.zeros((128, cfg.T), np.float32)
        cosT[:, :cfg.S] = np.cos(ang).T
        sinT[:, :cfg.S] = np.sin(ang).T * sign[:, None]
        shared["ropec"] = cosT
        shared["ropes"] = sinT
    n_odd = depth // 2
    if n_odd:
        rep = lambda a: np.ascontiguousarray(np.broadcast_to(a.reshape(n_odd, 1, 128), (n_odd, 128, 128)))
        shared["ssm_w_in"] = f(inputs["ssm_w_in"][:n_odd])
        shared["ssm_w_out"] = f(inputs["ssm_w_out"][:n_odd])
        cwt = f(inputs["ssm_conv_w"][:n_odd])
        shared["conv_wT"] = np.ascontiguousarray(cwt.reshape(n_odd, 4, 48, 128).transpose(0, 3, 2, 1))
        shared["conv_bT"] = pmajor(f(inputs["ssm_conv_b"][:n_odd]))
        shared["ssm_a_log"] = rep(f(inputs["ssm_a_log"][:n_odd]))
        shared["ssm_dt_bias"] = rep(f(inputs["ssm_dt_bias"][:n_odd]))
        shared["ssm_d"] = rep(f(inputs["ssm_d"][:n_odd]))
        shared["ssm_norm_gT"] = pmajor(f(inputs["ssm_norm_g"][:n_odd]))
        u = np.arange(128)[:, None]
        l = np.arange(128)[None, :]
        shared["ssm_masks"] = np.ascontiguousarray(
            np.stack([u <= l, u > l, u >= l, u < l], axis=1).astype(np.float32))
    cc = pmajor(f(inputs["c_ctx"]))
    maps = []
    for b in range(n_cores):
        m = dict(shared)
        m["x"] = f(inputs["x"][b])
        m["ctx"] = f(inputs["ctx"][b])
        cb = pmajor(f(inputs["c"][b]))
        m["cT"] = np.ascontiguousarray(np.stack([cb, cc], axis=-1))
        maps.append(m)
    return maps


_CACHE = {}


def run(cfg, inputs, n_cores, mixers=True):
    import time as _t
    t0 = _t.time()
    b = Builder(cfg, mixers)
    nc = b.build()
    print("[kernel] build %.1fs n_ins=%d n_wait=%d" % (_t.time() - t0, b.P.n_ins, b.P.n_wait), flush=True)
    t0 = _t.time()
    maps = make_in_maps(cfg, inputs, n_cores)
    print("[kernel] layout %.1fs" % (_t.time() - t0), flush=True)
    t0 = _t.time()
    res = run_bass_kernel_spmd(nc, maps, core_ids=list(range(n_cores)))
    print("[kernel] run %.1fs" % (_t.time() - t0), flush=True)
    return np.stack([np.asarray(r["out"]) for r in res.results], axis=0)


def kernel(**inputs):
    cfg = Cfg(seq=4096, depth=4)
    return run(cfg, inputs, 8).astype(np.float32)
```

```python
import math
from contextlib import ExitStack
import numpy as np
import concourse.bass as bass
import concourse.mybir as mybir
from concourse.bass_utils import run_bass_kernel_spmd

F32 = mybir.dt.float32
BF16 = mybir.dt.bfloat16
AF = mybir.ActivationFunctionType
ALU = mybir.AluOpType
AX = mybir.AxisListType

D = 2048
DC = D // 128
CTX = 256
GRID_W = 64
N_MOD = 9
FFN = 5632
FC = FFN // 128
HD = 128
NA_H = 8
DF_H = 4
ATT_W = 6144
SSM_INNER = 4096
SSM_H = 64
SSM_P = 64
SSM_N = 128
SSM_G = 8
SSM_CONV_CH = SSM_INNER + 2 * SSM_G * SSM_N
SSM_IN_W = SSM_INNER + SSM_CONV_CH + 2 * SSM_H
EPS = 1e-6


class Res:
    __slots__ = ("name", "w", "r", "dsem", "dkey")

    def __init__(self, name):
        self.name = name
        self.w = {}
        self.r = {}
        self.dsem = None
        self.dkey = None


class Prog:
    ENG = ("pe", "act", "dve", "pool", "sp")

    def __init__(self, n_dma_sems=80):
        self.nc = bass.Bass("TRN2", target_bir_lowering=False)
        nc = self.nc
        self.es = ExitStack()
        self.eng = {"pe": nc.tensor, "act": nc.scalar, "dve": nc.vector, "pool": nc.gpsimd, "sp": nc.sync}
        self.sems = {}
        self.cnt = {}
        for e in self.ENG:
            self.sems["E" + e] = self.es.enter_context(nc.semaphore("E" + e))
            self.cnt["E" + e] = 0
        self.dfree = []
        for i in range(n_dma_sems):
            k = "D%d" % i
            self.sems[k] = self.es.enter_context(nc.semaphore(k))
            self.cnt[k] = 0
            self.dfree.append(k)
        self.seen = {e: {} for e in self.ENG}
        self.n_ins = 0
        self.n_wait = 0
        self.bg = set()

    def sb(self, stack, name, shape, dt):
        self.n_sb = getattr(self, "n_sb", 0) + 1
        name = "%s_%d" % (name, self.n_sb)
        t = stack.enter_context(self.nc.sbuf_tensor(name, list(shape), dt))
        r = Res(name)
        return t, r

    def dsem_for(self, res):
        if res.dkey is None:
            res.dkey = self.dfree.pop()
        return res.dkey

    def release(self, res_list):
        for r in res_list:
            if r.dkey is not None:
                self.dfree.append(r.dkey)
                r.dkey = None

    @staticmethod
    def _merge(dst, src):
        for k, v in src.items():
            if dst.get(k, 0) < v:
                dst[k] = v

    def _need(self, reads, writes):
        need = {}
        for r in reads:
            self._merge(need, r.w)
        for w in writes:
            self._merge(need, w.w)
            self._merge(need, w.r)
        return need

    def _waits(self, e, need):
        seen = self.seen[e]
        own = "E" + e
        lst = []
        for k, v in need.items():
            if e == "pe" and k == own:
                continue
            if seen.get(k, 0) < v:
                lst.append((k, v))
                seen[k] = v
        return lst

    def _commit(self, key, reads, writes):
        ev = {key: self.cnt[key]}
        for r in reads:
            self._merge(r.r, ev)
        for w in writes:
            w.w = dict(ev)
            w.r = {}

    def op(self, e, fn, reads=(), writes=()):
        need = self._need(reads, writes)
        lst = self._waits(e, need)
        eng = self.eng[e]
        for (k, v) in lst[1:]:
            eng.wait_ge(self.sems[k], v)
            self.n_wait += 1
        ins = fn()
        if lst:
            ins._wait_ge(self.sems[lst[0][0]], lst[0][1])
        key = "E" + e
        self.cnt[key] += 1
        ins.then_inc(self.sems[key], 1)
        self.n_ins += 1
        self._commit(key, reads, writes)
        return ins

    def group(self, e, fns, reads=(), writes=()):
        need = self._need(reads, writes)
        lst = self._waits(e, need)
        eng = self.eng[e]
        for (k, v) in lst[1:]:
            eng.wait_ge(self.sems[k], v)
            self.n_wait += 1
        ins = None
        for i, fn in enumerate(fns):
            ins = fn()
            if i == 0 and lst:
                ins._wait_ge(self.sems[lst[0][0]], lst[0][1])
            self.n_ins += 1
        key = "E" + e
        self.cnt[key] += 1
        ins.then_inc(self.sems[key], 1)
        self._commit(key, reads, writes)

    def dma(self, q, out, in_, reads, writes, owner, **kw):
        need = self._need(reads, writes)
        lst = self._waits(q, need)
        eng = self.eng[q]
        for (k, v) in lst:
            eng.wait_ge(self.sems[k], v)
            self.n_wait += 1
        key = self.dsem_for(owner)
        ins = eng.dma_start(out=out, in_=in_, **kw)
        self.cnt[key] += 16
        ins.then_inc(self.sems[key], 16)
        self.n_ins += 1
        self._commit(key, reads, writes)

    def barrier(self):
        for e in self.ENG:
            seen = self.seen[e]
            for k, v in self.cnt.items():
                if v == 0 or k == "E" + e or k in self.bg:
                    continue
                if seen.get(k, 0) < v:
                    self.eng[e].wait_ge(self.sems[k], v)
                    seen[k] = v
                    self.n_wait += 1

    def finish(self):
        for k, v in self.cnt.items():
            if v and self.seen["sp"].get(k, 0) < v and k != "Esp":
                self.eng["sp"].wait_ge(self.sems[k], v)
                self.seen["sp"][k] = v


class Cfg:
    def __init__(self, seq=4096, depth=4):
        self.S = seq
        self.L = CTX
        self.T = seq + CTX
        self.NT = self.T // 128
        self.depth = depth
        self.rows = seq // GRID_W


def blocks_of(cfg, tb):
    out = []
    t = 0
    while t < cfg.S:
        n = min(tb, cfg.S - t)
        out.append((t, n, 0))
        t += n
    t = cfg.S
    while t < cfg.T:
        n = min(tb, cfg.T - t)
        out.append((t, n, 1))
        t += n
    return out


def _scoped(fn):
    def w(self, *a, **k):
        self._scn = getattr(self, "_scn", 0) + 1
        with self.nc.named_scope("%s_%03d" % (fn.__name__, self._scn)):
            return fn(self, *a, **k)
    return w


class Builder:
    def __init__(self, cfg, mixers=True):
        self.cfg = cfg
        self.mixers = mixers
        self.P = Prog()
        self.nc = self.P.nc
        self.dram_in = {}
        self.dres = {}

    def din(self, name, shape, dt=F32):
        t = self.nc.dram_tensor(name, list(shape), dt, kind="ExternalInput").ap()
        self.dram_in[name] = t
        self.dres[name] = [Res(name)]
        return t

    def dscr(self, name, shape, dt, ntiles=1, kind="Internal"):
        t = self.nc.dram_tensor(name, list(shape), dt, kind=kind).ap()
        self.dres[name] = [Res("%s.%d" % (name, i)) for i in range(ntiles)]
        return t

    def tr(self, name, t0=None, n=None):
        rs = self.dres[name]
        if t0 is None or len(rs) == 1:
            return rs
        return rs[t0 // 128:(t0 + n + 127) // 128]

    def build(self):
        cfg, P, nc = self.cfg, self.P, self.nc
        S, L, T, NT = cfg.S, cfg.L, cfg.T, cfg.NT
        depth = cfg.depth
        n_even = (depth + 1) // 2
        n_odd = depth // 2
        x = self.din("x", [S, D])
        ctx = self.din("ctx", [L, D])
        cT = self.din("cT", [128, DC, 2])
        mod_w = self.din("mod_w", [depth, D, N_MOD * D])
        mod_bT = self.din("mod_bT", [depth, 128, N_MOD * DC])
        norm_gT = self.din("norm_gT", [depth, 128, 3 * DC])
        ffn_w1 = self.din("ffn_w1", [depth, 2, FC * 128, D])
        ffn_w3 = self.din("ffn_w3", [depth, 2, FC * 128, D])
        ffn_w2 = self.din("ffn_w2", [depth, 2, DC * 128, FFN])
        fin_g = self.din("fin_g", [128, D])
        if n_even:
            attn_w_in = self.din("attn_w_in", [n_even, D, ATT_W])
            attn_w_rot = self.din("attn_w_rot", [n_even, D, 2048])
            attn_w_out = self.din("attn_w_out", [n_even, D, D])
            na_bias = self.din("na_bias", [n_even, NA_H, 15, 64, 64])
            na_cmask = self.din("na_cmask", [128, 4, 64])
            lamT = self.din("lamT", [n_even, 128, 4])
            subg = self.din("subg", [n_even, 128, 2])
            ropec = self.din("ropec", [128, T])
            ropes = self.din("ropes", [128, T])
            self.ain = dict(w_in=attn_w_in, w_rot=attn_w_rot, w_out=attn_w_out, na_bias=na_bias,
                            na_cmask=na_cmask, lamT=lamT, subg=subg, ropec=ropec, ropes=ropes)
            self.awb = self.dscr("awb", [D, ATT_W], BF16)
            self.arb = self.dscr("arb", [D, 2048], BF16)
            self.aob = self.dscr("aob", [D, D], BF16)
            self.QKT = self.dscr("QKT", [32, 128, T], BF16)
            self.VTM = self.dscr("VTM", [T, 2048], BF16)
            self.CAT = self.dscr("CAT", [D, T], BF16)
        ident = self.din("ident", [128, 128])
        if n_odd:
            self.sin = dict(
                w_in=self.din("ssm_w_in", [n_odd, D, SSM_IN_W]),
                w_out=self.din("ssm_w_out", [n_odd, SSM_INNER, D]),
                conv_wT=self.din("conv_wT", [n_odd, 128, 48, 4]),
                conv_bT=self.din("conv_bT", [n_odd, 128, 48]),
                a_log=self.din("ssm_a_log", [n_odd, 128, 128]),
                dt_bias=self.din("ssm_dt_bias", [n_odd, 128, 128]),
                d_skip=self.din("ssm_d", [n_odd, 128, 128]),
                norm_gT=self.din("ssm_norm_gT", [n_odd, 128, 32]),
                masks=self.din("ssm_masks", [128, 4, 128]))
            self.swb = self.dscr("swb", [D, SSM_IN_W], BF16)
            self.sob = self.dscr("sob", [SSM_INNER, D], BF16)
            self.SZ = self.dscr("SZ", [T, SSM_INNER], F32)
            self.XBC = self.dscr("XBC", [48, 128, T], F32)
            self.DTR = self.dscr("DTR", [T, 128], F32)
            self.DTA = self.dscr("DTA", [T, 2, 128], F32)
            self.XS = self.dscr("XS", [T, SSM_INNER], F32)
            self.BCT = self.dscr("BCT", [16, 128, T], BF16)
            self.BTM = self.dscr("BTM", [T, SSM_G * SSM_N], BF16)
            self.YF = self.dscr("YF", [T, SSM_INNER], F32)
            self.YT = self.dscr("YT", [SSM_INNER, T], BF16)
        out = self.dscr("out", [S, D], F32, ntiles=1, kind="ExternalOutput")
        self.out = out
        XT = self.dscr("XT", [D, T], F32, ntiles=NT)
        self.XT = XT
        w1b = self.dscr("w1b", [2, FC, 128, D], BF16, ntiles=2)
        w3b = self.dscr("w3b", [2, FC, 128, D], BF16, ntiles=2)
        w2b = self.dscr("w2b", [2, DC, 128, FFN], BF16, ntiles=2)

        es = P.es
        self.ps = []
        self.psr = []
        for i in range(8):
            t = es.enter_context(nc.psum_tensor("ps%d" % i, [128, 512], F32))
            self.ps.append(t)
            self.psr.append(Res("ps%d" % i))
        self.identf, self.identf_r = P.sb(es, "identf", [128, 128], F32)
        self.identb, self.identb_r = P.sb(es, "identb", [128, 128], BF16)
        self.onesf, self.onesf_r = P.sb(es, "onesf", [128, 128], F32)
        self.onesb, self.onesb_r = P.sb(es, "onesb", [128, 128], BF16)
        self.modt, self.modt_r = P.sb(es, "modt", [128, N_MOD * DC, 2], F32)
        self.gs, self.gs_r = P.sb(es, "gs", [128, 3, DC, 2], F32)
        self.gate, self.gate_r = P.sb(es, "gate", [128, 3, DC, 2], F32)
        self.sc, self.sc_r = P.sb(es, "sc", [128, DC, 2], F32)

        P.dma("sp", self.identf[:], ident[:, :], self.tr("ident"), [self.identf_r], self.identf_r)
        P.op("dve", lambda: nc.vector.tensor_copy(out=self.identb[:], in_=self.identf[:]),
             [self.identf_r], [self.identb_r])
        P.op("dve", lambda: nc.vector.memset(self.onesf[:], 1.0), [], [self.onesf_r])
        P.op("dve", lambda: nc.vector.memset(self.onesb[:], 1.0), [], [self.onesb_r])
        self.eps_t, self.eps_r = P.sb(es, "epsc", [128, 1], F32)
        P.op("dve", lambda: nc.vector.memset(self.eps_t[:], EPS), [], [self.eps_r])
        with ExitStack() as st:
            craw, craw_r = P.sb(st, "craw", [128, DC, 2], F32)
            P.dma("sp", craw[:], cT[:, :, :], self.tr("cT"), [craw_r], craw_r)
            P.op("act", lambda: nc.scalar.activation(out=self.sc[:], in_=craw[:], func=AF.Silu),
                 [craw_r], [self.sc_r])
            P.barrier()
            P.release([craw_r])

        self.stage_in_transpose(x, ctx)
        for layer in range(depth):
            self.stage_wcast_ffn(layer, ffn_w1, ffn_w3, ffn_w2, w1b, w3b, w2b)
            self.stage_mod(layer, mod_w, mod_bT, norm_gT)
            self.stage_ffn(layer, 0, w1b, w3b, w2b)
            if self.mixers:
                ctx_out = layer < depth - 1
                if layer % 2 == 0:
                    self.stage_attn(layer, ctx_out)
                else:
                    self.stage_ssd(layer, ctx_out)
            self.stage_ffn(layer, 1, w1b, w3b, w2b)
        self.stage_final(fin_g)
        P.barrier()
        P.finish()
        return nc

    @_scoped
    def stage_in_transpose(self, x, ctx):
        cfg, P, nc = self.cfg, self.P, self.nc
        XTv = self.XT.rearrange("(c p) t -> p c t", p=128)
        with ExitStack() as st:
            NB = 2
            xin = [P.sb(st, "xin%d" % i, [128, D], F32) for i in range(NB)]
            xo = [P.sb(st, "xo%d" % i, [128, DC, 128], F32) for i in range(NB)]
            for i in range(cfg.NT):
                t0 = i * 128
                xi, xi_r = xin[i % NB]
                xoT, xo_r = xo[i % NB]
                src = x[t0:t0 + 128, :] if t0 < cfg.S else ctx[t0 - cfg.S:t0 - cfg.S + 128, :]
                srcres = self.tr("x") if t0 < cfg.S else self.tr("ctx")
                P.dma("sp", xi[:], src, srcres, [xi_r], xi_r)
                for q in range(DC // 4):
                    pb = (i * (DC // 4) + q) % 8
                    pst, psr = self.ps[pb], self.psr[pb]
                    P.group("pe", [
                        (lambda c=c, pst=pst: nc.tensor.transpose(
                            out=pst[:, (c % 4) * 128:(c % 4 + 1) * 128],
                            in_=xi[:, c * 128:(c + 1) * 128], identity=self.identf[:]))
                        for c in range(q * 4, q * 4 + 4)],
                        [xi_r, self.identf_r], [psr])
                    eng = "dve" if q % 2 == 0 else "act"
                    dst = xoT[:, q * 4:(q + 1) * 4, :]
                    srcp = pst[:, :].rearrange("p (c t) -> p c t", c=4)
                    if eng == "dve":
                        P.op("dve", lambda dst=dst, srcp=srcp: nc.vector.tensor_copy(out=dst, in_=srcp),
                             [psr], [xo_r])
                    else:
                        P.op("act", lambda dst=dst, srcp=srcp: nc.scalar.copy(out=dst, in_=srcp),
                             [psr], [xo_r])
                P.dma("act", XTv[:, :, t0:t0 + 128], xoT[:], [xo_r], self.tr("XT", t0, 128), xo_r)
            P.barrier()
            P.release([r for _, r in xin] + [r for _, r in xo])

    def stage_wcast_ffn(self, layer, w1, w3, w2, w1b, w3b, w2b):
        for f in range(2):
            for (src, dst, K, name) in ((w1, w1b, FC * 128, "w1b"), (w3, w3b, FC * 128, "w3b"), (w2, w2b, DC * 128, "w2b")):
                self.wcast(src[layer, f], dst[f].rearrange("c p n -> (c p) n"), K, name, res=[self.dres[name][f]])

    @_scoped
    def stage_mod(self, layer, mod_w, mod_bT, norm_gT):
        P, nc = self.P, self.nc
        NB = 512
        nblk = N_MOD * D // NB
        mwv = mod_w[layer].rearrange("(kt p) n -> p kt n", p=128)
        with ExitStack() as st:
            wt = [P.sb(st, "modw%d" % i, [128, DC, NB], F32) for i in range(2)]
            mb, mb_r = P.sb(st, "modb", [128, N_MOD * DC], F32)
            ng, ng_r = P.sb(st, "normg", [128, 3 * DC], F32)
            P.dma("sp", mb[:], mod_bT[layer], self.tr("mod_bT"), [mb_r], mb_r)
            P.dma("sp", ng[:], norm_gT[layer], self.tr("norm_gT"), [ng_r], ng_r)
            for b in range(nblk):
                w, w_r = wt[b % 2]
                for h in range(2):
                    P.dma("sp" if h == 0 else "act", w[:, h * 8:(h + 1) * 8, :],
                          mwv[:, h * 8:(h + 1) * 8, b * NB:(b + 1) * NB],
                          self.tr("mod_w"), [w_r], w_r)
                pb = b % 2
                pst, psr = self.ps[pb], self.psr[pb]
                fns = []
                for j in range(NB // 128):
                    for kt in range(DC):
                        fns.append(lambda j=j, kt=kt, w=w, pst=pst: nc.tensor.matmul(
                            pst[:, 2 * j:2 * j + 2], w[:, kt, j * 128:(j + 1) * 128], self.sc[:, kt, :],
                            start=(kt == 0), stop=(kt == DC - 1)))
                P.group("pe", fns, [w_r, self.sc_r], [psr])
                nj = NB // 128
                P.op("dve", lambda b=b, pst=pst: nc.vector.tensor_tensor(
                    out=self.modt[:, b * nj:(b + 1) * nj, :],
                    in0=pst[:, 0:2 * nj].rearrange("p (j s) -> p j s", s=2),
                    in1=mb[:, b * nj:(b + 1) * nj].unsqueeze(2).to_broadcast([128, nj, 2]),
                    op=ALU.add), [psr, mb_r], [self.modt_r])
            for i in range(3):
                sh = self.modt[:, (3 * i + 0) * DC:(3 * i + 1) * DC, :]
                scl = self.modt[:, (3 * i + 1) * DC:(3 * i + 2) * DC, :]
                gt = self.modt[:, (3 * i + 2) * DC:(3 * i + 3) * DC, :]
                P.op("dve", lambda i=i, scl=scl: nc.vector.scalar_tensor_tensor(
                    out=self.gs[:, i, :, :], in0=scl, scalar=1.0,
                    in1=ng[:, i * DC:(i + 1) * DC].unsqueeze(2).to_broadcast([128, DC, 2]),
                    op0=ALU.add, op1=ALU.mult), [self.modt_r, ng_r], [self.gs_r])
                fac = 1.0 if i == 1 else 0.5
                P.op("dve", lambda i=i, gt=gt, fac=fac: nc.vector.tensor_scalar(
                    out=self.gate[:, i, :, :], in0=gt, scalar1=fac, scalar2=None, op0=ALU.mult),
                    [self.modt_r], [self.gate_r])
            P.barrier()
            P.release([r for _, r in wt] + [mb_r, ng_r])

    def shift_ap(self, i, c, s):
        return self.modt[:, 3 * i * DC + c, s:s + 1]

    def norm_mod(self, st_tiles, i, t0, n, s, hT, hT_r, off=0):
        P, nc = self.P, self.nc
        XTv = self.XT.rearrange("(c p) t -> p c t", p=128)
        xs, sq, rstd, tmp = st_tiles
        (rstd_t, rstd_r) = rstd
        pst, psr = self.ps[7], self.psr[7]
        xres = self.tr("XT", t0, n)
        fns = []
        for c in range(DC):
            xt, xt_r = xs[c % len(xs)]
            sqt, sq_r = sq[c % len(sq)]
            P.dma("sp", xt[:, :n], XTv[:, c, t0:t0 + n], xres, [xt_r], xt_r)
            P.op("act", lambda xt=xt, sqt=sqt: nc.scalar.activation(out=sqt[:, :n], in_=xt[:, :n], func=AF.Square),
                 [xt_r], [sq_r])
            P.group("pe", [lambda sqt=sqt, c=c: nc.tensor.matmul(
                pst[:, :n], self.onesf[:], sqt[:, :n], start=(c == 0), stop=(c == DC - 1))],
                [sq_r, self.onesf_r], [psr])
        P.op("act", lambda: nc.scalar.activation(out=rstd_t[:, :n], in_=pst[:, :n], func=AF.Sqrt,
                                                 bias=self.eps_t[:, 0:1], scale=1.0 / D), [psr, self.eps_r], [rstd_r])
        P.op("dve", lambda: nc.vector.reciprocal(out=rstd_t[:, :n], in_=rstd_t[:, :n]), [rstd_r], [rstd_r])
        for c in range(DC):
            xt, xt_r = xs[c % len(xs)]
            tt, tt_r = tmp[c % len(tmp)]
            P.dma("sp", xt[:, :n], XTv[:, c, t0:t0 + n], xres, [xt_r], xt_r)
            P.op("dve", lambda xt=xt, tt=tt, c=c: nc.vector.scalar_tensor_tensor(
                out=tt[:, :n], in0=xt[:, :n], scalar=self.gs[:, i, c, s:s + 1], in1=rstd_t[:, :n],
                op0=ALU.mult, op1=ALU.mult), [xt_r, rstd_r, self.gs_r], [tt_r])
            P.op("act", lambda tt=tt, c=c: nc.scalar.activation(
                out=hT[:, c, off:off + n], in_=tt[:, :n], func=AF.Identity,
                bias=self.shift_ap(i, c, s), scale=1.0), [tt_r, self.modt_r], [hT_r])

    def norm_tiles(self, st, tb):
        P = self.P
        xs = [P.sb(st, "nx%d" % i, [128, tb], F32) for i in range(4)]
        sq = [P.sb(st, "nsq%d" % i, [128, tb], F32) for i in range(2)]
        rstd = P.sb(st, "nrstd", [128, tb], F32)
        tmp = [P.sb(st, "ntmp%d" % i, [128, tb], F32) for i in range(2)]
        return (xs, sq, rstd, tmp), [r for _, r in xs] + [r for _, r in sq] + [rstd[1]] + [r for _, r in tmp]

    @_scoped
    def stage_ffn(self, layer, f, w1b, w3b, w2b):
        cfg, P, nc = self.cfg, self.P, self.nc
        TB = 1024
        slot = 0 if f == 0 else 2
        XTv = self.XT.rearrange("(c p) t -> p c t", p=128)
        with ExitStack() as st:
            ntl, ntl_res = self.norm_tiles(st, 512)
            hT, hT_r = P.sb(st, "hT", [128, DC, TB], BF16)
            gT, gT_r = P.sb(st, "gT", [128, FC, TB], BF16)
            wa = [P.sb(st, "wa%d" % i, [128, DC * 128], BF16) for i in range(3)]
            wb = [P.sb(st, "wb%d" % i, [128, DC * 128], BF16) for i in range(3)]
            wd = [P.sb(st, "wd%d" % i, [128, FC * 128], BF16) for i in range(2)]
            sil = [P.sb(st, "sil%d" % i, [128, 512], F32) for i in range(2)]
            xr = [P.sb(st, "xr%d" % i, [128, 512], F32) for i in range(2)]
            yo = [P.sb(st, "yo%d" % i, [128, 512], F32) for i in range(2)]
            w1r, w3r, w2r = [self.dres["w1b"][f]], [self.dres["w3b"][f]], [self.dres["w2b"][f]]
            it = 0
            it2 = 0
            for (t0, n, s) in blocks_of(cfg, TB):
                halves = [(h0, min(512, n - h0)) for h0 in range(0, n, 512)]
                for (h0, hn) in halves:
                    self.norm_mod(ntl, slot, t0 + h0, hn, s, hT, hT_r, off=h0)

                def load_up(fc):
                    a, a_r = wa[fc % 3]
                    b, b_r = wb[fc % 3]
                    P.dma("sp", a[:], w1b[f, fc], w1r, [a_r], a_r)
                    P.dma("sp", b[:], w3b[f, fc], w3r, [b_r], b_r)
                load_up(0)
                load_up(1)
                for fc in range(FC):
                    if fc + 2 < FC:
                        load_up(fc + 2)
                    a, a_r = wa[fc % 3]
                    b, b_r = wb[fc % 3]
                    for (h0, hn) in halves:
                        k = it % 3
                        pa, pa_r = self.ps[2 * k], self.psr[2 * k]
                        pb, pb_r = self.ps[2 * k + 1], self.psr[2 * k + 1]
                        sl, sl_r = sil[it % 2]
                        it += 1
                        P.group("pe", [lambda kt=kt, a=a, pa=pa, h0=h0, hn=hn: nc.tensor.matmul(
                            pa[:, :hn], a[:, kt * 128:(kt + 1) * 128], hT[:, kt, h0:h0 + hn],
                            start=(kt == 0), stop=(kt == DC - 1)) for kt in range(DC)],
                            [a_r, hT_r], [pa_r])
                        P.group("pe", [lambda kt=kt, b=b, pb=pb, h0=h0, hn=hn: nc.tensor.matmul(
                            pb[:, :hn], b[:, kt * 128:(kt + 1) * 128], hT[:, kt, h0:h0 + hn],
                            start=(kt == 0), stop=(kt == DC - 1)) for kt in range(DC)],
                            [b_r, hT_r], [pb_r])
                        P.op("act", lambda pa=pa, sl=sl, hn=hn: nc.scalar.activation(out=sl[:, :hn], in_=pa[:, :hn], func=AF.Silu),
                             [pa_r], [sl_r])
                        P.op("dve", lambda pb=pb, sl=sl, fc=fc, h0=h0, hn=hn: nc.vector.tensor_tensor(
                            out=gT[:, fc, h0:h0 + hn], in0=sl[:, :hn], in1=pb[:, :hn], op=ALU.mult),
                            [sl_r, pb_r], [gT_r])

                def load_dn(dc):
                    w, w_r = wd[dc % 2]
                    hk = (FC // 2) * 128
                    P.dma("sp", w[:, :hk], w2b[f, dc, :, :hk], w2r, [w_r], w_r)
                    P.dma("sp", w[:, hk:], w2b[f, dc, :, hk:], w2r, [w_r], w_r)
                load_dn(0)
                for dc in range(DC):
                    if dc + 1 < DC:
                        load_dn(dc + 1)
                    w, w_r = wd[dc % 2]
                    for (h0, hn) in halves:
                        py, py_r = self.ps[it2 % 6], self.psr[it2 % 6]
                        xrt, xr_r = xr[it2 % 2]
                        yot, yo_r = yo[it2 % 2]
                        it2 += 1
                        P.dma("sp", xrt[:, :hn], XTv[:, dc, t0 + h0:t0 + h0 + hn], self.tr("XT", t0 + h0, hn), [xr_r], xr_r)
                        P.group("pe", [lambda kt=kt, w=w, py=py, h0=h0, hn=hn: nc.tensor.matmul(
                            py[:, :hn], w[:, kt * 128:(kt + 1) * 128], gT[:, kt, h0:h0 + hn],
                            start=(kt == 0), stop=(kt == FC - 1)) for kt in range(FC)],
                            [w_r, gT_r], [py_r])
                        P.op("dve", lambda py=py, xrt=xrt, yot=yot, dc=dc, hn=hn: nc.vector.scalar_tensor_tensor(
                            out=yot[:, :hn], in0=py[:, :hn], scalar=self.gate[:, slot, dc, s:s + 1], in1=xrt[:, :hn],
                            op0=ALU.mult, op1=ALU.add), [py_r, xr_r, self.gate_r], [yo_r])
                        P.dma("act", XTv[:, dc, t0 + h0:t0 + h0 + hn], yot[:, :hn], [yo_r], self.tr("XT", t0 + h0, hn), yo_r)
            P.barrier()
            P.release(ntl_res + [r for _, r in wa + wb + wd + xr + yo])

    def wcast(self, src2d, dst2d, K, name, rb=256, res=None):
        P = self.P
        if not hasattr(self, "wc_r"):
            self.wc_r = Res("wcast")
        for k0 in range(0, K, rb):
            P.dma("pool", dst2d[k0:k0 + rb, :], src2d[k0:k0 + rb, :], [], res or self.tr(name), self.wc_r)
        P.bg.add(self.wc_r.dkey)

    def lin_tiles(self, st, KT, NW=256, tag="l"):
        return [self.P.sb(st, "%sw%d" % (tag, i), [128, KT, NW], BF16) for i in range(3)]

    def lin_fm(self, wt, hT, hT_r, KT, wv, wname, cols, n, epi, NW=256, pbanks=(0, 1)):
        P, nc = self.P, self.nc
        c0, c1 = cols
        blocks = list(range(c0, c1, NW))
        nb = len(wt)
        hk = KT // 2

        def load(bi):
            w, w_r = wt[bi % nb]
            b0 = blocks[bi]
            P.dma("sp", w[:, :hk, :], wv[:, :hk, b0:b0 + NW], self.tr(wname), [w_r], w_r)
            P.dma("sp", w[:, hk:, :], wv[:, hk:, b0:b0 + NW], self.tr(wname), [w_r], w_r)
        for bi in range(min(nb - 1, len(blocks))):
            load(bi)
        it = 0
        for bi, b0 in enumerate(blocks):
            if bi + nb - 1 < len(blocks):
                load(bi + nb - 1)
            w, w_r = wt[bi % nb]
            for j in range(NW // 128):
                pb = pbanks[it % len(pbanks)]
                it += 1
                ps, ps_r = self.ps[pb], self.psr[pb]
                P.group("pe", [lambda kt=kt, w=w, j=j, ps=ps: nc.tensor.matmul(
                    ps[:, :n], w[:, kt, j * 128:(j + 1) * 128], hT[:, kt, :n],
                    start=(kt == 0), stop=(kt == KT - 1)) for kt in range(KT)],
                    [w_r, hT_r], [ps_r])
                epi(ps, ps_r, (b0 - c0) // 128 + j)

    def stage_attn(self, layer, ctx_out):
        cfg, P, nc = self.cfg, self.P, self.nc
        i = layer // 2
        A = self.ain
        self.wcast(A["w_in"][i], self.awb, D, "awb")
        self.wcast(A["w_rot"][i], self.arb, D, "arb")
        self.wcast(A["w_out"][i], self.aob, D, "aob")
        self.stage_attn_inproj(layer, i)
        self.stage_na(layer, i, ctx_out)
        self.stage_diff(layer, i, ctx_out)
        self.stage_outproj(self.CAT, "CAT", DC, self.aob.rearrange("(kt p) n -> p kt n", p=128), "aob", ctx_out)

    @_scoped
    def stage_attn_inproj(self, layer, i):
        cfg, P, nc = self.cfg, self.P, self.nc
        A = self.ain
        TB = 512
        wv = self.awb.rearrange("(kt p) n -> p kt n", p=128)
        rv = self.arb.rearrange("(kt p) n -> p kt n", p=128)
        with ExitStack() as st:
            ntl, ntl_res = self.norm_tiles(st, TB)
            hT, hT_r = P.sb(st, "ahT", [128, DC, TB], BF16)
            cosb, cos_r = P.sb(st, "cosb", [128, TB], F32)
            sinb, sin_r = P.sb(st, "sinb", [128, TB], F32)
            qo = [P.sb(st, "qo%d" % k, [128, TB], BF16) for k in range(3)]
            t1 = [P.sb(st, "rt1%d" % k, [128, TB], F32) for k in range(2)]
            t2 = [P.sb(st, "rt2%d" % k, [128, TB], F32) for k in range(2)]
            vw = [P.sb(st, "vw%d" % k, [128, DC, 512], BF16) for k in range(2)]
            vo = [P.sb(st, "vo%d" % k, [128, 512], BF16) for k in range(3)]
            wrot = [P.sb(st, "wrot%d" % k, [128, DC, 256], BF16) for k in range(2)]
            wlin = self.lin_tiles(st, DC, tag="ap")
            rel = [r for _, r in wlin]
            cnt = [0, 0, 0]
            for (t0, n, s) in blocks_of(cfg, TB):
                self.norm_mod(ntl, 1, t0, n, s, hT, hT_r)
                P.dma("sp", cosb[:, :n], A["ropec"][:, t0:t0 + n], self.tr("ropec"), [cos_r], cos_r)
                P.dma("sp", sinb[:, :n], A["ropes"][:, t0:t0 + n], self.tr("ropes"), [sin_r], sin_r)

                def epi_plain(ps, ps_r, j, t0=t0, n=n):
                    q, q_r = qo[cnt[0] % 3]
                    cnt[0] += 1
                    P.op("act", lambda: nc.scalar.copy(out=q[:, :n], in_=ps[:, :n]), [ps_r], [q_r])
                    P.dma("act", self.QKT[j, :, t0:t0 + n], q[:, :n], [q_r], self.tr("QKT"), q_r)
                self.lin_fm(wlin, hT, hT_r, DC, wv, "awb", (0, 2048), n, epi_plain)

                for (c0, r0, ch0) in ((3072, 0, 16), (4096, 1024, 24)):
                    for bi in range(4):
                        w, w_r = wrot[cnt[1] % 2]
                        P.dma("sp", w[:], rv[:, :, r0 + bi * 256:r0 + (bi + 1) * 256], self.tr("arb"), [w_r], w_r)
                        wq, wq_r = vw[cnt[1] % 2]
                        cnt[1] += 1
                        P.dma("sp", wq[:, :, 0:256], wv[:, :, c0 + bi * 256:c0 + (bi + 1) * 256], self.tr("awb"), [wq_r], wq_r)
                        for j in range(2):
                            pa, pa_r = self.ps[2], self.psr[2]
                            pb, pb_r = self.ps[3], self.psr[3]
                            P.group("pe", [lambda kt=kt, wq=wq, j=j: nc.tensor.matmul(
                                pa[:, :n], wq[:, kt, j * 128:(j + 1) * 128], hT[:, kt, :n],
                                start=(kt == 0), stop=(kt == DC - 1)) for kt in range(DC)], [wq_r, hT_r], [pa_r])
                            P.group("pe", [lambda kt=kt, w=w, j=j: nc.tensor.matmul(
                                pb[:, :n], w[:, kt, j * 128:(j + 1) * 128], hT[:, kt, :n],
                                start=(kt == 0), stop=(kt == DC - 1)) for kt in range(DC)], [w_r, hT_r], [pb_r])
                            a, a_r = t1[cnt[2] % 2]
                            b, b_r = t2[cnt[2] % 2]
                            cnt[2] += 1
                            q, q_r = qo[cnt[0] % 3]
                            cnt[0] += 1
                            P.op("dve", lambda a=a: nc.vector.tensor_tensor(out=a[:, :n], in0=pa[:, :n], in1=cosb[:, :n], op=ALU.mult),
                                 [pa_r, cos_r], [a_r])
                            P.op("dve", lambda b=b: nc.vector.tensor_tensor(out=b[:, :n], in0=pb[:, :n], in1=sinb[:, :n], op=ALU.mult),
                                 [pb_r, sin_r], [b_r])
                            P.op("pool", lambda a=a, b=b, q=q: nc.gpsimd.tensor_tensor(out=q[:, :n], in0=a[:, :n], in1=b[:, :n], op=ALU.add),
                                 [a_r, b_r], [q_r])
                            ch = ch0 + bi * 2 + j
                            P.dma("act", self.QKT[ch, :, t0:t0 + n], q[:, :n], [q_r], self.tr("QKT"), q_r)

                for (c0, o0) in ((2048, 0), (2560, 512), (5120, 1024), (5632, 1536)):
                    w, w_r = vw[cnt[1] % 2]
                    cnt[1] += 1
                    P.dma("sp", w[:, :8, :], wv[:, :8, c0:c0 + 512], self.tr("awb"), [w_r], w_r)
                    P.dma("sp", w[:, 8:, :], wv[:, 8:, c0:c0 + 512], self.tr("awb"), [w_r], w_r)
                    for tt in range(n // 128):
                        pv, pv_r = self.ps[4 + tt % 2], self.psr[4 + tt % 2]
                        P.group("pe", [lambda kt=kt, w=w, tt=tt, pv=pv: nc.tensor.matmul(
                            pv[:, :], hT[:, kt, tt * 128:(tt + 1) * 128], w[:, kt, :],
                            start=(kt == 0), stop=(kt == DC - 1)) for kt in range(DC)], [w_r, hT_r], [pv_r])
                        v, v_r = vo[cnt[0] % 3]
                        cnt[0] += 1
                        P.op("act", lambda v=v, pv=pv: nc.scalar.copy(out=v[:, :], in_=pv[:, :]), [pv_r], [v_r])
                        P.dma("act", self.VTM[t0 + tt * 128:t0 + (tt + 1) * 128, o0:o0 + 512], v[:, :], [v_r],
                              self.tr("VTM"), v_r)
            P.barrier()
            P.release(ntl_res + rel + [cos_r, sin_r] + [r for _, r in qo + vw + vo + wrot])

    @_scoped
    def stage_na(self, layer, i, ctx_out):
        cfg, P, nc = self.cfg, self.P, self.nc
        A = self.ain
        S, T, NT, rows = cfg.S, cfg.T, cfg.NT, cfg.rows
        kr = 8
        scale = HD ** -0.5
        CATv = self.CAT.rearrange("(c p) t -> c p t", p=128)
        with ExitStack() as st:
            cm, cm_r = P.sb(st, "cmask", [128, 4, 64], F32)
            P.dma("sp", cm[:], A["na_cmask"][:, :, :], self.tr("na_cmask"), [cm_r], cm_r)
            qT, q_r = P.sb(st, "naq", [128, S], BF16)
            kT, k_r = P.sb(st, "nak", [128, T], BF16)
            ve, ve_r = P.sb(st, "nave", [128, NT, 128], BF16)
            vod, vo_r = P.sb(st, "navo", [128, NT - 1, 128], BF16)
            Wm = [P.sb(st, "naW%d" % k, [128, 4, 64], F32) for k in range(8)]
            oT, o_r = P.sb(st, "naoT", [128, T], BF16)
            E = [P.sb(st, "naE%d" % k, [128, 256], F32) for k in range(2)]
            Pt = [P.sb(st, "naP%d" % k, [128, 384], BF16) for k in range(2)]
            rc = [P.sb(st, "narc%d" % k, [128, 64], F32) for k in range(2)]
            Pc, Pc_r = P.sb(st, "naPc", [128, 512], BF16)
            rcc, rcc_r = P.sb(st, "narcc", [128, 256], F32)
            cqT, cqT_r = P.sb(st, "nacq", [128, 256], BF16)
            for h in range(NA_H):
                P.dma("sp", qT[:], self.QKT[h, :, 0:S], self.tr("QKT"), [q_r], q_r)
                P.dma("sp", kT[:], self.QKT[8 + h, :, :], self.tr("QKT"), [k_r], k_r)
                P.dma("sp", ve[:], self.VTM[:, h * 128:(h + 1) * 128].rearrange("(j p) e -> p j e", p=128),
                      self.tr("VTM"), [ve_r], ve_r)
                P.dma("sp", vod[:], self.VTM[64:T - 64, h * 128:(h + 1) * 128].rearrange("(j p) e -> p j e", p=128),
                      self.tr("VTM"), [vo_r], vo_r)
                for dl in range(8):
                    W, W_r = Wm[dl]
                    dr0 = 7 - dl
                    src = A["na_bias"][i, h, dr0:dr0 + 8].rearrange("(a i2) kc qc -> (i2 kc) a qc", i2=2)
                    P.dma("sp", W[:], src, self.tr("na_bias"), [W_r], W_r)
                    P.op("act", lambda W=W: nc.scalar.activation(out=W[:], in_=W[:], func=AF.Exp), [W_r], [W_r])
                    P.op("dve", lambda W=W: nc.vector.tensor_tensor(out=W[:], in0=W[:], in1=cm[:], op=ALU.mult),
                         [W_r, cm_r], [W_r])
                for r in range(rows):
                    rs = min(max(r - kr // 2, 0), rows - kr)
                    dl = r - rs
                    W, W_r = Wm[dl]
                    ps, ps_r = self.ps[r % 2], self.psr[r % 2]
                    po, po_r = self.ps[2 + r % 2], self.psr[2 + r % 2]
                    Et, E_r = E[r % 2]
                    Pp, Pp_r = Pt[r % 2]
                    rct, rc_r = rc[r % 2]
                    qs = qT[:, r * 64:(r + 1) * 64]
                    k0 = rs * 64
                    fns = [lambda a=a: nc.tensor.matmul(ps[:, a * 64:(a + 1) * 64], kT[:, k0 + a * 128:k0 + (a + 1) * 128], qs,
                                                        start=True, stop=True) for a in range(4)]
                    fns += [lambda a=a: nc.tensor.matmul(ps[:, (4 + a) * 64:(5 + a) * 64], kT[:, S + a * 128:S + (a + 1) * 128], qs,
                                                         start=True, stop=True) for a in range(2)]
                    P.group("pe", fns, [k_r, q_r], [ps_r])
                    P.op("act", lambda: nc.scalar.activation(out=Et[:, :], in_=ps[:, 0:256], func=AF.Exp, scale=scale),
                         [ps_r], [E_r])
                    P.op("act", lambda: nc.scalar.activation(out=Pp[:, 256:384], in_=ps[:, 256:384], func=AF.Exp, scale=scale),
                         [ps_r], [Pp_r])
                    P.op("dve", lambda: nc.vector.tensor_tensor(out=Pp[:, 0:256], in0=Et[:, :],
                                                                in1=W[:].rearrange("p a q -> p (a q)"), op=ALU.mult),
                         [E_r, W_r], [Pp_r])
                    if rs % 2 == 0:
                        vt = [ve[:, rs // 2 + a, :] for a in range(4)]
                    else:
                        vt = [vod[:, (rs - 1) // 2 + a, :] for a in range(4)]
                    vt += [ve[:, S // 128 + a, :] for a in range(2)]
                    fns = [lambda a=a: nc.tensor.matmul(po[:, 0:64], vt[a], Pp[:, a * 64:(a + 1) * 64],
                                                        start=(a == 0), stop=(a == 5)) for a in range(6)]
                    fns += [lambda a=a: nc.tensor.matmul(po[:, 64:128], self.onesb[:], Pp[:, a * 64:(a + 1) * 64],
                                                         start=(a == 0), stop=(a == 5)) for a in range(6)]
                    P.group("pe", fns, [Pp_r, ve_r, vo_r, self.onesb_r], [po_r])
                    P.op("dve", lambda: nc.vector.reciprocal(out=rct[:, :], in_=po[:, 64:128]), [po_r], [rc_r])
                    P.op("dve", lambda: nc.vector.tensor_tensor(out=oT[:, r * 64:(r + 1) * 64], in0=po[:, 0:64], in1=rct[:, :],
                                                                op=ALU.mult), [po_r, rc_r], [o_r])
                if ctx_out:
                    ps, ps_r = self.ps[4], self.psr[4]
                    po, po_r = self.ps[5], self.psr[5]
                    P.dma("sp", cqT[:], self.QKT[h, :, S:T], self.tr("QKT"), [cqT_r], cqT_r)
                    P.group("pe", [lambda a=a: nc.tensor.matmul(ps[:, a * 256:(a + 1) * 256], kT[:, S + a * 128:S + (a + 1) * 128],
                                                                cqT[:], start=True, stop=True) for a in range(2)],
                            [k_r, cqT_r], [ps_r])
                    P.op("act", lambda: nc.scalar.activation(out=Pc[:, :], in_=ps[:, :], func=AF.Exp, scale=scale), [ps_r], [Pc_r])
                    fns = [lambda a=a: nc.tensor.matmul(po[:, 0:256], ve[:, S // 128 + a, :], Pc[:, a * 256:(a + 1) * 256],
                                                        start=(a == 0), stop=(a == 1)) for a in range(2)]
                    fns += [lambda a=a: nc.tensor.matmul(po[:, 256:512], self.onesb[:], Pc[:, a * 256:(a + 1) * 256],
                                                         start=(a == 0), stop=(a == 1)) for a in range(2)]
                    P.group("pe", fns, [Pc_r, ve_r, self.onesb_r], [po_r])
                    P.op("dve", lambda: nc.vector.reciprocal(out=rcc[:, :], in_=po[:, 256:512]), [po_r], [rcc_r])
                    P.op("dve", lambda: nc.vector.tensor_tensor(out=oT[:, S:T], in0=po[:, 0:256], in1=rcc[:, :], op=ALU.mult),
                         [po_r, rcc_r], [o_r])
                nst = T if ctx_out else S
                P.dma("act", CATv[h, :, 0:nst], oT[:, 0:nst], [o_r], self.tr("CAT"), o_r)
            P.barrier()
            P.release([cm_r, q_r, k_r, ve_r, vo_r, o_r, cqT_r] + [r for _, r in Wm])

    @_scoped
    def stage_diff(self, layer, i, ctx_out):
        cfg, P, nc = self.cfg, self.P, self.nc
        A = self.ain
        S, T, NT = cfg.S, cfg.T, cfg.NT
        scale = HD ** -0.5
        lam_init = 0.8 - 0.6 * math.exp(-0.3 * layer)
        CATv = self.CAT.rearrange("(c p) t -> c p t", p=128)
        with ExitStack() as st:
            lt, lt_r = P.sb(st, "lamt", [128, 4], F32)
            sg, sg_r = P.sb(st, "subg", [128, 2], F32)
            lp, lp_r = P.sb(st, "lamp", [128, 2], F32)
            nl, nl_r = P.sb(st, "neglam", [128, 1], F32)
            P.dma("sp", lt[:], A["lamT"][i], self.tr("lamT"), [lt_r], lt_r)
            P.dma("sp", sg[:], A["subg"][i], self.tr("subg"), [sg_r], sg_r)
            P.op("dve", lambda: nc.vector.tensor_tensor(out=lp[:, 0:1], in0=lt[:, 0:1], in1=lt[:, 1:2], op=ALU.mult), [lt_r], [lp_r])
            P.op("dve", lambda: nc.vector.tensor_tensor(out=lp[:, 1:2], in0=lt[:, 2:3], in1=lt[:, 3:4], op=ALU.mult), [lt_r], [lp_r])
            ps, ps_r = self.ps[6], self.psr[6]
            P.group("pe", [lambda: nc.tensor.matmul(ps[:, 0:2], self.onesf[:], lp[:, :], start=True, stop=True)],
                    [lp_r, self.onesf_r], [ps_r])
            P.op("act", lambda: nc.scalar.activation(out=lp[:, :], in_=ps[:, 0:2], func=AF.Exp), [ps_r], [lp_r])
            P.op("dve", lambda: nc.vector.scalar_tensor_tensor(out=nl[:, :], in0=lp[:, 1:2], scalar=-lam_init, in1=lp[:, 0:1],
                                                               op0=ALU.add, op1=ALU.subtract), [lp_r], [nl_r])
            P.op("dve", lambda: nc.vector.tensor_scalar(out=sg[:, :], in0=sg[:, :], scalar1=1.0 - lam_init, scalar2=None, op0=ALU.mult),
                 [sg_r], [sg_r])
            qT = [P.sb(st, "dq%d" % m, [128, T], BF16) for m in range(2)]
            kT = [P.sb(st, "dk%d" % m, [128, T], BF16) for m in range(2)]
            v, v_r = P.sb(st, "dv", [128, NT, 256], BF16)
            Ering = [P.sb(st, "dE%d" % k, [128, 512], BF16) for k in range(4)]
            rec = [P.sb(st, "drec%d" % m, [128, 512], F32) for m in range(2)]
            oa = [P.sb(st, "doa%d" % e, [128, 512], F32) for e in range(2)]
            ob = [P.sb(st, "dob%d" % e, [128, 512], F32) for e in range(2)]
            sq, sq_r = P.sb(st, "dsq", [128, 512], F32)
            rstd, rstd_r = P.sb(st, "drstd", [128, 512], F32)
            oo = [P.sb(st, "doo%d" % e, [128, 512], BF16) for e in range(2)]
            qblocks = [(t0, n, s) for (t0, n, s) in blocks_of(cfg, 512) if s == 0 or ctx_out]
            for h in range(DF_H):
                for m in range(2):
                    P.dma("sp", qT[m][0][:], self.QKT[16 + 2 * h + m, :, :], self.tr("QKT"), [qT[m][1]], qT[m][1])
                    P.dma("sp", kT[m][0][:], self.QKT[24 + 2 * h + m, :, :], self.tr("QKT"), [kT[m][1]], kT[m][1])
                P.dma("sp", v[:], self.VTM[:, 1024 + h * 256:1024 + (h + 1) * 256].rearrange("(j p) e -> p j e", p=128),
                      self.tr("VTM"), [v_r], v_r)
                for (q0, nq, s) in qblocks:
                    ktiles = list(range(NT)) if s == 0 else list(range(S // 128, NT))
                    its = [(ki, kt, m) for ki, kt in enumerate(ktiles) for m in range(2)]

                    def emit_s(idx):
                        ki, kt, m = its[idx]
                        pss, pss_r = self.ps[6 + idx % 2], self.psr[6 + idx % 2]
                        Et, E_r = Ering[idx % 4]
                        P.group("pe", [lambda: nc.tensor.matmul(
                            pss[:, :nq], kT[m][0][:, kt * 128:(kt + 1) * 128], qT[m][0][:, q0:q0 + nq], start=True, stop=True)],
                            [kT[m][1], qT[m][1]], [pss_r])
                        P.op("act", lambda: nc.scalar.activation(out=Et[:, :nq], in_=pss[:, :nq], func=AF.Exp, scale=scale),
                             [pss_r], [E_r])

                    def emit_pv(idx):
                        ki, kt, m = its[idx]
                        Et, E_r = Ering[idx % 4]
                        first, last = (ki == 0), (ki == len(ktiles) - 1)
                        P.group("pe", [
                            lambda: nc.tensor.matmul(self.ps[2 * m][:, :nq], v[:, kt, 0:128], Et[:, :nq], start=first, stop=last),
                            lambda: nc.tensor.matmul(self.ps[2 * m + 1][:, :nq], v[:, kt, 128:256], Et[:, :nq], start=first, stop=last),
                            lambda: nc.tensor.matmul(self.ps[4 + m][:, :nq], self.onesb[:], Et[:, :nq], start=first, stop=last)],
                            [E_r, v_r, self.onesb_r], [self.psr[2 * m], self.psr[2 * m + 1], self.psr[4 + m]])

                    emit_s(0)
                    for idx in range(len(its)):
                        if idx + 1 < len(its):
                            emit_s(idx + 1)
                        emit_pv(idx)
                    P.op("dve", lambda: nc.vector.reciprocal(out=rec[0][0][:, :nq], in_=self.ps[4][:, :nq]), [self.psr[4]], [rec[0][1]])
                    P.op("dve", lambda: nc.vector.reciprocal(out=rec[1][0][:, :nq], in_=self.ps[5][:, :nq]), [self.psr[5]], [rec[1][1]])
                    P.op("dve", lambda: nc.vector.tensor_scalar(out=rec[1][0][:, :nq], in0=rec[1][0][:, :nq], scalar1=nl[:, 0:1],
                                                                scalar2=None, op0=ALU.mult), [rec[1][1], nl_r], [rec[1][1]])
                    for e in range(2):
                        P.op("dve", lambda e=e: nc.vector.tensor_tensor(out=oa[e][0][:, :nq], in0=self.ps[e][:, :nq], in1=rec[0][0][:, :nq],
                                                                        op=ALU.mult), [self.psr[e], rec[0][1]], [oa[e][1]])
                        P.op("dve", lambda e=e: nc.vector.tensor_tensor(out=ob[e][0][:, :nq], in0=self.ps[2 + e][:, :nq], in1=rec[1][0][:, :nq],
                                                                        op=ALU.mult), [self.psr[2 + e], rec[1][1]], [ob[e][1]])
                        P.op("pool", lambda e=e: nc.gpsimd.tensor_tensor(out=oa[e][0][:, :nq], in0=oa[e][0][:, :nq], in1=ob[e][0][:, :nq],
                                                                         op=ALU.add), [oa[e][1], ob[e][1]], [oa[e][1]])
                    pst, pst_r = self.ps[6], self.psr[6]
                    for e in range(2):
                        P.op("act", lambda e=e: nc.scalar.activation(out=sq[:, :nq], in_=oa[e][0][:, :nq], func=AF.Square), [oa[e][1]], [sq_r])
                        P.group("pe", [lambda e=e: nc.tensor.matmul(pst[:, :nq], self.onesf[:], sq[:, :nq], start=(e == 0), stop=(e == 1))],
                                [sq_r, self.onesf_r], [pst_r])
                    P.op("act", lambda: nc.scalar.activation(out=rstd[:, :nq], in_=pst[:, :nq], func=AF.Sqrt, bias=self.eps_t[:, 0:1],
                                                             scale=1.0 / 256), [pst_r, self.eps_r], [rstd_r])
                    P.op("dve", lambda: nc.vector.reciprocal(out=rstd[:, :nq], in_=rstd[:, :nq]), [rstd_r], [rstd_r])
                    for e in range(2):
                        P.op("dve", lambda e=e: nc.vector.scalar_tensor_tensor(
                            out=oo[e][0][:, :nq], in0=oa[e][0][:, :nq], scalar=sg[:, e:e + 1], in1=rstd[:, :nq],
                            op0=ALU.mult, op1=ALU.mult), [oa[e][1], sg_r, rstd_r], [oo[e][1]])
                        P.dma("act", CATv[8 + 2 * h + e, :, q0:q0 + nq], oo[e][0][:, :nq], [oo[e][1]], self.tr("CAT"), oo[e][1])
            P.barrier()
            P.release([lt_r, sg_r, v_r] + [r for _, r in qT + kT + oo])

    @_scoped
    def stage_outproj(self, ACT_T, aname, KT, wv, wname, ctx_out):
        cfg, P, nc = self.cfg, self.P, self.nc
        TB = 512
        XTv = self.XT.rearrange("(c p) t -> p c t", p=128)
        av = ACT_T.rearrange("(c p) t -> p c t", p=128)
        with ExitStack() as st:
            aT, aT_r = P.sb(st, "opa", [128, KT, TB], BF16)
            xr = [P.sb(st, "opx%d" % k, [128, TB], F32) for k in range(2)]
            yo = [P.sb(st, "opy%d" % k, [128, TB], F32) for k in range(2)]
            wlin = self.lin_tiles(st, KT, tag="op")
            rel = [r for _, r in wlin]
            cnt = [0]
            for (t0, n, s) in blocks_of(cfg, TB):
                if s == 1 and not ctx_out:
                    continue
                hk = KT // 2
                P.dma("sp", aT[:, :hk, :n], av[:, :hk, t0:t0 + n], self.tr(aname), [aT_r], aT_r)
                P.dma("sp", aT[:, hk:, :n], av[:, hk:, t0:t0 + n], self.tr(aname), [aT_r], aT_r)

                def epi(ps, ps_r, j, t0=t0, n=n, s=s):
                    xrt, xr_r = xr[cnt[0] % 2]
                    yot, yo_r = yo[cnt[0] % 2]
                    cnt[0] += 1
                    P.dma("sp", xrt[:, :n], XTv[:, j, t0:t0 + n], self.tr("XT", t0, n), [xr_r], xr_r)
                    P.op("dve", lambda: nc.vector.scalar_tensor_tensor(
                        out=yot[:, :n], in0=ps[:, :n], scalar=self.gate[:, 1, j, s:s + 1], in1=xrt[:, :n],
                        op0=ALU.mult, op1=ALU.add), [ps_r, xr_r, self.gate_r], [yo_r])
                    P.dma("act", XTv[:, j, t0:t0 + n], yot[:, :n], [yo_r], self.tr("XT", t0, n), yo_r)
                self.lin_fm(wlin, aT, aT_r, KT, wv, wname, (0, D), n, epi)
            P.barrier()
            P.release(rel + [aT_r] + [r for _, r in xr + yo])

    def lin_tm(self, wt, hT, hT_r, KT, wv, wname, cols, ntt, epi, pbanks=(4, 5)):
        P, nc = self.P, self.nc
        c0, c1 = cols
        blocks = list(range(c0, c1, 512))
        nb = len(wt)
        hk = KT // 2

        def load(bi):
            w, w_r = wt[bi % nb]
            b0 = blocks[bi]
            P.dma("sp", w[:, :hk, :], wv[:, :hk, b0:b0 + 512], self.tr(wname), [w_r], w_r)
            P.dma("sp", w[:, hk:, :], wv[:, hk:, b0:b0 + 512], self.tr(wname), [w_r], w_r)
        for bi in range(min(nb - 1, len(blocks))):
            load(bi)
        it = 0
        for bi, b0 in enumerate(blocks):
            if bi + nb - 1 < len(blocks):
                load(bi + nb - 1)
            w, w_r = wt[bi % nb]
            for tt in range(ntt):
                pb = pbanks[it % len(pbanks)]
                it += 1
                ps, ps_r = self.ps[pb], self.psr[pb]
                P.group("pe", [lambda kt=kt, w=w, tt=tt, ps=ps: nc.tensor.matmul(
                    ps[:, :], hT[:, kt, tt * 128:(tt + 1) * 128], w[:, kt, :],
                    start=(kt == 0), stop=(kt == KT - 1)) for kt in range(KT)], [w_r, hT_r], [ps_r])
                epi(ps, ps_r, bi, tt)

    def stage_ssd(self, layer, ctx_out):
        i = layer // 2
        A = self.sin
        self.wcast(A["w_in"][i], self.swb, D, "swb")
        self.wcast(A["w_out"][i], self.sob, SSM_INNER, "sob")
        self.stage_ssd_inproj(layer, i)
        self.stage_ssd_conv(layer, i)
        self.stage_ssd_dt(layer, i)
        self.stage_ssd_scan(layer, i, 0, ctx_out)
        self.stage_ssd_scan(layer, i, 1, ctx_out)
        self.stage_outproj(self.YT, "YT", SSM_INNER // 128, self.sob.rearrange("(kt p) n -> p kt n", p=128), "sob", ctx_out)

    @_scoped
    def stage_ssd_inproj(self, layer, i):
        cfg, P, nc = self.cfg, self.P, self.nc
        TB = 512
        wv = self.swb.rearrange("(kt p) n -> p kt n", p=128)
        with ExitStack() as st:
            ntl, ntl_res = self.norm_tiles(st, TB)
            hT, hT_r = P.sb(st, "shT", [128, DC, TB], BF16)
            wlin = self.lin_tiles(st, DC, tag="si")
            wtm = [P.sb(st, "stm%d" % k, [128, DC, 512], BF16) for k in range(2)]
            wdt, wdt_r = P.sb(st, "swdt", [128, DC, 128], BF16)
            zo = [P.sb(st, "szo%d" % k, [128, 512], F32) for k in range(3)]
            xo = [P.sb(st, "sxo%d" % k, [128, TB], F32) for k in range(3)]
            do = [P.sb(st, "sdo%d" % k, [128, 128], F32) for k in range(2)]
            cnt = [0, 0, 0]
            for (t0, n, s) in blocks_of(cfg, TB):
                self.norm_mod(ntl, 1, t0, n, s, hT, hT_r)

                def epi_z(ps, ps_r, cb, tt, t0=t0):
                    z, z_r = zo[cnt[0] % 3]
                    cnt[0] += 1
                    P.op("act", lambda: nc.scalar.activation(out=z[:, :], in_=ps[:, :], func=AF.Silu), [ps_r], [z_r])
                    P.dma("act", self.SZ[t0 + tt * 128:t0 + (tt + 1) * 128, cb * 512:(cb + 1) * 512], z[:, :], [z_r],
                          self.tr("SZ"), z_r)
                self.lin_tm(wtm, hT, hT_r, DC, wv, "swb", (0, SSM_INNER), n // 128, epi_z)

                def epi_x(ps, ps_r, j, t0=t0, n=n):
                    xx, x_r = xo[cnt[1] % 3]
                    cnt[1] += 1
                    P.op("act", lambda: nc.scalar.copy(out=xx[:, :n], in_=ps[:, :n]), [ps_r], [x_r])
                    P.dma("act", self.XBC[j, :, t0:t0 + n], xx[:, :n], [x_r], self.tr("XBC"), x_r)
                self.lin_fm(wlin, hT, hT_r, DC, wv, "swb", (SSM_INNER, SSM_INNER + SSM_CONV_CH), n, epi_x)

                P.dma("sp", wdt[:], wv[:, :, SSM_INNER + SSM_CONV_CH:SSM_IN_W], self.tr("swb"), [wdt_r], wdt_r)
                for tt in range(n // 128):
                    ps, ps_r = self.ps[6], self.psr[6]
                    P.group("pe", [lambda kt=kt, tt=tt: nc.tensor.matmul(
                        ps[:, 0:128], hT[:, kt, tt * 128:(tt + 1) * 128], wdt[:, kt, :],
                        start=(kt == 0), stop=(kt == DC - 1)) for kt in range(DC)], [wdt_r, hT_r], [ps_r])
                    dd, d_r = do[cnt[2] % 2]
                    cnt[2] += 1
                    P.op("dve", lambda dd=dd: nc.vector.tensor_copy(out=dd[:, :], in_=ps[:, 0:128]), [ps_r], [d_r])
                    P.dma("act", self.DTR[t0 + tt * 128:t0 + (tt + 1) * 128, :], dd[:, :], [d_r], self.tr("DTR"), d_r)
            P.barrier()
            P.release(ntl_res + [r for _, r in wlin + wtm + zo + xo + do] + [wdt_r])

    @_scoped
    def stage_ssd_conv(self, layer, i):
        cfg, P, nc = self.cfg, self.P, self.nc
        A = self.sin
        S, T, NT = cfg.S, cfg.T, cfg.NT
        with ExitStack() as st:
            cw, cw_r = P.sb(st, "cw", [128, 48, 4], F32)
            cb, cb_r = P.sb(st, "cb", [128, 48], F32)
            P.dma("sp", cw[:], A["conv_wT"][i], self.tr("conv_wT"), [cw_r], cw_r)
            P.dma("sp", cb[:], A["conv_bT"][i], self.tr("conv_bT"), [cb_r], cb_r)
            xin = [P.sb(st, "cxin%d" % k, [128, T], F32) for k in range(2)]
            acc = [P.sb(st, "cacc%d" % k, [128, T], F32) for k in range(2)]
            sf = [P.sb(st, "csf%d" % k, [128, T], F32) for k in range(2)]
            sbf = [P.sb(st, "csb%d" % k, [128, T], BF16) for k in range(2)]
            tmf = [P.sb(st, "ctmf%d" % k, [128, NT, 128], F32) for k in range(2)]
            tmb = [P.sb(st, "ctmb%d" % k, [128, NT, 128], BF16) for k in range(2)]
            for c in range(48):
                x, x_r = xin[c % 2]
                a, a_r = acc[c % 2]
                P.dma("sp", x[:], self.XBC[c, :, :], self.tr("XBC"), [x_r], x_r)
                for (lo, hi) in ((0, S), (S, T)):
                    P.op("act", lambda lo=lo, hi=hi: nc.scalar.activation(
                        out=a[:, lo:hi], in_=x[:, lo:hi], func=AF.Identity, bias=cb[:, c:c + 1], scale=cw[:, c, 1:2]),
                        [x_r, cw_r, cb_r], [a_r])
                    for (k, dlo, dhi, slo, shi) in ((0, lo + 1, hi, lo, hi - 1), (2, lo, hi - 1, lo + 1, hi), (3, lo, hi - 2, lo + 2, hi)):
                        P.op("dve", lambda k=k, dlo=dlo, dhi=dhi, slo=slo, shi=shi: nc.vector.scalar_tensor_tensor(
                            out=a[:, dlo:dhi], in0=x[:, slo:shi], scalar=cw[:, c, k:k + 1], in1=a[:, dlo:dhi],
                            op0=ALU.mult, op1=ALU.add), [x_r, a_r, cw_r], [a_r])
                if c < 32:
                    s_, s_r = sf[c % 2]
                    tm, tm_r = tmf[c % 2]
                    P.op("act", lambda: nc.scalar.activation(out=s_[:, :], in_=a[:, :], func=AF.Silu), [a_r], [s_r])
                    for q in range((NT + 3) // 4):
                        js = list(range(q * 4, min(q * 4 + 4, NT)))
                        pst, ps_r = self.ps[q % 4], self.psr[q % 4]
                        P.group("pe", [lambda j=j, pst=pst: nc.tensor.transpose(
                            out=pst[:, (j % 4) * 128:(j % 4 + 1) * 128], in_=s_[:, j * 128:(j + 1) * 128], identity=self.identf[:])
                            for j in js], [s_r, self.identf_r], [ps_r])
                        dst = tm[:, js[0]:js[-1] + 1, :]
                        src = pst[:, 0:len(js) * 128].rearrange("p (j e) -> p j e", e=128)
                        if q % 2 == 0:
                            P.op("dve", lambda dst=dst, src=src: nc.vector.tensor_copy(out=dst, in_=src), [ps_r], [tm_r])
                        else:
                            P.op("act", lambda dst=dst, src=src: nc.scalar.copy(out=dst, in_=src), [ps_r], [tm_r])
                    P.dma("act", self.XS[:, c * 128:(c + 1) * 128].rearrange("(j p) e -> p j e", p=128), tm[:], [tm_r],
                          self.tr("XS"), tm_r)
                else:
                    s_, s_r = sbf[c % 2]
                    P.op("act", lambda: nc.scalar.activation(out=s_[:, :], in_=a[:, :], func=AF.Silu), [a_r], [s_r])
                    P.dma("act", self.BCT[c - 32, :, :], s_[:, :], [s_r], self.tr("BCT"), s_r)
                    if c < 40:
                        tm, tm_r = tmb[c % 2]
                        for q in range((NT + 3) // 4):
                            js = list(range(q * 4, min(q * 4 + 4, NT)))
                            pst, ps_r = self.ps[4 + q % 4], self.psr[4 + q % 4]
                            pv = pst[:, :].bitcast(BF16)
                            P.group("pe", [lambda j=j, pv=pv: nc.tensor.transpose(
                                out=pv[:, (j % 4) * 128:(j % 4 + 1) * 128], in_=s_[:, j * 128:(j + 1) * 128], identity=self.identb[:])
                                for j in js], [s_r, self.identb_r], [ps_r])
                            dst = tm[:, js[0]:js[-1] + 1, :]
                            src = pv[:, 0:len(js) * 128].rearrange("p (j e) -> p j e", e=128)
                            P.op("dve", lambda dst=dst, src=src: nc.vector.tensor_copy(out=dst, in_=src), [ps_r], [tm_r])
                        g = c - 32
                        P.dma("act", self.BTM[:, g * 128:(g + 1) * 128].rearrange("(j p) e -> p j e", p=128), tm[:], [tm_r],
                              self.tr("BTM"), tm_r)
            P.barrier()
            P.release([cw_r, cb_r] + [r for _, r in xin + acc + sf + sbf + tmf + tmb])

    @_scoped
    def stage_ssd_dt(self, layer, i):
        cfg, P, nc = self.cfg, self.P, self.nc
        A = self.sin
        NT = cfg.NT
        with ExitStack() as st:
            x, x_r = P.sb(st, "dtx", [128, NT, 128], F32)
            ax, ax_r = P.sb(st, "dtax", [128, NT, 128], F32)
            bi, bi_r = P.sb(st, "dtb", [128, 128], F32)
            al, al_r = P.sb(st, "dtal", [128, 128], F32)
            P.dma("sp", x[:], self.DTR.rearrange("(j p) h -> p j h", p=128), self.tr("DTR"), [x_r], x_r)
            P.dma("sp", bi[:], A["dt_bias"][i], self.tr("ssm_dt_bias"), [bi_r], bi_r)
            P.dma("sp", al[:], A["a_log"][i], self.tr("ssm_a_log"), [al_r], al_r)
            bb = bi[:, :].unsqueeze(1).to_broadcast([128, NT, 128])
            P.op("dve", lambda: nc.vector.tensor_tensor(out=x[:], in0=x[:], in1=bb, op=ALU.add), [x_r, bi_r], [x_r])
            P.op("dve", lambda: nc.vector.scalar_tensor_tensor(out=ax[:], in0=x[:], scalar=-1.0, in1=x[:], op0=ALU.mult, op1=ALU.min),
                 [x_r], [ax_r])
            P.op("act", lambda: nc.scalar.activation(out=ax[:], in_=ax[:], func=AF.Exp), [ax_r], [ax_r])
            P.op("act", lambda: nc.scalar.activation(out=ax[:], in_=ax[:], func=AF.Ln, bias=self.onesf[:, 0:1], scale=1.0),
                 [ax_r, self.onesf_r], [ax_r])
            P.op("dve", lambda: nc.vector.scalar_tensor_tensor(out=x[:], in0=x[:], scalar=0.0, in1=ax[:], op0=ALU.max, op1=ALU.add),
                 [x_r, ax_r], [x_r])
            P.op("act", lambda: nc.scalar.activation(out=al[:], in_=al[:], func=AF.Exp), [al_r], [al_r])
            P.op("dve", lambda: nc.vector.scalar_tensor_tensor(out=ax[:], in0=x[:], scalar=-1.0,
                                                               in1=al[:, :].unsqueeze(1).to_broadcast([128, NT, 128]),
                                                               op0=ALU.mult, op1=ALU.mult), [x_r, al_r], [ax_r])
            P.dma("act", self.DTA[:, 0, :].rearrange("(j p) h -> p j h", p=128), x[:], [x_r], self.tr("DTA"), x_r)
            P.dma("act", self.DTA[:, 1, :].rearrange("(j p) h -> p j h", p=128), ax[:], [ax_r], self.tr("DTA"), ax_r)
            P.barrier()
            P.release([x_r, ax_r, bi_r, al_r])

    @_scoped
    def stage_ssd_scan(self, layer, i, d, ctx_out):
        cfg, P, nc = self.cfg, self.P, self.nc
        A = self.sin
        S, T, NT = cfg.S, cfg.T, cfg.NT
        G = SSM_G
        nlat = S // 128
        if d == 0:
            order = list(range(nlat, NT)) + list(range(nlat))
        else:
            order = list(range(NT - 1, nlat - 1, -1)) + list(range(nlat - 1, -1, -1))
        with ExitStack() as st:
            mk, mk_r = P.sb(st, "smask", [128, 4, 128], F32)
            P.dma("sp", mk[:], A["masks"][:, :, :], self.tr("ssm_masks"), [mk_r], mk_r)
            m_le, m_gt = (mk[:, 0, :], mk[:, 1, :]) if d == 0 else (mk[:, 2, :], mk[:, 3, :])
            xsb = [P.sb(st, "sxs%d" % k, [128, 64, 64], F32) for k in range(2)]
            yac, yac_r = P.sb(st, "syac", [128, SSM_INNER], F32)
            xdt, xdt_r = P.sb(st, "sxdt", [128, 64, 64], BF16)
            xds, xds_r = P.sb(st, "sxds", [128, 64, 64], BF16)
            dta, dta_r = P.sb(st, "sdta", [128, 2, 128], F32)
            eq, eq_r = P.sb(st, "seq", [128, 192], F32)
            dd, dd_r = P.sb(st, "sdd", [128, 64], F32)
            btm, btm_r = P.sb(st, "sbtm", [128, G * 128], BF16)
            bct, bct_r = P.sb(st, "sbct", [128, 16, 128], BF16)
            cbm = [P.sb(st, "scbm%d" % k, [128, 128], F32) for k in range(2)]
            Rt = [P.sb(st, "sR%d" % k, [128, 8, 128], F32) for k in range(2)]
            Lh = [P.sb(st, "sLh%d" % k, [128, 8, 128], F32) for k in range(2)]
            Mt = [P.sb(st, "sM%d" % k, [128, 8, 128], BF16) for k in range(2)]
            tmp = [P.sb(st, "stmp%d" % k, [128, 8, 64], F32) for k in range(2)]
            hf, hf_r = P.sb(st, "shf", [128, G, 512], F32)
            hb = [P.sb(st, "shb%d" % g, [128, 512], BF16) for g in range(G)]
            P.op("dve", lambda: nc.vector.memset(hf[:], 0.0), [], [hf_r])
            for g in range(G):
                P.op("pool", lambda g=g: nc.gpsimd.memset(hb[g][0][:], 0.0), [], [hb[g][1]])
            if d == 1:
                yfb = [P.sb(st, "syf%d" % k, [128, SSM_INNER], F32) for k in range(2)]
                szb = [P.sb(st, "ssz%d" % k, [128, SSM_INNER], F32) for k in range(2)]
                dsk, dsk_r = P.sb(st, "sdsk", [128, 128], F32)
                ngt, ng_r = P.sb(st, "sng", [128, 32], F32)
                ss, ss_r = P.sb(st, "sss", [128, 8], F32)
                ytr, ytr_r = P.sb(st, "sytr", [128, 32, 128], BF16)
                P.dma("sp", dsk[:], A["d_skip"][i], self.tr("ssm_d"), [dsk_r], dsk_r)
                P.dma("sp", ngt[:], A["norm_gT"][i], self.tr("ssm_norm_gT"), [ng_r], ng_r)
                P.op("dve", lambda: nc.vector.tensor_tensor(out=dsk[:, 0:64], in0=dsk[:, 0:64], in1=dsk[:, 64:128], op=ALU.add),
                     [dsk_r], [dsk_r])
            YTv = self.YT.rearrange("(c p) t -> p c t", p=128)
            def scan_loads(oi):
                j = order[oi]
                t0 = j * 128
                want_y = (j < nlat) or ctx_out
                xs, xs_r = xsb[oi % 2]
                P.dma("sp", xs[:].rearrange("p h e -> p (h e)"), self.XS[t0:t0 + 128, :], self.tr("XS"), [xs_r], xs_r)
                if d == 1 and want_y:
                    yf, yf_r = yfb[oi % 2]
                    sz, sz_r = szb[oi % 2]
                    P.dma("sp", yf[:], self.YF[t0:t0 + 128, :], self.tr("YF"), [yf_r], yf_r)
                    P.dma("sp", sz[:], self.SZ[t0:t0 + 128, :], self.tr("SZ"), [sz_r], sz_r)
            scan_loads(0)
            for oi, j in enumerate(order):
                t0 = j * 128
                is_ctx = j >= nlat
                want_y = (not is_ctx) or ctx_out
                xs, xs_r = xsb[oi % 2]
                if d == 1:
                    yf, yf_r = yfb[oi % 2]
                    sz, sz_r = szb[oi % 2]
                if oi + 1 < len(order):
                    scan_loads(oi + 1)
                P.dma("sp", dta[:], self.DTA[t0:t0 + 128, :, :], self.tr("DTA"), [dta_r], dta_r)
                P.dma("sp", btm[:], self.BTM[t0:t0 + 128, :], self.tr("BTM"), [btm_r], btm_r)
                P.dma("sp", bct[:], self.BCT[:, :, t0:t0 + 128].rearrange("c n t -> n c t"), self.tr("BCT"), [bct_r], bct_r)
                dt_d = dta[:, 0, d * 64:(d + 1) * 64]
                a_d = dta[:, 1, d * 64:(d + 1) * 64]
                pc, pc_r = self.ps[7], self.psr[7]
                P.group("pe", [
                    lambda: nc.tensor.matmul(pc[:, 0:64], m_le, a_d, start=True, stop=True),
                    lambda: nc.tensor.matmul(pc[:, 64:128], m_gt, a_d, start=True, stop=True),
                    lambda: nc.tensor.matmul(pc[:, 128:192], self.onesf[:], a_d, start=True, stop=True)],
                    [mk_r, dta_r, self.onesf_r], [pc_r])
                P.op("act", lambda: nc.scalar.activation(out=eq[:, :], in_=pc[:, 0:192], func=AF.Exp), [pc_r], [eq_r])
                P.op("dve", lambda: nc.vector.tensor_tensor(out=dd[:, :], in0=dt_d, in1=eq[:, 64:128], op=ALU.mult), [dta_r, eq_r], [dd_r])
                if want_y:
                    P.op("dve", lambda: nc.vector.tensor_tensor(out=xdt[:], in0=xs[:], in1=dt_d.unsqueeze(2).to_broadcast([128, 64, 64]),
                                                                op=ALU.mult), [xs_r, dta_r], [xdt_r])
                P.op("pool", lambda: nc.gpsimd.tensor_tensor(out=xds[:], in0=xs[:], in1=dd[:, :].unsqueeze(2).to_broadcast([128, 64, 64]),
                                                             op=ALU.mult), [xs_r, dd_r], [xds_r])
                for g in range(G):
                    k2 = g % 2
                    if want_y:
                        pcb, pcb_r = self.ps[0], self.psr[0]
                        P.group("pe", [lambda g=g: nc.tensor.matmul(pcb[:, 0:128], bct[:, g, :], bct[:, 8 + g, :], start=True, stop=True)],
                                [bct_r], [pcb_r])
                        cm, cm_r = cbm[k2]
                        P.op("dve", lambda cm=cm: nc.vector.tensor_tensor(out=cm[:, :], in0=pcb[:, 0:128], in1=m_le, op=ALU.mult),
                             [pcb_r, mk_r], [cm_r])
                        R, R_r = Rt[k2]
                        P.op("pool", lambda R=R, g=g: nc.gpsimd.tensor_tensor(
                            out=R[:], in0=a_d[:, g * 8:(g + 1) * 8].unsqueeze(2).to_broadcast([128, 8, 128]),
                            in1=m_le.unsqueeze(1).to_broadcast([128, 8, 128]), op=ALU.mult), [dta_r, mk_r], [R_r])
                        L, L_r = Lh[k2]
                        for hh in range(2):
                            pd, pd_r = self.ps[1 + hh], self.psr[1 + hh]
                            P.group("pe", [lambda R=R, hh=hh, pd=pd: nc.tensor.matmul(
                                pd[:, :], m_gt, R[:, hh * 4:(hh + 1) * 4, :].rearrange("p a l -> p (a l)"), start=True, stop=True)],
                                [mk_r, R_r], [pd_r])
                            P.op("act", lambda L=L, hh=hh, pd=pd: nc.scalar.activation(
                                out=L[:, hh * 4:(hh + 1) * 4, :].rearrange("p a l -> p (a l)"), in_=pd[:, :], func=AF.Exp), [pd_r], [L_r])
                        M, M_r = Mt[k2]
                        P.op("dve", lambda M=M, L=L, cm=cm: nc.vector.tensor_tensor(
                            out=M[:], in0=L[:], in1=cm[:, :].unsqueeze(1).to_broadcast([128, 8, 128]), op=ALU.mult),
                            [L_r, cm_r], [M_r])
                        pyd, pyd_r = self.ps[3], self.psr[3]
                        P.group("pe", [lambda M=M, g=g, jh=jh: nc.tensor.matmul(
                            pyd[:, jh * 64:(jh + 1) * 64], M[:, jh, :], xdt[:, g * 8 + jh, :], start=True, stop=True) for jh in range(8)],
                            [M_r, xdt_r], [pyd_r])
                        pyo, pyo_r = self.ps[4], self.psr[4]
                        P.group("pe", [lambda g=g: nc.tensor.matmul(pyo[:, :], bct[:, 8 + g, :], hb[g][0][:, :], start=True, stop=True)],
                                [bct_r, hb[g][1]], [pyo_r])
                        tp, tp_r = tmp[k2]
                        P.op("dve", lambda tp=tp, g=g: nc.vector.tensor_tensor(
                            out=tp[:], in0=pyo[:, :].rearrange("p (a e) -> p a e", e=64),
                            in1=eq[:, g * 8:(g + 1) * 8].unsqueeze(2).to_broadcast([128, 8, 64]), op=ALU.mult), [pyo_r, eq_r], [tp_r])
                        P.op("dve", lambda tp=tp, g=g: nc.vector.tensor_tensor(
                            out=yac[:, g * 512:(g + 1) * 512], in0=pyd[:, :], in1=tp[:].rearrange("p a e -> p (a e)"), op=ALU.add),
                            [pyd_r, tp_r], [yac_r])
                    pst, pst_r = self.ps[5 + g % 2], self.psr[5 + g % 2]
                    P.group("pe", [lambda g=g, pst=pst: nc.tensor.matmul(
                        pst[:, :], btm[:, g * 128:(g + 1) * 128], xds[:, g * 8:(g + 1) * 8, :].rearrange("p a e -> p (a e)"),
                        start=True, stop=True)], [btm_r, xds_r], [pst_r])
                    hv = hf[:, g, :].rearrange("p (a e) -> p a e", e=64)
                    P.op("pool", lambda g=g, hv=hv: nc.gpsimd.tensor_tensor(
                        out=hv, in0=hv, in1=eq[:, 128 + g * 8:128 + (g + 1) * 8].unsqueeze(2).to_broadcast([128, 8, 64]), op=ALU.mult),
                        [hf_r, eq_r], [hf_r])
                    P.op("dve", lambda g=g, pst=pst: nc.vector.tensor_tensor(out=hf[:, g, :], in0=hf[:, g, :], in1=pst[:, :], op=ALU.add),
                         [hf_r, pst_r], [hf_r])
                    P.op("act", lambda g=g: nc.scalar.copy(out=hb[g][0][:, :], in_=hf[:, g, :]), [hf_r], [hb[g][1]])
                if not want_y:
                    continue
                if d == 0:
                    P.dma("act", self.YF[t0:t0 + 128, :], yac[:, :], [yac_r], self.tr("YF"), yac_r)
                    continue
                P.op("dve", lambda: nc.vector.tensor_tensor(out=yac[:, :], in0=yac[:, :], in1=yf[:, :], op=ALU.add), [yac_r, yf_r], [yac_r])
                P.op("pool", lambda: nc.gpsimd.tensor_tensor(out=yf[:, :].rearrange("p (h e) -> p h e", e=64), in0=xs[:],
                                                             in1=dsk[:, 0:64].unsqueeze(2).to_broadcast([128, 64, 64]), op=ALU.mult),
                     [xs_r, dsk_r], [yf_r])
                P.op("dve", lambda: nc.vector.tensor_tensor(out=yac[:, :], in0=yac[:, :], in1=yf[:, :], op=ALU.add), [yac_r, yf_r], [yac_r])
                P.op("dve", lambda: nc.vector.tensor_tensor(out=yac[:, :], in0=yac[:, :], in1=sz[:, :], op=ALU.mult), [yac_r, sz_r], [yac_r])
                for g in range(G):
                    P.op("act", lambda g=g: nc.scalar.activation(out=yf[:, g * 512:(g + 1) * 512], in_=yac[:, g * 512:(g + 1) * 512],
                                                                 func=AF.Square, accum_out=ss[:, g:g + 1]), [yac_r], [yf_r, ss_r])
                P.op("act", lambda: nc.scalar.activation(out=ss[:, :], in_=ss[:, :], func=AF.Sqrt, bias=self.eps_t[:, 0:1], scale=1.0 / 512),
                     [ss_r, self.eps_r], [ss_r])
                P.op("dve", lambda: nc.vector.reciprocal(out=ss[:, :], in_=ss[:, :]), [ss_r], [ss_r])
                P.op("dve", lambda: nc.vector.tensor_tensor(out=yac[:, :].rearrange("p (g e) -> p g e", e=512),
                                                            in0=yac[:, :].rearrange("p (g e) -> p g e", e=512),
                                                            in1=ss[:, :].unsqueeze(2).to_broadcast([128, 8, 512]), op=ALU.mult),
                     [yac_r, ss_r], [yac_r])
                for q in range(8):
                    pst, ps_r = self.ps[q % 2], self.psr[q % 2]
                    P.group("pe", [lambda c=c, pst=pst: nc.tensor.transpose(
                        out=pst[:, (c % 4) * 128:(c % 4 + 1) * 128], in_=yac[:, c * 128:(c + 1) * 128], identity=self.identf[:])
                        for c in range(q * 4, q * 4 + 4)], [yac_r, self.identf_r], [ps_r])
                    for c in range(q * 4, q * 4 + 4):
                        P.op("act", lambda c=c, pst=pst: nc.scalar.activation(
                            out=ytr[:, c, :], in_=pst[:, (c % 4) * 128:(c % 4 + 1) * 128], func=AF.Copy, scale=ngt[:, c:c + 1]),
                            [ps_r, ng_r], [ytr_r])
                P.dma("act", YTv[:, :, t0:t0 + 128], ytr[:], [ytr_r], self.tr("YT"), ytr_r)
            P.barrier()
            rel = [mk_r, yac_r, dta_r, btm_r, bct_r] + [r for _, r in xsb]
            if d == 1:
                rel += [dsk_r, ng_r, ytr_r] + [r for _, r in yfb + szb]
            P.release(rel)

    @_scoped
    def stage_final(self, fin_g):
        cfg, P, nc = self.cfg, self.P, self.nc
        XTv = self.XT.rearrange("(c p) t -> p c t", p=128)
        with ExitStack() as st:
            g, g_r = P.sb(st, "fing", [128, D], F32)
            P.dma("sp", g[:], fin_g[:, :], self.tr("fin_g"), [g_r], g_r)
            xin = [P.sb(st, "fx%d" % i, [128, DC, 128], F32) for i in range(2)]
            xtm = [P.sb(st, "ft%d" % i, [128, D], F32) for i in range(2)]
            junk, junk_r = P.sb(st, "fjunk", [128, D], F32)
            ssq = [P.sb(st, "fss%d" % i, [128, 1], F32) for i in range(2)]
            for i in range(cfg.S // 128):
                t0 = i * 128
                xi, xi_r = xin[i % 2]
                xt, xt_r = xtm[i % 2]
                ss, ss_r = ssq[i % 2]
                P.dma("sp", xi[:], XTv[:, :, t0:t0 + 128], self.tr("XT", t0, 128), [xi_r], xi_r)
                for q in range(DC // 4):
                    pb = (i * (DC // 4) + q) % 8
                    pst, psr = self.ps[pb], self.psr[pb]
                    P.group("pe", [
                        (lambda c=c, pst=pst: nc.tensor.transpose(
                            out=pst[:, (c % 4) * 128:(c % 4 + 1) * 128],
                            in_=xi[:, c, :], identity=self.identf[:]))
                        for c in range(q * 4, q * 4 + 4)], [xi_r, self.identf_r], [psr])
                    P.op("dve", lambda q=q, pst=pst: nc.vector.tensor_copy(out=xt[:, q * 512:(q + 1) * 512], in_=pst[:, :]),
                         [psr], [xt_r])
                P.op("act", lambda: nc.scalar.activation(out=junk[:], in_=xt[:], func=AF.Square, accum_out=ss[:, 0:1]),
                     [xt_r], [junk_r, ss_r])
                P.op("act", lambda: nc.scalar.activation(out=ss[:], in_=ss[:], func=AF.Sqrt, bias=self.eps_t[:, 0:1], scale=1.0 / D),
                     [ss_r, self.eps_r], [ss_r])
                P.op("dve", lambda: nc.vector.reciprocal(out=ss[:], in_=ss[:]), [ss_r], [ss_r])
                P.op("dve", lambda: nc.vector.scalar_tensor_tensor(
                    out=xt[:], in0=xt[:], scalar=ss[:, 0:1], in1=g[:], op0=ALU.mult, op1=ALU.mult),
                    [xt_r, ss_r, g_r], [xt_r])
                P.dma("act", self.out[t0:t0 + 128, :], xt[:], [xt_r], self.tr("out"), xt_r)
            P.barrier()
            P.release([g_r] + [r for _, r in xin + xtm + ssq])


def pmajor(v):
    sh = v.shape[:-1]
    n = v.shape[-1] // 128
    return np.ascontiguousarray(np.swapaxes(v.reshape(sh + (n, 128)), -1, -2))


def pretile(w):
    lead = w.shape[:-2]
    K, N = w.shape[-2:]
    v = w.reshape(lead + (K // 128, 128, N // 128, 128))
    nl = len(lead)
    v = v.transpose(tuple(range(nl)) + (nl + 2, nl + 1, nl + 0, nl + 3))
    return np.ascontiguousarray(v).reshape(lead + (N, K))


def make_in_maps(cfg, inputs, n_cores):
    f = lambda a: np.ascontiguousarray(np.asarray(a, dtype=np.float32))
    depth = cfg.depth
    shared = {
        "mod_w": f(inputs["mod_w"][:depth]),
        "mod_bT": pmajor(f(inputs["mod_b"][:depth])),
        "norm_gT": pmajor(f(inputs["norm_g"][:depth]).reshape(depth, 3 * D)),
        "ffn_w1": pretile(f(inputs["ffn_w1"][:depth])),
        "ffn_w3": pretile(f(inputs["ffn_w3"][:depth])),
        "ffn_w2": pretile(f(inputs["ffn_w2"][:depth])),
        "fin_g": np.ascontiguousarray(np.broadcast_to(f(inputs["final_norm_g"])[None, :], (128, D))),
        "ident": np.eye(128, dtype=np.float32),
    }
    n_even = (depth + 1) // 2
    if n_even:
        w_in = f(inputs["attn_w_in"][:n_even])
        d = np.arange(128)
        perm = np.where(d % 64 < 32, d + 32, d - 32)
        sign = np.where(d % 64 < 32, -1.0, 1.0).astype(np.float32)
        cols = (np.arange(8)[:, None] * 128 + perm[None, :]).reshape(-1)
        shared["attn_w_in"] = w_in
        shared["attn_w_rot"] = np.ascontiguousarray(
            np.concatenate([w_in[:, :, 3072:4096][:, :, cols], w_in[:, :, 4096:5120][:, :, cols]], axis=-1))
        shared["attn_w_out"] = f(inputs["attn_w_out"][:n_even])
        rpb = f(inputs["na_rpb"][:n_even])
        kc = np.arange(64)[:, None]
        qc = np.arange(64)[None, :]
        coff = np.clip(kc - qc, -15, 15) + 15
        shared["na_bias"] = np.ascontiguousarray(rpb[:, :, :, coff])
        cs = np.clip(qc - 8, 0, 64 - 16)
        cmask = ((kc >= cs) & (kc < cs + 16)).astype(np.float32)
        shared["na_cmask"] = np.ascontiguousarray(
            np.broadcast_to(cmask[None, :, None, :], (2, 64, 4, 64)).reshape(128, 4, 64))
        shared["lamT"] = np.ascontiguousarray(np.swapaxes(f(inputs["diff_lambda"][:n_even]), 1, 2))
        shared["subg"] = pmajor(f(inputs["diff_subln_g"][:n_even]))
        quarter = 32
        inv = (1.0 / (10000.0 ** (np.arange(quarter, dtype=np.float32) / quarter))).astype(np.float32)
        t = np.arange(cfg.S)
        row = (t // GRID_W).astype(np.float32)[:, None] * inv
        col = (t % GRID_W).astype(np.float32)[:, None] * inv
        ang = np.concatenate([row, row, col, col], axis=-1).astype(np.float32)
        cosT = np.ones((128, cfg.T), np.float32)
        sinT = np.zeros((128, cfg.T), np.float32)
        cosT[:, :cfg.S] = np.cos(ang).T
        sinT[:, :cfg.S] = np.sin(ang).T * sign[:, None]
        shared["ropec"] = cosT
        shared["ropes"] = sinT
    n_odd = depth // 2
    if n_odd:
        rep = lambda a: np.ascontiguousarray(np.broadcast_to(a.reshape(n_odd, 1, 128), (n_odd, 128, 128)))
        shared["ssm_w_in"] = f(inputs["ssm_w_in"][:n_odd])
        shared["ssm_w_out"] = f(inputs["ssm_w_out"][:n_odd])
        cwt = f(inputs["ssm_conv_w"][:n_odd])
        shared["conv_wT"] = np.ascontiguousarray(cwt.reshape(n_odd, 4, 48, 128).transpose(0, 3, 2, 1))
        shared["conv_bT"] = pmajor(f(inputs["ssm_conv_b"][:n_odd]))
        shared["ssm_a_log"] = rep(f(inputs["ssm_a_log"][:n_odd]))
        shared["ssm_dt_bias"] = rep(f(inputs["ssm_dt_bias"][:n_odd]))
        shared["ssm_d"] = rep(f(inputs["ssm_d"][:n_odd]))
        shared["ssm_norm_gT"] = pmajor(f(inputs["ssm_norm_g"][:n_odd]))
        u = np.arange(128)[:, None]
        l = np.arange(128)[None, :]
        shared["ssm_masks"] = np.ascontiguousarray(
            np.stack([u <= l, u > l, u >= l, u < l], axis=1).astype(np.float32))
    cc = pmajor(f(inputs["c_ctx"]))
    maps = []
    for b in range(n_cores):
        m = dict(shared)
        m["x"] = f(inputs["x"][b])
        m["ctx"] = f(inputs["ctx"][b])
        cb = pmajor(f(inputs["c"][b]))
        m["cT"] = np.ascontiguousarray(np.stack([cb, cc], axis=-1))
        maps.append(m)
    return maps


_CACHE = {}


def run(cfg, inputs, n_cores, mixers=True, trace=False):
    import time as _t
    t0 = _t.time()
    b = Builder(cfg, mixers)
    nc = b.build()
    print("[kernel] build %.1fs n_ins=%d n_wait=%d" % (_t.time() - t0, b.P.n_ins, b.P.n_wait), flush=True)
    t0 = _t.time()
    maps = make_in_maps(cfg, inputs, n_cores)
    print("[kernel] layout %.1fs" % (_t.time() - t0), flush=True)
    t0 = _t.time()
    if trace:
        res = run_bass_kernel_spmd(nc, maps, core_ids=list(range(n_cores)), trace=True)
        print("[kernel] exec_time_ns", res.exec_time_ns, flush=True)
        if res.per_core_scope_times:
            for k in sorted(res.per_core_scope_times, key=lambda k: k.split("_")[-1]):
                print("[scope] %-28s %s" % (k, res.per_core_scope_times[k]), flush=True)
    else:
        res = run_bass_kernel_spmd(nc, maps, core_ids=list(range(n_cores)))
    print("[kernel] run %.1fs" % (_t.time() - t0), flush=True)
    return np.stack([np.asarray(r["out"]) for r in res.results], axis=0)


def kernel(**inputs):
    cfg = Cfg(seq=4096, depth=4)
    return run(cfg, inputs, 8).astype(np.float32)
```

```python
import math
from contextlib import ExitStack
import numpy as np
import concourse.bass as bass
import concourse.mybir as mybir
from concourse.bass_utils import run_bass_kernel_spmd

F32 = mybir.dt.float32
BF16 = mybir.dt.bfloat16
AF = mybir.ActivationFunctionType
ALU = mybir.AluOpType
AX = mybir.AxisListType

D = 2048
DC = D // 128
CTX = 256
GRID_W = 64
N_MOD = 9
FFN = 5632
FC = FFN // 128
HD = 128
NA_H = 8
DF_H = 4
ATT_W = 6144
SSM_INNER = 4096
SSM_H = 64
SSM_P = 64
SSM_N = 128
SSM_G = 8
SSM_CONV_CH = SSM_INNER + 2 * SSM_G * SSM_N
SSM_IN_W = SSM_INNER + SSM_CONV_CH + 2 * SSM_H
EPS = 1e-6


class Res:
    __slots__ = ("name", "w", "r", "dsem", "dkey")

    def __init__(self, name):
        self.name = name
        self.w = {}
        self.r = {}
        self.dsem = None
        self.dkey = None


class Prog:
    ENG = ("pe", "act", "dve", "pool", "sp")

    def __init__(self, n_dma_sems=80):
        self.nc = bass.Bass("TRN2", target_bir_lowering=False)
        nc = self.nc
        self.es = ExitStack()
        self.eng = {"pe": nc.tensor, "act": nc.scalar, "dve": nc.vector, "pool": nc.gpsimd, "sp": nc.sync}
        self.sems = {}
        self.cnt = {}
        for e in self.ENG:
            self.sems["E" + e] = self.es.enter_context(nc.semaphore("E" + e))
            self.cnt["E" + e] = 0
        self.dfree = []
        for i in range(n_dma_sems):
            k = "D%d" % i
            self.sems[k] = self.es.enter_context(nc.semaphore(k))
            self.cnt[k] = 0
            self.dfree.append(k)
        self.seen = {e: {} for e in self.ENG}
        self.n_ins = 0
        self.n_wait = 0
        self.bg = set()

    def sb(self, stack, name, shape, dt):
        self.n_sb = getattr(self, "n_sb", 0) + 1
        name = "%s_%d" % (name, self.n_sb)
        t = stack.enter_context(self.nc.sbuf_tensor(name, list(shape), dt))
        r = Res(name)
        return t, r

    def dsem_for(self, res):
        if res.dkey is None:
            res.dkey = self.dfree.pop()
        return res.dkey

    def release(self, res_list):
        for r in res_list:
            if r.dkey is not None:
                self.dfree.append(r.dkey)
                r.dkey = None

    @staticmethod
    def _merge(dst, src):
        for k, v in src.items():
            if dst.get(k, 0) < v:
                dst[k] = v

    def _need(self, reads, writes):
        need = {}
        for r in reads:
            self._merge(need, r.w)
        for w in writes:
            self._merge(need, w.w)
            self._merge(need, w.r)
        return need

    def _waits(self, e, need):
        seen = self.seen[e]
        own = "E" + e
        lst = []
        for k, v in need.items():
            if e == "pe" and k == own:
                continue
            if seen.get(k, 0) < v:
                lst.append((k, v))
                seen[k] = v
        return lst

    def _commit(self, key, reads, writes):
        ev = {key: self.cnt[key]}
        for r in reads:
            self._merge(r.r, ev)
        for w in writes:
            w.w = dict(ev)
            w.r = {}

    def op(self, e, fn, reads=(), writes=()):
        need = self._need(reads, writes)
        lst = self._waits(e, need)
        eng = self.eng[e]
        for (k, v) in lst[1:]:
            eng.wait_ge(self.sems[k], v)
            self.n_wait += 1
        ins = fn()
        if lst:
            ins._wait_ge(self.sems[lst[0][0]], lst[0][1])
        key = "E" + e
        self.cnt[key] += 1
        ins.then_inc(self.sems[key], 1)
        self.n_ins += 1
        self._commit(key, reads, writes)
        return ins

    def group(self, e, fns, reads=(), writes=()):
        need = self._need(reads, writes)
        lst = self._waits(e, need)
        eng = self.eng[e]
        for (k, v) in lst[1:]:
            eng.wait_ge(self.sems[k], v)
            self.n_wait += 1
        ins = None
        for i, fn in enumerate(fns):
            ins = fn()
            if i == 0 and lst:
                ins._wait_ge(self.sems[lst[0][0]], lst[0][1])
            self.n_ins += 1
        key = "E" + e
        self.cnt[key] += 1
        ins.then_inc(self.sems[key], 1)
        self._commit(key, reads, writes)

    def dma(self, q, out, in_, reads, writes, owner, **kw):
        need = self._need(reads, writes)
        lst = self._waits(q, need)
        eng = self.eng[q]
        for (k, v) in lst:
            eng.wait_ge(self.sems[k], v)
            self.n_wait += 1
        key = self.dsem_for(owner)
        ins = eng.dma_start(out=out, in_=in_, **kw)
        self.cnt[key] += 16
        ins.then_inc(self.sems[key], 16)
        self.n_ins += 1
        self._commit(key, reads, writes)

    def barrier(self):
        for e in self.ENG:
            seen = self.seen[e]
            for k, v in self.cnt.items():
                if v == 0 or k == "E" + e or k in self.bg:
                    continue
                if seen.get(k, 0) < v:
                    self.eng[e].wait_ge(self.sems[k], v)
                    seen[k] = v
                    self.n_wait += 1

    def finish(self):
        for k, v in self.cnt.items():
            if v and self.seen["sp"].get(k, 0) < v and k != "Esp":
                self.eng["sp"].wait_ge(self.sems[k], v)
                self.seen["sp"][k] = v


class Cfg:
    def __init__(self, seq=4096, depth=4):
        self.S = seq
        self.L = CTX
        self.T = seq + CTX
        self.NT = self.T // 128
        self.depth = depth
        self.rows = seq // GRID_W


def blocks_of(cfg, tb):
    out = []
    t = 0
    while t < cfg.S:
        n = min(tb, cfg.S - t)
        out.append((t, n, 0))
        t += n
    t = cfg.S
    while t < cfg.T:
        n = min(tb, cfg.T - t)
        out.append((t, n, 1))
        t += n
    return out


def _scoped(fn):
    def w(self, *a, **k):
        self._scn = getattr(self, "_scn", 0) + 1
        with self.nc.named_scope("%s_%03d" % (fn.__name__, self._scn)):
            return fn(self, *a, **k)
    return w


class Builder:
    def __init__(self, cfg, mixers=True):
        self.cfg = cfg
        self.mixers = mixers
        self.P = Prog()
        self.nc = self.P.nc
        self.dram_in = {}
        self.dres = {}

    def din(self, name, shape, dt=F32):
        t = self.nc.dram_tensor(name, list(shape), dt, kind="ExternalInput").ap()
        self.dram_in[name] = t
        self.dres[name] = [Res(name)]
        return t

    def dscr(self, name, shape, dt, ntiles=1, kind="Internal"):
        t = self.nc.dram_tensor(name, list(shape), dt, kind=kind).ap()
        self.dres[name] = [Res("%s.%d" % (name, i)) for i in range(ntiles)]
        return t

    def tr(self, name, t0=None, n=None):
        rs = self.dres[name]
        if t0 is None or len(rs) == 1:
            return rs
        return rs[t0 // 128:(t0 + n + 127) // 128]

    def build(self):
        cfg, P, nc = self.cfg, self.P, self.nc
        S, L, T, NT = cfg.S, cfg.L, cfg.T, cfg.NT
        depth = cfg.depth
        n_even = (depth + 1) // 2
        n_odd = depth // 2
        x = self.din("x", [S, D])
        ctx = self.din("ctx", [L, D])
        cT = self.din("cT", [128, DC, 2])
        mod_w = self.din("mod_w", [depth, D, N_MOD * D])
        mod_bT = self.din("mod_bT", [depth, 128, N_MOD * DC])
        norm_gT = self.din("norm_gT", [depth, 128, 3 * DC])
        ffn_w1 = self.din("ffn_w1", [depth, 2, FC * 128, D])
        ffn_w3 = self.din("ffn_w3", [depth, 2, FC * 128, D])
        ffn_w2 = self.din("ffn_w2", [depth, 2, DC * 128, FFN])
        fin_g = self.din("fin_g", [128, D])
        if n_even:
            attn_w_in = self.din("attn_w_in", [n_even, D, ATT_W])
            attn_w_rot = self.din("attn_w_rot", [n_even, D, 2048])
            attn_w_out = self.din("attn_w_out", [n_even, D, D])
            na_bias = self.din("na_bias", [n_even, NA_H, 15, 64, 64])
            na_cmask = self.din("na_cmask", [128, 4, 64])
            lamT = self.din("lamT", [n_even, 128, 4])
            subg = self.din("subg", [n_even, 128, 2])
            ropec = self.din("ropec", [128, T])
            ropes = self.din("ropes", [128, T])
            self.ain = dict(w_in=attn_w_in, w_rot=attn_w_rot, w_out=attn_w_out, na_bias=na_bias,
                            na_cmask=na_cmask, lamT=lamT, subg=subg, ropec=ropec, ropes=ropes)
            self.awb = self.dscr("awb", [D, ATT_W], BF16)
            self.arb = self.dscr("arb", [D, 2048], BF16)
            self.aob = self.dscr("aob", [D, D], BF16)
            self.QKT = self.dscr("QKT", [32, 128, T], BF16)
            self.VTM = self.dscr("VTM", [T, 2048], BF16)
            self.CAT = self.dscr("CAT", [D, T], BF16)
        ident = self.din("ident", [128, 128])
        if n_odd:
            self.sin = dict(
                w_in=self.din("ssm_w_in", [n_odd, D, SSM_IN_W]),
                w_out=self.din("ssm_w_out", [n_odd, SSM_INNER, D]),
                conv_wT=self.din("conv_wT", [n_odd, 128, 48, 4]),
                conv_bT=self.din("conv_bT", [n_odd, 128, 48]),
                a_log=self.din("ssm_a_log", [n_odd, 128, 128]),
                dt_bias=self.din("ssm_dt_bias", [n_odd, 128, 128]),
                d_skip=self.din("ssm_d", [n_odd, 128, 128]),
                norm_gT=self.din("ssm_norm_gT", [n_odd, 128, 32]),
                masks=self.din("ssm_masks", [128, 4, 128]))
            self.swb = self.dscr("swb", [D, SSM_IN_W], BF16)
            self.sob = self.dscr("sob", [SSM_INNER, D], BF16)
            self.SZ = self.dscr("SZ", [T, SSM_INNER], F32)
            self.XBC = self.dscr("XBC", [48, 128, T], F32)
            self.DTR = self.dscr("DTR", [T, 128], F32)
            self.DTA = self.dscr("DTA", [T, 2, 128], F32)
            self.XS = self.dscr("XS", [T, SSM_INNER], F32)
            self.BCT = self.dscr("BCT", [16, 128, T], BF16)
            self.BTM = self.dscr("BTM", [T, SSM_G * SSM_N], BF16)
            self.YF = self.dscr("YF", [T, SSM_INNER], F32)
            self.YT = self.dscr("YT", [SSM_INNER, T], BF16)
        out = self.dscr("out", [S, D], F32, ntiles=1, kind="ExternalOutput")
        self.out = out
        XT = self.dscr("XT", [D, T], F32, ntiles=NT)
        self.XT = XT
        w1b = self.dscr("w1b", [2, FC, 128, D], BF16, ntiles=2)
        w3b = self.dscr("w3b", [2, FC, 128, D], BF16, ntiles=2)
        w2b = self.dscr("w2b", [2, DC, 128, FFN], BF16, ntiles=2)

        es = P.es
        self.ps = []
        self.psr = []
        for i in range(8):
            t = es.enter_context(nc.psum_tensor("ps%d" % i, [128, 512], F32))
            self.ps.append(t)
            self.psr.append(Res("ps%d" % i))
        self.identf, self.identf_r = P.sb(es, "identf", [128, 128], F32)
        self.identb, self.identb_r = P.sb(es, "identb", [128, 128], BF16)
        self.onesf, self.onesf_r = P.sb(es, "onesf", [128, 128], F32)
        self.onesb, self.onesb_r = P.sb(es, "onesb", [128, 128], BF16)
        self.modt, self.modt_r = P.sb(es, "modt", [128, N_MOD * DC, 2], F32)
        self.gs, self.gs_r = P.sb(es, "gs", [128, 3, DC, 2], F32)
        self.gate, self.gate_r = P.sb(es, "gate", [128, 3, DC, 2], F32)
        self.sc, self.sc_r = P.sb(es, "sc", [128, DC, 2], F32)

        P.dma("sp", self.identf[:], ident[:, :], self.tr("ident"), [self.identf_r], self.identf_r)
        P.op("dve", lambda: nc.vector.tensor_copy(out=self.identb[:], in_=self.identf[:]),
             [self.identf_r], [self.identb_r])
        P.op("dve", lambda: nc.vector.memset(self.onesf[:], 1.0), [], [self.onesf_r])
        P.op("dve", lambda: nc.vector.memset(self.onesb[:], 1.0), [], [self.onesb_r])
        self.eps_t, self.eps_r = P.sb(es, "epsc", [128, 1], F32)
        P.op("dve", lambda: nc.vector.memset(self.eps_t[:], EPS), [], [self.eps_r])
        with ExitStack() as st:
            craw, craw_r = P.sb(st, "craw", [128, DC, 2], F32)
            P.dma("sp", craw[:], cT[:, :, :], self.tr("cT"), [craw_r], craw_r)
            P.op("act", lambda: nc.scalar.activation(out=self.sc[:], in_=craw[:], func=AF.Silu),
                 [craw_r], [self.sc_r])
            P.barrier()
            P.release([craw_r])

        self.stage_in_transpose(x, ctx)
        for layer in range(depth):
            self.stage_wcast_ffn(layer, ffn_w1, ffn_w3, ffn_w2, w1b, w3b, w2b)
            self.stage_mod(layer, mod_w, mod_bT, norm_gT)
            self.stage_ffn(layer, 0, w1b, w3b, w2b)
            if self.mixers:
                ctx_out = layer < depth - 1
                if layer % 2 == 0:
                    self.stage_attn(layer, ctx_out)
                else:
                    self.stage_ssd(layer, ctx_out)
            self.stage_ffn(layer, 1, w1b, w3b, w2b)
        self.stage_final(fin_g)
        P.barrier()
        P.finish()
        return nc

    @_scoped
    def stage_in_transpose(self, x, ctx):
        cfg, P, nc = self.cfg, self.P, self.nc
        XTv = self.XT.rearrange("(c p) t -> p c t", p=128)
        with ExitStack() as st:
            NB = 2
            xin = [P.sb(st, "xin%d" % i, [128, D], F32) for i in range(NB)]
            xo = [P.sb(st, "xo%d" % i, [128, DC, 128], F32) for i in range(NB)]
            for i in range(cfg.NT):
                t0 = i * 128
                xi, xi_r = xin[i % NB]
                xoT, xo_r = xo[i % NB]
                src = x[t0:t0 + 128, :] if t0 < cfg.S else ctx[t0 - cfg.S:t0 - cfg.S + 128, :]
                srcres = self.tr("x") if t0 < cfg.S else self.tr("ctx")
                P.dma("sp", xi[:], src, srcres, [xi_r], xi_r)
                for q in range(DC // 4):
                    pb = (i * (DC // 4) + q) % 8
                    pst, psr = self.ps[pb], self.psr[pb]
                    P.group("pe", [
                        (lambda c=c, pst=pst: nc.tensor.transpose(
                            out=pst[:, (c % 4) * 128:(c % 4 + 1) * 128],
                            in_=xi[:, c * 128:(c + 1) * 128], identity=self.identf[:]))
                        for c in range(q * 4, q * 4 + 4)],
                        [xi_r, self.identf_r], [psr])
                    eng = "dve" if q % 2 == 0 else "act"
                    dst = xoT[:, q * 4:(q + 1) * 4, :]
                    srcp = pst[:, :].rearrange("p (c t) -> p c t", c=4)
                    if eng == "dve":
                        P.op("dve", lambda dst=dst, srcp=srcp: nc.vector.tensor_copy(out=dst, in_=srcp),
                             [psr], [xo_r])
                    else:
                        P.op("act", lambda dst=dst, srcp=srcp: nc.scalar.copy(out=dst, in_=srcp),
                             [psr], [xo_r])
                P.dma("act", XTv[:, :, t0:t0 + 128], xoT[:], [xo_r], self.tr("XT", t0, 128), xo_r)
            P.barrier()
            P.release([r for _, r in xin] + [r for _, r in xo])

    def stage_wcast_ffn(self, layer, w1, w3, w2, w1b, w3b, w2b):
        for f in range(2):
            for (src, dst, K, name) in ((w1, w1b, FC * 128, "w1b"), (w3, w3b, FC * 128, "w3b"), (w2, w2b, DC * 128, "w2b")):
                self.wcast(src[layer, f], dst[f].rearrange("c p n -> (c p) n"), K, name, res=[self.dres[name][f]])

    @_scoped
    def stage_mod(self, layer, mod_w, mod_bT, norm_gT):
        P, nc = self.P, self.nc
        NB = 512
        nblk = N_MOD * D // NB
        mwv = mod_w[layer].rearrange("(kt p) n -> p kt n", p=128)
        with ExitStack() as st:
            wt = [P.sb(st, "modw%d" % i, [128, DC, NB], F32) for i in range(3)]
            mb, mb_r = P.sb(st, "modb", [128, N_MOD * DC], F32)
            ng, ng_r = P.sb(st, "normg", [128, 3 * DC], F32)
            P.dma("sp", mb[:], mod_bT[layer], self.tr("mod_bT"), [mb_r], mb_r)
            P.dma("sp", ng[:], norm_gT[layer], self.tr("norm_gT"), [ng_r], ng_r)
            for b in range(nblk):
                w, w_r = wt[b % 3]
                for h in range(4):
                    P.dma("sp" if h % 2 == 0 else "act", w[:, h * 4:(h + 1) * 4, :],
                          mwv[:, h * 4:(h + 1) * 4, b * NB:(b + 1) * NB],
                          self.tr("mod_w"), [w_r], w_r)
                pb = b % 2
                pst, psr = self.ps[pb], self.psr[pb]
                fns = []
                for j in range(NB // 128):
                    for kt in range(DC):
                        fns.append(lambda j=j, kt=kt, w=w, pst=pst: nc.tensor.matmul(
                            pst[:, 2 * j:2 * j + 2], w[:, kt, j * 128:(j + 1) * 128], self.sc[:, kt, :],
                            start=(kt == 0), stop=(kt == DC - 1)))
                P.group("pe", fns, [w_r, self.sc_r], [psr])
                nj = NB // 128
                P.op("dve", lambda b=b, pst=pst: nc.vector.tensor_tensor(
                    out=self.modt[:, b * nj:(b + 1) * nj, :],
                    in0=pst[:, 0:2 * nj].rearrange("p (j s) -> p j s", s=2),
                    in1=mb[:, b * nj:(b + 1) * nj].unsqueeze(2).to_broadcast([128, nj, 2]),
                    op=ALU.add), [psr, mb_r], [self.modt_r])
            for i in range(3):
                sh = self.modt[:, (3 * i + 0) * DC:(3 * i + 1) * DC, :]
                scl = self.modt[:, (3 * i + 1) * DC:(3 * i + 2) * DC, :]
                gt = self.modt[:, (3 * i + 2) * DC:(3 * i + 3) * DC, :]
                P.op("dve", lambda i=i, scl=scl: nc.vector.scalar_tensor_tensor(
                    out=self.gs[:, i, :, :], in0=scl, scalar=1.0,
                    in1=ng[:, i * DC:(i + 1) * DC].unsqueeze(2).to_broadcast([128, DC, 2]),
                    op0=ALU.add, op1=ALU.mult), [self.modt_r, ng_r], [self.gs_r])
                fac = 1.0 if i == 1 else 0.5
                P.op("dve", lambda i=i, gt=gt, fac=fac: nc.vector.tensor_scalar(
                    out=self.gate[:, i, :, :], in0=gt, scalar1=fac, scalar2=None, op0=ALU.mult),
                    [self.modt_r], [self.gate_r])
            P.barrier()
            P.release([r for _, r in wt] + [mb_r, ng_r])

    def shift_ap(self, i, c, s):
        return self.modt[:, 3 * i * DC + c, s:s + 1]

    def norm_mod(self, st_tiles, i, t0, n, s, hT, hT_r, off=0):
        P, nc = self.P, self.nc
        XTv = self.XT.rearrange("(c p) t -> p c t", p=128)
        xs, sq, rstd, tmp = st_tiles
        (rstd_t, rstd_r) = rstd
        pst, psr = self.ps[7], self.psr[7]
        xres = self.tr("XT", t0, n)
        fns = []
        for c in range(DC):
            xt, xt_r = xs[c % len(xs)]
            sqt, sq_r = sq[c % len(sq)]
            P.dma("sp", xt[:, :n], XTv[:, c, t0:t0 + n], xres, [xt_r], xt_r)
            P.op("act", lambda xt=xt, sqt=sqt: nc.scalar.activation(out=sqt[:, :n], in_=xt[:, :n], func=AF.Square),
                 [xt_r], [sq_r])
            P.group("pe", [lambda sqt=sqt, c=c: nc.tensor.matmul(
                pst[:, :n], self.onesf[:], sqt[:, :n], start=(c == 0), stop=(c == DC - 1))],
                [sq_r, self.onesf_r], [psr])
        P.op("act", lambda: nc.scalar.activation(out=rstd_t[:, :n], in_=pst[:, :n], func=AF.Sqrt,
                                                 bias=self.eps_t[:, 0:1], scale=1.0 / D), [psr, self.eps_r], [rstd_r])
        P.op("dve", lambda: nc.vector.reciprocal(out=rstd_t[:, :n], in_=rstd_t[:, :n]), [rstd_r], [rstd_r])
        for c in range(DC):
            xt, xt_r = xs[c % len(xs)]
            tt, tt_r = tmp[c % len(tmp)]
            P.dma("sp", xt[:, :n], XTv[:, c, t0:t0 + n], xres, [xt_r], xt_r)
            P.op("dve", lambda xt=xt, tt=tt, c=c: nc.vector.scalar_tensor_tensor(
                out=tt[:, :n], in0=xt[:, :n], scalar=self.gs[:, i, c, s:s + 1], in1=rstd_t[:, :n],
                op0=ALU.mult, op1=ALU.mult), [xt_r, rstd_r, self.gs_r], [tt_r])
            P.op("act", lambda tt=tt, c=c: nc.scalar.activation(
                out=hT[:, c, off:off + n], in_=tt[:, :n], func=AF.Identity,
                bias=self.shift_ap(i, c, s), scale=1.0), [tt_r, self.modt_r], [hT_r])

    def norm_tiles(self, st, tb):
        P = self.P
        xs = [P.sb(st, "nx%d" % i, [128, tb], F32) for i in range(4)]
        sq = [P.sb(st, "nsq%d" % i, [128, tb], F32) for i in range(2)]
        rstd = P.sb(st, "nrstd", [128, tb], F32)
        tmp = [P.sb(st, "ntmp%d" % i, [128, tb], F32) for i in range(2)]
        return (xs, sq, rstd, tmp), [r for _, r in xs] + [r for _, r in sq] + [rstd[1]] + [r for _, r in tmp]

    @_scoped
    def stage_ffn(self, layer, f, w1b, w3b, w2b):
        cfg, P, nc = self.cfg, self.P, self.nc
        TB = 1024
        slot = 0 if f == 0 else 2
        XTv = self.XT.rearrange("(c p) t -> p c t", p=128)
        with ExitStack() as st:
            ntl, ntl_res = self.norm_tiles(st, 512)
            hT, hT_r = P.sb(st, "hT", [128, DC, TB], BF16)
            gT, gT_r = P.sb(st, "gT", [128, FC, TB], BF16)
            wa = [P.sb(st, "wa%d" % i, [128, DC * 128], BF16) for i in range(3)]
            wb = [P.sb(st, "wb%d" % i, [128, DC * 128], BF16) for i in range(3)]
            wd = [P.sb(st, "wd%d" % i, [128, FC * 128], BF16) for i in range(2)]
            sil = [P.sb(st, "sil%d" % i, [128, 512], F32) for i in range(2)]
            xr = [P.sb(st, "xr%d" % i, [128, 512], F32) for i in range(2)]
            yo = [P.sb(st, "yo%d" % i, [128, 512], F32) for i in range(2)]
            w1r, w3r, w2r = [self.dres["w1b"][f]], [self.dres["w3b"][f]], [self.dres["w2b"][f]]
            it = 0
            it2 = 0
            for (t0, n, s) in blocks_of(cfg, TB):
                halves = [(h0, min(512, n - h0)) for h0 in range(0, n, 512)]
                for (h0, hn) in halves:
                    self.norm_mod(ntl, slot, t0 + h0, hn, s, hT, hT_r, off=h0)

                def load_up(fc):
                    a, a_r = wa[fc % 3]
                    b, b_r = wb[fc % 3]
                    P.dma("sp", a[:], w1b[f, fc], w1r, [a_r], a_r)
                    P.dma("sp", b[:], w3b[f, fc], w3r, [b_r], b_r)
                load_up(0)
                load_up(1)
                for fc in range(FC):
                    if fc + 2 < FC:
                        load_up(fc + 2)
                    a, a_r = wa[fc % 3]
                    b, b_r = wb[fc % 3]
                    for (h0, hn) in halves:
                        k = it % 3
                        pa, pa_r = self.ps[2 * k], self.psr[2 * k]
                        pb, pb_r = self.ps[2 * k + 1], self.psr[2 * k + 1]
                        sl, sl_r = sil[it % 2]
                        it += 1
                        P.group("pe", [lambda kt=kt, a=a, pa=pa, h0=h0, hn=hn: nc.tensor.matmul(
                            pa[:, :hn], a[:, kt * 128:(kt + 1) * 128], hT[:, kt, h0:h0 + hn],
                            start=(kt == 0), stop=(kt == DC - 1)) for kt in range(DC)],
                            [a_r, hT_r], [pa_r])
                        P.group("pe", [lambda kt=kt, b=b, pb=pb, h0=h0, hn=hn: nc.tensor.matmul(
                            pb[:, :hn], b[:, kt * 128:(kt + 1) * 128], hT[:, kt, h0:h0 + hn],
                            start=(kt == 0), stop=(kt == DC - 1)) for kt in range(DC)],
                            [b_r, hT_r], [pb_r])
                        P.op("act", lambda pa=pa, sl=sl, hn=hn: nc.scalar.activation(out=sl[:, :hn], in_=pa[:, :hn], func=AF.Silu),
                             [pa_r], [sl_r])
                        P.op("dve", lambda pb=pb, sl=sl, fc=fc, h0=h0, hn=hn: nc.vector.tensor_tensor(
                            out=gT[:, fc, h0:h0 + hn], in0=sl[:, :hn], in1=pb[:, :hn], op=ALU.mult),
                            [sl_r, pb_r], [gT_r])

                def load_dn(dc):
                    w, w_r = wd[dc % 2]
                    hk = (FC // 2) * 128
                    P.dma("sp", w[:, :hk], w2b[f, dc, :, :hk], w2r, [w_r], w_r)
                    P.dma("sp", w[:, hk:], w2b[f, dc, :, hk:], w2r, [w_r], w_r)
                load_dn(0)
                for dc in range(DC):
                    if dc + 1 < DC:
                        load_dn(dc + 1)
                    w, w_r = wd[dc % 2]
                    for (h0, hn) in halves:
                        py, py_r = self.ps[it2 % 6], self.psr[it2 % 6]
                        xrt, xr_r = xr[it2 % 2]
                        yot, yo_r = yo[it2 % 2]
                        it2 += 1
                        P.dma("sp", xrt[:, :hn], XTv[:, dc, t0 + h0:t0 + h0 + hn], self.tr("XT", t0 + h0, hn), [xr_r], xr_r)
                        P.group("pe", [lambda kt=kt, w=w, py=py, h0=h0, hn=hn: nc.tensor.matmul(
                            py[:, :hn], w[:, kt * 128:(kt + 1) * 128], gT[:, kt, h0:h0 + hn],
                            start=(kt == 0), stop=(kt == FC - 1)) for kt in range(FC)],
                            [w_r, gT_r], [py_r])
                        P.op("dve", lambda py=py, xrt=xrt, yot=yot, dc=dc, hn=hn: nc.vector.scalar_tensor_tensor(
                            out=yot[:, :hn], in0=py[:, :hn], scalar=self.gate[:, slot, dc, s:s + 1], in1=xrt[:, :hn],
                            op0=ALU.mult, op1=ALU.add), [py_r, xr_r, self.gate_r], [yo_r])
                        P.dma("act", XTv[:, dc, t0 + h0:t0 + h0 + hn], yot[:, :hn], [yo_r], self.tr("XT", t0 + h0, hn), yo_r)
            P.barrier()
            P.release(ntl_res + [r for _, r in wa + wb + wd + xr + yo])

    def wcast(self, src2d, dst2d, K, name, rb=256, res=None):
        P = self.P
        if not hasattr(self, "wc_r"):
            self.wc_r = Res("wcast")
        for k0 in range(0, K, rb):
            P.dma("pool", dst2d[k0:k0 + rb, :], src2d[k0:k0 + rb, :], [], res or self.tr(name), self.wc_r)
        P.bg.add(self.wc_r.dkey)

    def lin_tiles(self, st, KT, NW=256, tag="l"):
        return [self.P.sb(st, "%sw%d" % (tag, i), [128, KT, NW], BF16) for i in range(3)]

    def lin_fm(self, wt, hT, hT_r, KT, wv, wname, cols, n, epi, NW=256, pbanks=(0, 1)):
        P, nc = self.P, self.nc
        c0, c1 = cols
        blocks = list(range(c0, c1, NW))
        nb = len(wt)
        hk = KT // 2

        def load(bi):
            w, w_r = wt[bi % nb]
            b0 = blocks[bi]
            P.dma("sp", w[:, :hk, :], wv[:, :hk, b0:b0 + NW], self.tr(wname), [w_r], w_r)
            P.dma("sp", w[:, hk:, :], wv[:, hk:, b0:b0 + NW], self.tr(wname), [w_r], w_r)
        for bi in range(min(nb - 1, len(blocks))):
            load(bi)
        it = 0
        for bi, b0 in enumerate(blocks):
            if bi + nb - 1 < len(blocks):
                load(bi + nb - 1)
            w, w_r = wt[bi % nb]
            for j in range(NW // 128):
                pb = pbanks[it % len(pbanks)]
                it += 1
                ps, ps_r = self.ps[pb], self.psr[pb]
                P.group("pe", [lambda kt=kt, w=w, j=j, ps=ps: nc.tensor.matmul(
                    ps[:, :n], w[:, kt, j * 128:(j + 1) * 128], hT[:, kt, :n],
                    start=(kt == 0), stop=(kt == KT - 1)) for kt in range(KT)],
                    [w_r, hT_r], [ps_r])
                epi(ps, ps_r, (b0 - c0) // 128 + j)

    def stage_attn(self, layer, ctx_out):
        cfg, P, nc = self.cfg, self.P, self.nc
        i = layer // 2
        A = self.ain
        self.wcast(A["w_in"][i], self.awb, D, "awb")
        self.wcast(A["w_rot"][i], self.arb, D, "arb")
        self.wcast(A["w_out"][i], self.aob, D, "aob")
        self.stage_attn_inproj(layer, i)
        self.stage_na(layer, i, ctx_out)
        self.stage_diff(layer, i, ctx_out)
        self.stage_outproj(self.CAT, "CAT", DC, self.aob.rearrange("(kt p) n -> p kt n", p=128), "aob", ctx_out)

    @_scoped
    def stage_attn_inproj(self, layer, i):
        cfg, P, nc = self.cfg, self.P, self.nc
        A = self.ain
        TB = 512
        wv = self.awb.rearrange("(kt p) n -> p kt n", p=128)
        rv = self.arb.rearrange("(kt p) n -> p kt n", p=128)
        with ExitStack() as st:
            ntl, ntl_res = self.norm_tiles(st, TB)
            hT, hT_r = P.sb(st, "ahT", [128, DC, TB], BF16)
            cosb, cos_r = P.sb(st, "cosb", [128, TB], F32)
            sinb, sin_r = P.sb(st, "sinb", [128, TB], F32)
            qo = [P.sb(st, "qo%d" % k, [128, TB], BF16) for k in range(3)]
            t1 = [P.sb(st, "rt1%d" % k, [128, TB], F32) for k in range(2)]
            t2 = [P.sb(st, "rt2%d" % k, [128, TB], F32) for k in range(2)]
            vw = [P.sb(st, "vw%d" % k, [128, DC, 512], BF16) for k in range(2)]
            vo = [P.sb(st, "vo%d" % k, [128, 512], BF16) for k in range(3)]
            wrot = [P.sb(st, "wrot%d" % k, [128, DC, 256], BF16) for k in range(2)]
            wlin = self.lin_tiles(st, DC, tag="ap")
            rel = [r for _, r in wlin]
            cnt = [0, 0, 0]
            for (t0, n, s) in blocks_of(cfg, TB):
                self.norm_mod(ntl, 1, t0, n, s, hT, hT_r)
                P.dma("sp", cosb[:, :n], A["ropec"][:, t0:t0 + n], self.tr("ropec"), [cos_r], cos_r)
                P.dma("sp", sinb[:, :n], A["ropes"][:, t0:t0 + n], self.tr("ropes"), [sin_r], sin_r)

                def epi_plain(ps, ps_r, j, t0=t0, n=n):
                    q, q_r = qo[cnt[0] % 3]
                    cnt[0] += 1
                    P.op("act", lambda: nc.scalar.copy(out=q[:, :n], in_=ps[:, :n]), [ps_r], [q_r])
                    P.dma("act", self.QKT[j, :, t0:t0 + n], q[:, :n], [q_r], self.tr("QKT"), q_r)
                self.lin_fm(wlin, hT, hT_r, DC, wv, "awb", (0, 2048), n, epi_plain)

                for (c0, r0, ch0) in ((3072, 0, 16), (4096, 1024, 24)):
                    for bi in range(4):
                        w, w_r = wrot[cnt[1] % 2]
                        P.dma("sp", w[:], rv[:, :, r0 + bi * 256:r0 + (bi + 1) * 256], self.tr("arb"), [w_r], w_r)
                        wq, wq_r = vw[cnt[1] % 2]
                        cnt[1] += 1
                        P.dma("sp", wq[:, :, 0:256], wv[:, :, c0 + bi * 256:c0 + (bi + 1) * 256], self.tr("awb"), [wq_r], wq_r)
                        for j in range(2):
                            kk = 2 + 2 * (j % 2)
                            pa, pa_r = self.ps[kk], self.psr[kk]
                            pb, pb_r = self.ps[kk + 1], self.psr[kk + 1]
                            P.group("pe", [lambda kt=kt, wq=wq, j=j, pa=pa: nc.tensor.matmul(
                                pa[:, :n], wq[:, kt, j * 128:(j + 1) * 128], hT[:, kt, :n],
                                start=(kt == 0), stop=(kt == DC - 1)) for kt in range(DC)], [wq_r, hT_r], [pa_r])
                            P.group("pe", [lambda kt=kt, w=w, j=j, pb=pb: nc.tensor.matmul(
                                pb[:, :n], w[:, kt, j * 128:(j + 1) * 128], hT[:, kt, :n],
                                start=(kt == 0), stop=(kt == DC - 1)) for kt in range(DC)], [w_r, hT_r], [pb_r])
                            a, a_r = t1[cnt[2] % 2]
                            b, b_r = t2[cnt[2] % 2]
                            cnt[2] += 1
                            q, q_r = qo[cnt[0] % 3]
                            cnt[0] += 1
                            P.op("dve", lambda a=a, pa=pa: nc.vector.tensor_tensor(out=a[:, :n], in0=pa[:, :n], in1=cosb[:, :n], op=ALU.mult),
                                 [pa_r, cos_r], [a_r])
                            P.op("dve", lambda b=b, pb=pb: nc.vector.tensor_tensor(out=b[:, :n], in0=pb[:, :n], in1=sinb[:, :n], op=ALU.mult),
                                 [pb_r, sin_r], [b_r])
                            P.op("pool", lambda a=a, b=b, q=q: nc.gpsimd.tensor_tensor(out=q[:, :n], in0=a[:, :n], in1=b[:, :n], op=ALU.add),
                                 [a_r, b_r], [q_r])
                            ch = ch0 + bi * 2 + j
                            P.dma("act", self.QKT[ch, :, t0:t0 + n], q[:, :n], [q_r], self.tr("QKT"), q_r)

                for (c0, o0) in ((2048, 0), (2560, 512), (5120, 1024), (5632, 1536)):
                    w, w_r = vw[cnt[1] % 2]
                    cnt[1] += 1
                    P.dma("sp", w[:, :8, :], wv[:, :8, c0:c0 + 512], self.tr("awb"), [w_r], w_r)
                    P.dma("sp", w[:, 8:, :], wv[:, 8:, c0:c0 + 512], self.tr("awb"), [w_r], w_r)
                    for tt in range(n // 128):
                        pv, pv_r = self.ps[4 + tt % 2], self.psr[4 + tt % 2]
                        P.group("pe", [lambda kt=kt, w=w, tt=tt, pv=pv: nc.tensor.matmul(
                            pv[:, :], hT[:, kt, tt * 128:(tt + 1) * 128], w[:, kt, :],
                            start=(kt == 0), stop=(kt == DC - 1)) for kt in range(DC)], [w_r, hT_r], [pv_r])
                        v, v_r = vo[cnt[0] % 3]
                        cnt[0] += 1
                        P.op("act", lambda v=v, pv=pv: nc.scalar.copy(out=v[:, :], in_=pv[:, :]), [pv_r], [v_r])
                        P.dma("act", self.VTM[t0 + tt * 128:t0 + (tt + 1) * 128, o0:o0 + 512], v[:, :], [v_r],
                              self.tr("VTM"), v_r)
            P.barrier()
            P.release(ntl_res + rel + [cos_r, sin_r] + [r for _, r in qo + vw + vo + wrot])

    @_scoped
    def stage_na(self, layer, i, ctx_out):
        cfg, P, nc = self.cfg, self.P, self.nc
        A = self.ain
        S, T, NT, rows = cfg.S, cfg.T, cfg.NT, cfg.rows
        kr = 8
        scale = HD ** -0.5
        CATv = self.CAT.rearrange("(c p) t -> c p t", p=128)
        with ExitStack() as st:
            cm, cm_r = P.sb(st, "cmask", [128, 4, 64], F32)
            P.dma("sp", cm[:], A["na_cmask"][:, :, :], self.tr("na_cmask"), [cm_r], cm_r)
            qT, q_r = P.sb(st, "naq", [128, S], BF16)
            kT, k_r = P.sb(st, "nak", [128, T], BF16)
            ve, ve_r = P.sb(st, "nave", [128, NT, 128], BF16)
            vod, vo_r = P.sb(st, "navo", [128, NT - 1, 128], BF16)
            Wm = [P.sb(st, "naW%d" % k, [128, 4, 64], F32) for k in range(8)]
            oT, o_r = P.sb(st, "naoT", [128, T], BF16)
            E = [P.sb(st, "naE%d" % k, [128, 256], F32) for k in range(2)]
            Pt = [P.sb(st, "naP%d" % k, [128, 384], BF16) for k in range(2)]
            rc = [P.sb(st, "narc%d" % k, [128, 64], F32) for k in range(2)]
            Pc, Pc_r = P.sb(st, "naPc", [128, 512], BF16)
            rcc, rcc_r = P.sb(st, "narcc", [128, 256], F32)
            cqT, cqT_r = P.sb(st, "nacq", [128, 256], BF16)
            for h in range(NA_H):
                P.dma("sp", qT[:], self.QKT[h, :, 0:S], self.tr("QKT"), [q_r], q_r)
                P.dma("sp", kT[:], self.QKT[8 + h, :, :], self.tr("QKT"), [k_r], k_r)
                P.dma("sp", ve[:], self.VTM[:, h * 128:(h + 1) * 128].rearrange("(j p) e -> p j e", p=128),
                      self.tr("VTM"), [ve_r], ve_r)
                P.dma("sp", vod[:], self.VTM[64:T - 64, h * 128:(h + 1) * 128].rearrange("(j p) e -> p j e", p=128),
                      self.tr("VTM"), [vo_r], vo_r)
                for dl in range(8):
                    W, W_r = Wm[dl]
                    dr0 = 7 - dl
                    src = A["na_bias"][i, h, dr0:dr0 + 8].rearrange("(a i2) kc qc -> (i2 kc) a qc", i2=2)
                    P.dma("sp", W[:], src, self.tr("na_bias"), [W_r], W_r)
                    P.op("act", lambda W=W: nc.scalar.activation(out=W[:], in_=W[:], func=AF.Exp), [W_r], [W_r])
                    P.op("dve", lambda W=W: nc.vector.tensor_tensor(out=W[:], in0=W[:], in1=cm[:], op=ALU.mult),
                         [W_r, cm_r], [W_r])
                for r in range(rows):
                    rs = min(max(r - kr // 2, 0), rows - kr)
                    dl = r - rs
                    W, W_r = Wm[dl]
                    ps, ps_r = self.ps[r % 2], self.psr[r % 2]
                    po, po_r = self.ps[2 + r % 2], self.psr[2 + r % 2]
                    Et, E_r = E[r % 2]
                    Pp, Pp_r = Pt[r % 2]
                    rct, rc_r = rc[r % 2]
                    qs = qT[:, r * 64:(r + 1) * 64]
                    k0 = rs * 64
                    fns = [lambda a=a: nc.tensor.matmul(ps[:, a * 64:(a + 1) * 64], kT[:, k0 + a * 128:k0 + (a + 1) * 128], qs,
                                                        start=True, stop=True) for a in range(4)]
                    fns += [lambda a=a: nc.tensor.matmul(ps[:, (4 + a) * 64:(5 + a) * 64], kT[:, S + a * 128:S + (a + 1) * 128], qs,
                                                         start=True, stop=True) for a in range(2)]
                    P.group("pe", fns, [k_r, q_r], [ps_r])
                    P.op("act", lambda: nc.scalar.activation(out=Et[:, :], in_=ps[:, 0:256], func=AF.Exp, scale=scale),
                         [ps_r], [E_r])
                    P.op("act", lambda: nc.scalar.activation(out=Pp[:, 256:384], in_=ps[:, 256:384], func=AF.Exp, scale=scale),
                         [ps_r], [Pp_r])
                    P.op("dve", lambda: nc.vector.tensor_tensor(out=Pp[:, 0:256], in0=Et[:, :],
                                                                in1=W[:].rearrange("p a q -> p (a q)"), op=ALU.mult),
                         [E_r, W_r], [Pp_r])
                    if rs % 2 == 0:
                        vt = [ve[:, rs // 2 + a, :] for a in range(4)]
                    else:
                        vt = [vod[:, (rs - 1) // 2 + a, :] for a in range(4)]
                    vt += [ve[:, S // 128 + a, :] for a in range(2)]
                    fns = [lambda a=a: nc.tensor.matmul(po[:, 0:64], vt[a], Pp[:, a * 64:(a + 1) * 64],
                                                        start=(a == 0), stop=(a == 5)) for a in range(6)]
                    fns += [lambda a=a: nc.tensor.matmul(po[:, 64:128], self.onesb[:], Pp[:, a * 64:(a + 1) * 64],
                                                         start=(a == 0), stop=(a == 5)) for a in range(6)]
                    P.group("pe", fns, [Pp_r, ve_r, vo_r, self.onesb_r], [po_r])
                    P.op("dve", lambda: nc.vector.reciprocal(out=rct[:, :], in_=po[:, 64:128]), [po_r], [rc_r])
                    P.op("dve", lambda: nc.vector.tensor_tensor(out=oT[:, r * 64:(r + 1) * 64], in0=po[:, 0:64], in1=rct[:, :],
                                                                op=ALU.mult), [po_r, rc_r], [o_r])
                if ctx_out:
                    ps, ps_r = self.ps[4], self.psr[4]
                    po, po_r = self.ps[5], self.psr[5]
                    P.dma("sp", cqT[:], self.QKT[h, :, S:T], self.tr("QKT"), [cqT_r], cqT_r)
                    P.group("pe", [lambda a=a: nc.tensor.matmul(ps[:, a * 256:(a + 1) * 256], kT[:, S + a * 128:S + (a + 1) * 128],
                                                                cqT[:], start=True, stop=True) for a in range(2)],
                            [k_r, cqT_r], [ps_r])
                    P.op("act", lambda: nc.scalar.activation(out=Pc[:, :], in_=ps[:, :], func=AF.Exp, scale=scale), [ps_r], [Pc_r])
                    fns = [lambda a=a: nc.tensor.matmul(po[:, 0:256], ve[:, S // 128 + a, :], Pc[:, a * 256:(a + 1) * 256],
                                                        start=(a == 0), stop=(a == 1)) for a in range(2)]
                    fns += [lambda a=a: nc.tensor.matmul(po[:, 256:512], self.onesb[:], Pc[:, a * 256:(a + 1) * 256],
                                                         start=(a == 0), stop=(a == 1)) for a in range(2)]
                    P.group("pe", fns, [Pc_r, ve_r, self.onesb_r], [po_r])
                    P.op("dve", lambda: nc.vector.reciprocal(out=rcc[:, :], in_=po[:, 256:512]), [po_r], [rcc_r])
                    P.op("dve", lambda: nc.vector.tensor_tensor(out=oT[:, S:T], in0=po[:, 0:256], in1=rcc[:, :], op=ALU.mult),
                         [po_r, rcc_r], [o_r])
                nst = T if ctx_out else S
                P.dma("act", CATv[h, :, 0:nst], oT[:, 0:nst], [o_r], self.tr("CAT"), o_r)
            P.barrier()
            P.release([cm_r, q_r, k_r, ve_r, vo_r, o_r, cqT_r] + [r for _, r in Wm])

    @_scoped
    def stage_diff(self, layer, i, ctx_out):
        cfg, P, nc = self.cfg, self.P, self.nc
        A = self.ain
        S, T, NT = cfg.S, cfg.T, cfg.NT
        scale = HD ** -0.5
        lam_init = 0.8 - 0.6 * math.exp(-0.3 * layer)
        CATv = self.CAT.rearrange("(c p) t -> c p t", p=128)
        with ExitStack() as st:
            lt, lt_r = P.sb(st, "lamt", [128, 4], F32)
            sg, sg_r = P.sb(st, "subg", [128, 2], F32)
            lp, lp_r = P.sb(st, "lamp", [128, 2], F32)
            nl, nl_r = P.sb(st, "neglam", [128, 1], F32)
            P.dma("sp", lt[:], A["lamT"][i], self.tr("lamT"), [lt_r], lt_r)
            P.dma("sp", sg[:], A["subg"][i], self.tr("subg"), [sg_r], sg_r)
            P.op("dve", lambda: nc.vector.tensor_tensor(out=lp[:, 0:1], in0=lt[:, 0:1], in1=lt[:, 1:2], op=ALU.mult), [lt_r], [lp_r])
            P.op("dve", lambda: nc.vector.tensor_tensor(out=lp[:, 1:2], in0=lt[:, 2:3], in1=lt[:, 3:4], op=ALU.mult), [lt_r], [lp_r])
            ps, ps_r = self.ps[6], self.psr[6]
            P.group("pe", [lambda: nc.tensor.matmul(ps[:, 0:2], self.onesf[:], lp[:, :], start=True, stop=True)],
                    [lp_r, self.onesf_r], [ps_r])
            P.op("act", lambda: nc.scalar.activation(out=lp[:, :], in_=ps[:, 0:2], func=AF.Exp), [ps_r], [lp_r])
            P.op("dve", lambda: nc.vector.scalar_tensor_tensor(out=nl[:, :], in0=lp[:, 1:2], scalar=-lam_init, in1=lp[:, 0:1],
                                                               op0=ALU.add, op1=ALU.subtract), [lp_r], [nl_r])
            P.op("dve", lambda: nc.vector.tensor_scalar(out=sg[:, :], in0=sg[:, :], scalar1=1.0 - lam_init, scalar2=None, op0=ALU.mult),
                 [sg_r], [sg_r])
            qT = [P.sb(st, "dq%d" % m, [128, T], BF16) for m in range(2)]
            kT = [P.sb(st, "dk%d" % m, [128, T], BF16) for m in range(2)]
            v, v_r = P.sb(st, "dv", [128, NT, 256], BF16)
            Ering = [P.sb(st, "dE%d" % k, [128, 512], BF16) for k in range(4)]
            rec = [P.sb(st, "drec%d" % m, [128, 512], F32) for m in range(2)]
            oa = [P.sb(st, "doa%d" % e, [128, 512], F32) for e in range(2)]
            ob = [P.sb(st, "dob%d" % e, [128, 512], F32) for e in range(2)]
            sq, sq_r = P.sb(st, "dsq", [128, 512], F32)
            rstd, rstd_r = P.sb(st, "drstd", [128, 512], F32)
            oo = [P.sb(st, "doo%d" % e, [128, 512], BF16) for e in range(2)]
            qblocks = [(t0, n, s) for (t0, n, s) in blocks_of(cfg, 512) if s == 0 or ctx_out]
            for h in range(DF_H):
                for m in range(2):
                    P.dma("sp", qT[m][0][:], self.QKT[16 + 2 * h + m, :, :], self.tr("QKT"), [qT[m][1]], qT[m][1])
                    P.dma("sp", kT[m][0][:], self.QKT[24 + 2 * h + m, :, :], self.tr("QKT"), [kT[m][1]], kT[m][1])
                P.dma("sp", v[:], self.VTM[:, 1024 + h * 256:1024 + (h + 1) * 256].rearrange("(j p) e -> p j e", p=128),
                      self.tr("VTM"), [v_r], v_r)
                for (q0, nq, s) in qblocks:
                    ktiles = list(range(NT)) if s == 0 else list(range(S // 128, NT))
                    its = [(ki, kt, m) for ki, kt in enumerate(ktiles) for m in range(2)]

                    def emit_s(idx):
                        ki, kt, m = its[idx]
                        pss, pss_r = self.ps[6 + idx % 2], self.psr[6 + idx % 2]
                        Et, E_r = Ering[idx % 4]
                        P.group("pe", [lambda: nc.tensor.matmul(
                            pss[:, :nq], kT[m][0][:, kt * 128:(kt + 1) * 128], qT[m][0][:, q0:q0 + nq], start=True, stop=True)],
                            [kT[m][1], qT[m][1]], [pss_r])
                        P.op("act", lambda: nc.scalar.activation(out=Et[:, :nq], in_=pss[:, :nq], func=AF.Exp, scale=scale),
                             [pss_r], [E_r])

                    def emit_pv(idx):
                        ki, kt, m = its[idx]
                        Et, E_r = Ering[idx % 4]
                        first, last = (ki == 0), (ki == len(ktiles) - 1)
                        P.group("pe", [
                            lambda: nc.tensor.matmul(self.ps[2 * m][:, :nq], v[:, kt, 0:128], Et[:, :nq], start=first, stop=last),
                            lambda: nc.tensor.matmul(self.ps[2 * m + 1][:, :nq], v[:, kt, 128:256], Et[:, :nq], start=first, stop=last),
                            lambda: nc.tensor.matmul(self.ps[4 + m][:, :nq], self.onesb[:], Et[:, :nq], start=first, stop=last)],
                            [E_r, v_r, self.onesb_r], [self.psr[2 * m], self.psr[2 * m + 1], self.psr[4 + m]])

                    emit_s(0)
                    for idx in range(len(its)):
                        if idx + 1 < len(its):
                            emit_s(idx + 1)
                        emit_pv(idx)
                    P.op("dve", lambda: nc.vector.reciprocal(out=rec[0][0][:, :nq], in_=self.ps[4][:, :nq]), [self.psr[4]], [rec[0][1]])
                    P.op("dve", lambda: nc.vector.reciprocal(out=rec[1][0][:, :nq], in_=self.ps[5][:, :nq]), [self.psr[5]], [rec[1][1]])
                    P.op("dve", lambda: nc.vector.tensor_scalar(out=rec[1][0][:, :nq], in0=rec[1][0][:, :nq], scalar1=nl[:, 0:1],
                                                                scalar2=None, op0=ALU.mult), [rec[1][1], nl_r], [rec[1][1]])
                    for e in range(2):
                        P.op("dve", lambda e=e: nc.vector.tensor_tensor(out=oa[e][0][:, :nq], in0=self.ps[e][:, :nq], in1=rec[0][0][:, :nq],
                                                                        op=ALU.mult), [self.psr[e], rec[0][1]], [oa[e][1]])
                        P.op("dve", lambda e=e: nc.vector.tensor_tensor(out=ob[e][0][:, :nq], in0=self.ps[2 + e][:, :nq], in1=rec[1][0][:, :nq],
                                                                        op=ALU.mult), [self.psr[2 + e], rec[1][1]], [ob[e][1]])
                        P.op("pool", lambda e=e: nc.gpsimd.tensor_tensor(out=oa[e][0][:, :nq], in0=oa[e][0][:, :nq], in1=ob[e][0][:, :nq],
                                                                         op=ALU.add), [oa[e][1], ob[e][1]], [oa[e][1]])
                    pst, pst_r = self.ps[6], self.psr[6]
                    for e in range(2):
                        P.op("act", lambda e=e: nc.scalar.activation(out=sq[:, :nq], in_=oa[e][0][:, :nq], func=AF.Square), [oa[e][1]], [sq_r])
                        P.group("pe", [lambda e=e: nc.tensor.matmul(pst[:, :nq], self.onesf[:], sq[:, :nq], start=(e == 0), stop=(e == 1))],
                                [sq_r, self.onesf_r], [pst_r])
                    P.op("act", lambda: nc.scalar.activation(out=rstd[:, :nq], in_=pst[:, :nq], func=AF.Sqrt, bias=self.eps_t[:, 0:1],
                                                             scale=1.0 / 256), [pst_r, self.eps_r], [rstd_r])
                    P.op("dve", lambda: nc.vector.reciprocal(out=rstd[:, :nq], in_=rstd[:, :nq]), [rstd_r], [rstd_r])
                    for e in range(2):
                        P.op("dve", lambda e=e: nc.vector.scalar_tensor_tensor(
                            out=oo[e][0][:, :nq], in0=oa[e][0][:, :nq], scalar=sg[:, e:e + 1], in1=rstd[:, :nq],
                            op0=ALU.mult, op1=ALU.mult), [oa[e][1], sg_r, rstd_r], [oo[e][1]])
                        P.dma("act", CATv[8 + 2 * h + e, :, q0:q0 + nq], oo[e][0][:, :nq], [oo[e][1]], self.tr("CAT"), oo[e][1])
            P.barrier()
            P.release([lt_r, sg_r, v_r] + [r for _, r in qT + kT + oo])

    @_scoped
    def stage_outproj(self, ACT_T, aname, KT, wv, wname, ctx_out):
        cfg, P, nc = self.cfg, self.P, self.nc
        TB = 512
        XTv = self.XT.rearrange("(c p) t -> p c t", p=128)
        av = ACT_T.rearrange("(c p) t -> p c t", p=128)
        with ExitStack() as st:
            aT, aT_r = P.sb(st, "opa", [128, KT, TB], BF16)
            xr = [P.sb(st, "opx%d" % k, [128, TB], F32) for k in range(2)]
            yo = [P.sb(st, "opy%d" % k, [128, TB], F32) for k in range(2)]
            wlin = self.lin_tiles(st, KT, tag="op")
            rel = [r for _, r in wlin]
            cnt = [0]
            for (t0, n, s) in blocks_of(cfg, TB):
                if s == 1 and not ctx_out:
                    continue
                hk = KT // 2
                P.dma("sp", aT[:, :hk, :n], av[:, :hk, t0:t0 + n], self.tr(aname), [aT_r], aT_r)
                P.dma("sp", aT[:, hk:, :n], av[:, hk:, t0:t0 + n], self.tr(aname), [aT_r], aT_r)

                def epi(ps, ps_r, j, t0=t0, n=n, s=s):
                    xrt, xr_r = xr[cnt[0] % 2]
                    yot, yo_r = yo[cnt[0] % 2]
                    cnt[0] += 1
                    P.dma("sp", xrt[:, :n], XTv[:, j, t0:t0 + n], self.tr("XT", t0, n), [xr_r], xr_r)
                    P.op("dve", lambda: nc.vector.scalar_tensor_tensor(
                        out=yot[:, :n], in0=ps[:, :n], scalar=self.gate[:, 1, j, s:s + 1], in1=xrt[:, :n],
                        op0=ALU.mult, op1=ALU.add), [ps_r, xr_r, self.gate_r], [yo_r])
                    P.dma("act", XTv[:, j, t0:t0 + n], yot[:, :n], [yo_r], self.tr("XT", t0, n), yo_r)
                self.lin_fm(wlin, aT, aT_r, KT, wv, wname, (0, D), n, epi)
            P.barrier()
            P.release(rel + [aT_r] + [r for _, r in xr + yo])

    def lin_tm(self, wt, hT, hT_r, KT, wv, wname, cols, ntt, epi, pbanks=(4, 5)):
        P, nc = self.P, self.nc
        c0, c1 = cols
        blocks = list(range(c0, c1, 512))
        nb = len(wt)
        hk = KT // 2

        def load(bi):
            w, w_r = wt[bi % nb]
            b0 = blocks[bi]
            P.dma("sp", w[:, :hk, :], wv[:, :hk, b0:b0 + 512], self.tr(wname), [w_r], w_r)
            P.dma("sp", w[:, hk:, :], wv[:, hk:, b0:b0 + 512], self.tr(wname), [w_r], w_r)
        for bi in range(min(nb - 1, len(blocks))):
            load(bi)
        it = 0
        for bi, b0 in enumerate(blocks):
            if bi + nb - 1 < len(blocks):
                load(bi + nb - 1)
            w, w_r = wt[bi % nb]
            for tt in range(ntt):
                pb = pbanks[it % len(pbanks)]
                it += 1
                ps, ps_r = self.ps[pb], self.psr[pb]
                P.group("pe", [lambda kt=kt, w=w, tt=tt, ps=ps: nc.tensor.matmul(
                    ps[:, :], hT[:, kt, tt * 128:(tt + 1) * 128], w[:, kt, :],
                    start=(kt == 0), stop=(kt == KT - 1)) for kt in range(KT)], [w_r, hT_r], [ps_r])
                epi(ps, ps_r, bi, tt)

    def stage_ssd(self, layer, ctx_out):
        i = layer // 2
        A = self.sin
        self.wcast(A["w_in"][i], self.swb, D, "swb")
        self.wcast(A["w_out"][i], self.sob, SSM_INNER, "sob")
        self.stage_ssd_inproj(layer, i)
        self.stage_ssd_conv(layer, i)
        self.stage_ssd_dt(layer, i)
        self.stage_ssd_scan(layer, i, 0, ctx_out)
        self.stage_ssd_scan(layer, i, 1, ctx_out)
        self.stage_outproj(self.YT, "YT", SSM_INNER // 128, self.sob.rearrange("(kt p) n -> p kt n", p=128), "sob", ctx_out)

    @_scoped
    def stage_ssd_inproj(self, layer, i):
        cfg, P, nc = self.cfg, self.P, self.nc
        TB = 512
        wv = self.swb.rearrange("(kt p) n -> p kt n", p=128)
        with ExitStack() as st:
            ntl, ntl_res = self.norm_tiles(st, TB)
            hT, hT_r = P.sb(st, "shT", [128, DC, TB], BF16)
            wlin = self.lin_tiles(st, DC, tag="si")
            wtm = [P.sb(st, "stm%d" % k, [128, DC, 512], BF16) for k in range(2)]
            wdt, wdt_r = P.sb(st, "swdt", [128, DC, 128], BF16)
            zo = [P.sb(st, "szo%d" % k, [128, 512], F32) for k in range(3)]
            xo = [P.sb(st, "sxo%d" % k, [128, TB], F32) for k in range(3)]
            do = [P.sb(st, "sdo%d" % k, [128, 128], F32) for k in range(2)]
            cnt = [0, 0, 0]
            for (t0, n, s) in blocks_of(cfg, TB):
                self.norm_mod(ntl, 1, t0, n, s, hT, hT_r)

                def epi_z(ps, ps_r, cb, tt, t0=t0):
                    z, z_r = zo[cnt[0] % 3]
                    cnt[0] += 1
                    P.op("act", lambda: nc.scalar.activation(out=z[:, :], in_=ps[:, :], func=AF.Silu), [ps_r], [z_r])
                    P.dma("act", self.SZ[t0 + tt * 128:t0 + (tt + 1) * 128, cb * 512:(cb + 1) * 512], z[:, :], [z_r],
                          self.tr("SZ"), z_r)
                self.lin_tm(wtm, hT, hT_r, DC, wv, "swb", (0, SSM_INNER), n // 128, epi_z)

                def epi_x(ps, ps_r, j, t0=t0, n=n):
                    xx, x_r = xo[cnt[1] % 3]
                    cnt[1] += 1
                    P.op("act", lambda: nc.scalar.copy(out=xx[:, :n], in_=ps[:, :n]), [ps_r], [x_r])
                    P.dma("act", self.XBC[j, :, t0:t0 + n], xx[:, :n], [x_r], self.tr("XBC"), x_r)
                self.lin_fm(wlin, hT, hT_r, DC, wv, "swb", (SSM_INNER, SSM_INNER + SSM_CONV_CH), n, epi_x)

                P.dma("sp", wdt[:], wv[:, :, SSM_INNER + SSM_CONV_CH:SSM_IN_W], self.tr("swb"), [wdt_r], wdt_r)
                for tt in range(n // 128):
                    ps, ps_r = self.ps[6], self.psr[6]
                    P.group("pe", [lambda kt=kt, tt=tt: nc.tensor.matmul(
                        ps[:, 0:128], hT[:, kt, tt * 128:(tt + 1) * 128], wdt[:, kt, :],
                        start=(kt == 0), stop=(kt == DC - 1)) for kt in range(DC)], [wdt_r, hT_r], [ps_r])
                    dd, d_r = do[cnt[2] % 2]
                    cnt[2] += 1
                    P.op("dve", lambda dd=dd: nc.vector.tensor_copy(out=dd[:, :], in_=ps[:, 0:128]), [ps_r], [d_r])
                    P.dma("act", self.DTR[t0 + tt * 128:t0 + (tt + 1) * 128, :], dd[:, :], [d_r], self.tr("DTR"), d_r)
            P.barrier()
            P.release(ntl_res + [r for _, r in wlin + wtm + zo + xo + do] + [wdt_r])

    @_scoped
    def stage_ssd_conv(self, layer, i):
        cfg, P, nc = self.cfg, self.P, self.nc
        A = self.sin
        S, T, NT = cfg.S, cfg.T, cfg.NT
        with ExitStack() as st:
            cw, cw_r = P.sb(st, "cw", [128, 48, 4], F32)
            cb, cb_r = P.sb(st, "cb", [128, 48], F32)
            P.dma("sp", cw[:], A["conv_wT"][i], self.tr("conv_wT"), [cw_r], cw_r)
            P.dma("sp", cb[:], A["conv_bT"][i], self.tr("conv_bT"), [cb_r], cb_r)
            xin = [P.sb(st, "cxin%d" % k, [128, T], F32) for k in range(2)]
            acc = [P.sb(st, "cacc%d" % k, [128, T], F32) for k in range(2)]
            sf = [P.sb(st, "csf%d" % k, [128, T], F32) for k in range(2)]
            sbf = [P.sb(st, "csb%d" % k, [128, T], BF16) for k in range(2)]
            tmf = [P.sb(st, "ctmf%d" % k, [128, NT, 128], F32) for k in range(2)]
            tmb = [P.sb(st, "ctmb%d" % k, [128, NT, 128], BF16) for k in range(2)]
            for c in range(48):
                x, x_r = xin[c % 2]
                a, a_r = acc[c % 2]
                P.dma("sp", x[:], self.XBC[c, :, :], self.tr("XBC"), [x_r], x_r)
                for (lo, hi) in ((0, S), (S, T)):
                    P.op("act", lambda lo=lo, hi=hi: nc.scalar.activation(
                        out=a[:, lo:hi], in_=x[:, lo:hi], func=AF.Identity, bias=cb[:, c:c + 1], scale=cw[:, c, 1:2]),
                        [x_r, cw_r, cb_r], [a_r])
                    for (k, dlo, dhi, slo, shi) in ((0, lo + 1, hi, lo, hi - 1), (2, lo, hi - 1, lo + 1, hi), (3, lo, hi - 2, lo + 2, hi)):
                        P.op("dve", lambda k=k, dlo=dlo, dhi=dhi, slo=slo, shi=shi: nc.vector.scalar_tensor_tensor(
                            out=a[:, dlo:dhi], in0=x[:, slo:shi], scalar=cw[:, c, k:k + 1], in1=a[:, dlo:dhi],
                            op0=ALU.mult, op1=ALU.add), [x_r, a_r, cw_r], [a_r])
                if c < 32:
                    s_, s_r = sf[c % 2]
                    tm, tm_r = tmf[c % 2]
                    P.op("act", lambda: nc.scalar.activation(out=s_[:, :], in_=a[:, :], func=AF.Silu), [a_r], [s_r])
                    for q in range((NT + 3) // 4):
                        js = list(range(q * 4, min(q * 4 + 4, NT)))
                        pst, ps_r = self.ps[q % 4], self.psr[q % 4]
                        P.group("pe", [lambda j=j, pst=pst: nc.tensor.transpose(
                            out=pst[:, (j % 4) * 128:(j % 4 + 1) * 128], in_=s_[:, j * 128:(j + 1) * 128], identity=self.identf[:])
                            for j in js], [s_r, self.identf_r], [ps_r])
                        dst = tm[:, js[0]:js[-1] + 1, :]
                        src = pst[:, 0:len(js) * 128].rearrange("p (j e) -> p j e", e=128)
                        if q % 2 == 0:
                            P.op("dve", lambda dst=dst, src=src: nc.vector.tensor_copy(out=dst, in_=src), [ps_r], [tm_r])
                        else:
                            P.op("act", lambda dst=dst, src=src: nc.scalar.copy(out=dst, in_=src), [ps_r], [tm_r])
                    P.dma("act", self.XS[:, c * 128:(c + 1) * 128].rearrange("(j p) e -> p j e", p=128), tm[:], [tm_r],
                          self.tr("XS"), tm_r)
                else:
                    s_, s_r = sbf[c % 2]
                    P.op("act", lambda: nc.scalar.activation(out=s_[:, :], in_=a[:, :], func=AF.Silu), [a_r], [s_r])
                    P.dma("act", self.BCT[c - 32, :, :], s_[:, :], [s_r], self.tr("BCT"), s_r)
                    if c < 40:
                        tm, tm_r = tmb[c % 2]
                        for q in range((NT + 3) // 4):
                            js = list(range(q * 4, min(q * 4 + 4, NT)))
                            pst, ps_r = self.ps[4 + q % 4], self.psr[4 + q % 4]
                            pv = pst[:, :].bitcast(BF16)
                            P.group("pe", [lambda j=j, pv=pv: nc.tensor.transpose(
                                out=pv[:, (j % 4) * 128:(j % 4 + 1) * 128], in_=s_[:, j * 128:(j + 1) * 128], identity=self.identb[:])
                                for j in js], [s_r, self.identb_r], [ps_r])
                            dst = tm[:, js[0]:js[-1] + 1, :]
                            src = pv[:, 0:len(js) * 128].rearrange("p (j e) -> p j e", e=128)
                            P.op("dve", lambda dst=dst, src=src: nc.vector.tensor_copy(out=dst, in_=src), [ps_r], [tm_r])
                        g = c - 32
                        P.dma("act", self.BTM[:, g * 128:(g + 1) * 128].rearrange("(j p) e -> p j e", p=128), tm[:], [tm_r],
                              self.tr("BTM"), tm_r)
            P.barrier()
            P.release([cw_r, cb_r] + [r for _, r in xin + acc + sf + sbf + tmf + tmb])

    @_scoped
    def stage_ssd_dt(self, layer, i):
        cfg, P, nc = self.cfg, self.P, self.nc
        A = self.sin
        NT = cfg.NT
        with ExitStack() as st:
            x, x_r = P.sb(st, "dtx", [128, NT, 128], F32)
            ax, ax_r = P.sb(st, "dtax", [128, NT, 128], F32)
            bi, bi_r = P.sb(st, "dtb", [128, 128], F32)
            al, al_r = P.sb(st, "dtal", [128, 128], F32)
            P.dma("sp", x[:], self.DTR.rearrange("(j p) h -> p j h", p=128), self.tr("DTR"), [x_r], x_r)
            P.dma("sp", bi[:], A["dt_bias"][i], self.tr("ssm_dt_bias"), [bi_r], bi_r)
            P.dma("sp", al[:], A["a_log"][i], self.tr("ssm_a_log"), [al_r], al_r)
            bb = bi[:, :].unsqueeze(1).to_broadcast([128, NT, 128])
            P.op("dve", lambda: nc.vector.tensor_tensor(out=x[:], in0=x[:], in1=bb, op=ALU.add), [x_r, bi_r], [x_r])
            P.op("dve", lambda: nc.vector.scalar_tensor_tensor(out=ax[:], in0=x[:], scalar=-1.0, in1=x[:], op0=ALU.mult, op1=ALU.min),
                 [x_r], [ax_r])
            P.op("act", lambda: nc.scalar.activation(out=ax[:], in_=ax[:], func=AF.Exp), [ax_r], [ax_r])
            P.op("act", lambda: nc.scalar.activation(out=ax[:], in_=ax[:], func=AF.Ln, bias=self.onesf[:, 0:1], scale=1.0),
                 [ax_r, self.onesf_r], [ax_r])
            P.op("dve", lambda: nc.vector.scalar_tensor_tensor(out=x[:], in0=x[:], scalar=0.0, in1=ax[:], op0=ALU.max, op1=ALU.add),
                 [x_r, ax_r], [x_r])
            P.op("act", lambda: nc.scalar.activation(out=al[:], in_=al[:], func=AF.Exp), [al_r], [al_r])
            P.op("dve", lambda: nc.vector.scalar_tensor_tensor(out=ax[:], in0=x[:], scalar=-1.0,
                                                               in1=al[:, :].unsqueeze(1).to_broadcast([128, NT, 128]),
                                                               op0=ALU.mult, op1=ALU.mult), [x_r, al_r], [ax_r])
            P.dma("act", self.DTA[:, 0, :].rearrange("(j p) h -> p j h", p=128), x[:], [x_r], self.tr("DTA"), x_r)
            P.dma("act", self.DTA[:, 1, :].rearrange("(j p) h -> p j h", p=128), ax[:], [ax_r], self.tr("DTA"), ax_r)
            P.barrier()
            P.release([x_r, ax_r, bi_r, al_r])

    @_scoped
    def stage_ssd_scan(self, layer, i, d, ctx_out):
        cfg, P, nc = self.cfg, self.P, self.nc
        A = self.sin
        S, T, NT = cfg.S, cfg.T, cfg.NT
        G = SSM_G
        nlat = S // 128
        if d == 0:
            order = list(range(nlat, NT)) + list(range(nlat))
        else:
            order = list(range(NT - 1, nlat - 1, -1)) + list(range(nlat - 1, -1, -1))
        with ExitStack() as st:
            mk, mk_r = P.sb(st, "smask", [128, 4, 128], F32)
            P.dma("sp", mk[:], A["masks"][:, :, :], self.tr("ssm_masks"), [mk_r], mk_r)
            m_le, m_gt = (mk[:, 0, :], mk[:, 1, :]) if d == 0 else (mk[:, 2, :], mk[:, 3, :])
            xsb = [P.sb(st, "sxs%d" % k, [128, 64, 64], F32) for k in range(2)]
            yac, yac_r = P.sb(st, "syac", [128, SSM_INNER], F32)
            xdt, xdt_r = P.sb(st, "sxdt", [128, 64, 64], BF16)
            xds, xds_r = P.sb(st, "sxds", [128, 64, 64], BF16)
            dta, dta_r = P.sb(st, "sdta", [128, 2, 128], F32)
            eq, eq_r = P.sb(st, "seq", [128, 192], F32)
            dd, dd_r = P.sb(st, "sdd", [128, 64], F32)
            btm, btm_r = P.sb(st, "sbtm", [128, G * 128], BF16)
            bct, bct_r = P.sb(st, "sbct", [128, 16, 128], BF16)
            cbm = [P.sb(st, "scbm%d" % k, [128, 128], F32) for k in range(2)]
            Rt = [P.sb(st, "sR%d" % k, [128, 8, 128], F32) for k in range(2)]
            Lh = [P.sb(st, "sLh%d" % k, [128, 8, 128], F32) for k in range(2)]
            Mt = [P.sb(st, "sM%d" % k, [128, 8, 128], BF16) for k in range(2)]
            tmp = [P.sb(st, "stmp%d" % k, [128, 8, 64], F32) for k in range(2)]
            hf, hf_r = P.sb(st, "shf", [128, G, 512], F32)
            hb = [P.sb(st, "shb%d" % g, [128, 512], BF16) for g in range(G)]
            P.op("dve", lambda: nc.vector.memset(hf[:], 0.0), [], [hf_r])
            for g in range(G):
                P.op("pool", lambda g=g: nc.gpsimd.memset(hb[g][0][:], 0.0), [], [hb[g][1]])
            if d == 1:
                yfb = [P.sb(st, "syf%d" % k, [128, SSM_INNER], F32) for k in range(2)]
                szb = [P.sb(st, "ssz%d" % k, [128, SSM_INNER], F32) for k in range(2)]
                dsk, dsk_r = P.sb(st, "sdsk", [128, 128], F32)
                ngt, ng_r = P.sb(st, "sng", [128, 32], F32)
                ss, ss_r = P.sb(st, "sss", [128, 8], F32)
                ytr, ytr_r = P.sb(st, "sytr", [128, 32, 128], BF16)
                P.dma("sp", dsk[:], A["d_skip"][i], self.tr("ssm_d"), [dsk_r], dsk_r)
                P.dma("sp", ngt[:], A["norm_gT"][i], self.tr("ssm_norm_gT"), [ng_r], ng_r)
                P.op("dve", lambda: nc.vector.tensor_tensor(out=dsk[:, 0:64], in0=dsk[:, 0:64], in1=dsk[:, 64:128], op=ALU.add),
                     [dsk_r], [dsk_r])
            YTv = self.YT.rearrange("(c p) t -> p c t", p=128)
            def scan_loads(oi):
                j = order[oi]
                t0 = j * 128
                want_y = (j < nlat) or ctx_out
                xs, xs_r = xsb[oi % 2]
                P.dma("sp", xs[:].rearrange("p h e -> p (h e)"), self.XS[t0:t0 + 128, :], self.tr("XS"), [xs_r], xs_r)
                if d == 1 and want_y:
                    yf, yf_r = yfb[oi % 2]
                    sz, sz_r = szb[oi % 2]
                    P.dma("sp", yf[:], self.YF[t0:t0 + 128, :], self.tr("YF"), [yf_r], yf_r)
                    P.dma("sp", sz[:], self.SZ[t0:t0 + 128, :], self.tr("SZ"), [sz_r], sz_r)
            scan_loads(0)
            for oi, j in enumerate(order):
                t0 = j * 128
                is_ctx = j >= nlat
                want_y = (not is_ctx) or ctx_out
                xs, xs_r = xsb[oi % 2]
                if d == 1:
                    yf, yf_r = yfb[oi % 2]
                    sz, sz_r = szb[oi % 2]
                if oi + 1 < len(order):
                    scan_loads(oi + 1)
                P.dma("sp", dta[:], self.DTA[t0:t0 + 128, :, :], self.tr("DTA"), [dta_r], dta_r)
                P.dma("sp", btm[:], self.BTM[t0:t0 + 128, :], self.tr("BTM"), [btm_r], btm_r)
                P.dma("sp", bct[:], self.BCT[:, :, t0:t0 + 128].rearrange("c n t -> n c t"), self.tr("BCT"), [bct_r], bct_r)
                dt_d = dta[:, 0, d * 64:(d + 1) * 64]
                a_d = dta[:, 1, d * 64:(d + 1) * 64]
                pc, pc_r = self.ps[7], self.psr[7]
                P.group("pe", [
                    lambda: nc.tensor.matmul(pc[:, 0:64], m_le, a_d, start=True, stop=True),
                    lambda: nc.tensor.matmul(pc[:, 64:128], m_gt, a_d, start=True, stop=True),
                    lambda: nc.tensor.matmul(pc[:, 128:192], self.onesf[:], a_d, start=True, stop=True)],
                    [mk_r, dta_r, self.onesf_r], [pc_r])
                P.op("act", lambda: nc.scalar.activation(out=eq[:, :], in_=pc[:, 0:192], func=AF.Exp), [pc_r], [eq_r])
                P.op("dve", lambda: nc.vector.tensor_tensor(out=dd[:, :], in0=dt_d, in1=eq[:, 64:128], op=ALU.mult), [dta_r, eq_r], [dd_r])
                if want_y:
                    P.op("dve", lambda: nc.vector.tensor_tensor(out=xdt[:], in0=xs[:], in1=dt_d.unsqueeze(2).to_broadcast([128, 64, 64]),
                                                                op=ALU.mult), [xs_r, dta_r], [xdt_r])
                P.op("pool", lambda: nc.gpsimd.tensor_tensor(out=xds[:], in0=xs[:], in1=dd[:, :].unsqueeze(2).to_broadcast([128, 64, 64]),
                                                             op=ALU.mult), [xs_r, dd_r], [xds_r])
                def stepA(g):
                    k2 = g % 2
                    pcb, pcb_r = self.ps[7], self.psr[7]
                    P.group("pe", [lambda: nc.tensor.matmul(pcb[:, 256:384], bct[:, g, :], bct[:, 8 + g, :], start=True, stop=True)],
                            [bct_r], [pcb_r])
                    cm, cm_r = cbm[k2]
                    P.op("dve", lambda: nc.vector.tensor_tensor(out=cm[:, :], in0=pcb[:, 256:384], in1=m_le, op=ALU.mult),
                         [pcb_r, mk_r], [cm_r])
                    R, R_r = Rt[k2]
                    P.op("pool", lambda: nc.gpsimd.tensor_tensor(
                        out=R[:], in0=a_d[:, g * 8:(g + 1) * 8].unsqueeze(2).to_broadcast([128, 8, 128]),
                        in1=m_le.unsqueeze(1).to_broadcast([128, 8, 128]), op=ALU.mult), [dta_r, mk_r], [R_r])
                    L, L_r = Lh[k2]
                    for hh in range(2):
                        pd, pd_r = self.ps[1 + hh], self.psr[1 + hh]
                        P.group("pe", [lambda hh=hh, pd=pd: nc.tensor.matmul(
                            pd[:, :], m_gt, R[:, hh * 4:(hh + 1) * 4, :].rearrange("p a l -> p (a l)"), start=True, stop=True)],
                            [mk_r, R_r], [pd_r])
                        P.op("act", lambda hh=hh, pd=pd: nc.scalar.activation(
                            out=L[:, hh * 4:(hh + 1) * 4, :].rearrange("p a l -> p (a l)"), in_=pd[:, :], func=AF.Exp), [pd_r], [L_r])
                    M, M_r = Mt[k2]
                    P.op("dve", lambda: nc.vector.tensor_tensor(
                        out=M[:], in0=L[:], in1=cm[:, :].unsqueeze(1).to_broadcast([128, 8, 128]), op=ALU.mult),
                        [L_r, cm_r], [M_r])

                def stepB(g):
                    k2 = g % 2
                    if want_y:
                        M, M_r = Mt[k2]
                        pyd, pyd_r = (self.ps[3], self.psr[3]) if k2 == 0 else (self.ps[0], self.psr[0])
                        P.group("pe", [lambda jh=jh: nc.tensor.matmul(
                            pyd[:, jh * 64:(jh + 1) * 64], M[:, jh, :], xdt[:, g * 8 + jh, :], start=True, stop=True) for jh in range(8)],
                            [M_r, xdt_r], [pyd_r])
                        pyo, pyo_r = self.ps[4], self.psr[4]
                        P.group("pe", [lambda: nc.tensor.matmul(pyo[:, :], bct[:, 8 + g, :], hb[g][0][:, :], start=True, stop=True)],
                                [bct_r, hb[g][1]], [pyo_r])
                        tp, tp_r = tmp[k2]
                        P.op("dve", lambda: nc.vector.tensor_tensor(
                            out=tp[:], in0=pyo[:, :].rearrange("p (a e) -> p a e", e=64),
                            in1=eq[:, g * 8:(g + 1) * 8].unsqueeze(2).to_broadcast([128, 8, 64]), op=ALU.mult), [pyo_r, eq_r], [tp_r])
                        P.op("dve", lambda: nc.vector.tensor_tensor(
                            out=yac[:, g * 512:(g + 1) * 512], in0=pyd[:, :], in1=tp[:].rearrange("p a e -> p (a e)"), op=ALU.add),
                            [pyd_r, tp_r], [yac_r])
                    pst, pst_r = self.ps[5 + g % 2], self.psr[5 + g % 2]
                    P.group("pe", [lambda: nc.tensor.matmul(
                        pst[:, :], btm[:, g * 128:(g + 1) * 128], xds[:, g * 8:(g + 1) * 8, :].rearrange("p a e -> p (a e)"),
                        start=True, stop=True)], [btm_r, xds_r], [pst_r])
                    hv = hf[:, g, :].rearrange("p (a e) -> p a e", e=64)
                    P.op("pool", lambda: nc.gpsimd.tensor_tensor(
                        out=hv, in0=hv, in1=eq[:, 128 + g * 8:128 + (g + 1) * 8].unsqueeze(2).to_broadcast([128, 8, 64]), op=ALU.mult),
                        [hf_r, eq_r], [hf_r])
                    P.op("dve", lambda: nc.vector.tensor_tensor(out=hf[:, g, :], in0=hf[:, g, :], in1=pst[:, :], op=ALU.add),
                         [hf_r, pst_r], [hf_r])
                    P.op("act", lambda: nc.scalar.copy(out=hb[g][0][:, :], in_=hf[:, g, :]), [hf_r], [hb[g][1]])

                if want_y:
                    stepA(0)
                for g in range(G):
                    if want_y and g + 1 < G:
                        stepA(g + 1)
                    stepB(g)
                if not want_y:
                    continue
                if d == 0:
                    P.dma("act", self.YF[t0:t0 + 128, :], yac[:, :], [yac_r], self.tr("YF"), yac_r)
                    continue
                P.op("dve", lambda: nc.vector.tensor_tensor(out=yac[:, :], in0=yac[:, :], in1=yf[:, :], op=ALU.add), [yac_r, yf_r], [yac_r])
                P.op("pool", lambda: nc.gpsimd.tensor_tensor(out=yf[:, :].rearrange("p (h e) -> p h e", e=64), in0=xs[:],
                                                             in1=dsk[:, 0:64].unsqueeze(2).to_broadcast([128, 64, 64]), op=ALU.mult),
                     [xs_r, dsk_r], [yf_r])
                P.op("dve", lambda: nc.vector.tensor_tensor(out=yac[:, :], in0=yac[:, :], in1=yf[:, :], op=ALU.add), [yac_r, yf_r], [yac_r])
                P.op("dve", lambda: nc.vector.tensor_tensor(out=yac[:, :], in0=yac[:, :], in1=sz[:, :], op=ALU.mult), [yac_r, sz_r], [yac_r])
                for g in range(G):
                    P.op("act", lambda g=g: nc.scalar.activation(out=yf[:, g * 512:(g + 1) * 512], in_=yac[:, g * 512:(g + 1) * 512],
                                                                 func=AF.Square, accum_out=ss[:, g:g + 1]), [yac_r], [yf_r, ss_r])
                P.op("act", lambda: nc.scalar.activation(out=ss[:, :], in_=ss[:, :], func=AF.Sqrt, bias=self.eps_t[:, 0:1], scale=1.0 / 512),
                     [ss_r, self.eps_r], [ss_r])
                P.op("dve", lambda: nc.vector.reciprocal(out=ss[:, :], in_=ss[:, :]), [ss_r], [ss_r])
                P.op("dve", lambda: nc.vector.tensor_tensor(out=yac[:, :].rearrange("p (g e) -> p g e", e=512),
                                                            in0=yac[:, :].rearrange("p (g e) -> p g e", e=512),
                                                            in1=ss[:, :].unsqueeze(2).to_broadcast([128, 8, 512]), op=ALU.mult),
                     [yac_r, ss_r], [yac_r])
                for q in range(8):
                    pst, ps_r = self.ps[q % 2], self.psr[q % 2]
                    P.group("pe", [lambda c=c, pst=pst: nc.tensor.transpose(
                        out=pst[:, (c % 4) * 128:(c % 4 + 1) * 128], in_=yac[:, c * 128:(c + 1) * 128], identity=self.identf[:])
                        for c in range(q * 4, q * 4 + 4)], [yac_r, self.identf_r], [ps_r])
                    for c in range(q * 4, q * 4 + 4):
                        P.op("act", lambda c=c, pst=pst: nc.scalar.activation(
                            out=ytr[:, c, :], in_=pst[:, (c % 4) * 128:(c % 4 + 1) * 128], func=AF.Copy, scale=ngt[:, c:c + 1]),
                            [ps_r, ng_r], [ytr_r])
                P.dma("act", YTv[:, :, t0:t0 + 128], ytr[:], [ytr_r], self.tr("YT"), ytr_r)
            P.barrier()
            rel = [mk_r, yac_r, dta_r, btm_r, bct_r] + [r for _, r in xsb]
            if d == 1:
                rel += [dsk_r, ng_r, ytr_r] + [r for _, r in yfb + szb]
            P.release(rel)

    @_scoped
    def stage_final(self, fin_g):
        cfg, P, nc = self.cfg, self.P, self.nc
        XTv = self.XT.rearrange("(c p) t -> p c t", p=128)
        with ExitStack() as st:
            g, g_r = P.sb(st, "fing", [128, D], F32)
            P.dma("sp", g[:], fin_g[:, :], self.tr("fin_g"), [g_r], g_r)
            xin = [P.sb(st, "fx%d" % i, [128, DC, 128], F32) for i in range(2)]
            xtm = [P.sb(st, "ft%d" % i, [128, D], F32) for i in range(2)]
            junk, junk_r = P.sb(st, "fjunk", [128, D], F32)
            ssq = [P.sb(st, "fss%d" % i, [128, 1], F32) for i in range(2)]
            for i in range(cfg.S // 128):
                t0 = i * 128
                xi, xi_r = xin[i % 2]
                xt, xt_r = xtm[i % 2]
                ss, ss_r = ssq[i % 2]
                P.dma("sp", xi[:], XTv[:, :, t0:t0 + 128], self.tr("XT", t0, 128), [xi_r], xi_r)
                for q in range(DC // 4):
                    pb = (i * (DC // 4) + q) % 8
                    pst, psr = self.ps[pb], self.psr[pb]
                    P.group("pe", [
                        (lambda c=c, pst=pst: nc.tensor.transpose(
                            out=pst[:, (c % 4) * 128:(c % 4 + 1) * 128],
                            in_=xi[:, c, :], identity=self.identf[:]))
                        for c in range(q * 4, q * 4 + 4)], [xi_r, self.identf_r], [psr])
                    P.op("dve", lambda q=q, pst=pst: nc.vector.tensor_copy(out=xt[:, q * 512:(q + 1) * 512], in_=pst[:, :]),
                         [psr], [xt_r])
                P.op("act", lambda: nc.scalar.activation(out=junk[:], in_=xt[:], func=AF.Square, accum_out=ss[:, 0:1]),
                     [xt_r], [junk_r, ss_r])
                P.op("act", lambda: nc.scalar.activation(out=ss[:], in_=ss[:], func=AF.Sqrt, bias=self.eps_t[:, 0:1], scale=1.0 / D),
                     [ss_r, self.eps_r], [ss_r])
                P.op("dve", lambda: nc.vector.reciprocal(out=ss[:], in_=ss[:]), [ss_r], [ss_r])
                P.op("dve", lambda: nc.vector.scalar_tensor_tensor(
                    out=xt[:], in0=xt[:], scalar=ss[:, 0:1], in1=g[:], op0=ALU.mult, op1=ALU.mult),
                    [xt_r, ss_r, g_r], [xt_r])
                P.dma("act", self.out[t0:t0 + 128, :], xt[:], [xt_r], self.tr("out"), xt_r)
            P.barrier()
            P.release([g_r] + [r for _, r in xin + xtm + ssq])


def pmajor(v):
    sh = v.shape[:-1]
    n = v.shape[-1] // 128
    return np.ascontiguousarray(np.swapaxes(v.reshape(sh + (n, 128)), -1, -2))


def pretile(w):
    lead = w.shape[:-2]
    K, N = w.shape[-2:]
    v = w.reshape(lead + (K // 128, 128, N // 128, 128))
    nl = len(lead)
    v = v.transpose(tuple(range(nl)) + (nl + 2, nl + 1, nl + 0, nl + 3))
    return np.ascontiguousarray(v).reshape(lead + (N, K))


def make_in_maps(cfg, inputs, n_cores):
    f = lambda a: np.ascontiguousarray(np.asarray(a, dtype=np.float32))
    depth = cfg.depth
    shared = {
        "mod_w": f(inputs["mod_w"][:depth]),
        "mod_bT": pmajor(f(inputs["mod_b"][:depth])),
        "norm_gT": pmajor(f(inputs["norm_g"][:depth]).reshape(depth, 3 * D)),
        "ffn_w1": pretile(f(inputs["ffn_w1"][:depth])),
        "ffn_w3": pretile(f(inputs["ffn_w3"][:depth])),
        "ffn_w2": pretile(f(inputs["ffn_w2"][:depth])),
        "fin_g": np.ascontiguousarray(np.broadcast_to(f(inputs["final_norm_g"])[None, :], (128, D))),
        "ident": np.eye(128, dtype=np.float32),
    }
    n_even = (depth + 1) // 2
    if n_even:
        w_in = f(inputs["attn_w_in"][:n_even])
        d = np.arange(128)
        perm = np.where(d % 64 < 32, d + 32, d - 32)
        sign = np.where(d % 64 < 32, -1.0, 1.0).astype(np.float32)
        cols = (np.arange(8)[:, None] * 128 + perm[None, :]).reshape(-1)
        shared["attn_w_in"] = w_in
        shared["attn_w_rot"] = np.ascontiguousarray(
            np.concatenate([w_in[:, :, 3072:4096][:, :, cols], w_in[:, :, 4096:5120][:, :, cols]], axis=-1))
        shared["attn_w_out"] = f(inputs["attn_w_out"][:n_even])
        rpb = f(inputs["na_rpb"][:n_even])
        kc = np.arange(64)[:, None]
        qc = np.arange(64)[None, :]
        coff = np.clip(kc - qc, -15, 15) + 15
        shared["na_bias"] = np.ascontiguousarray(rpb[:, :, :, coff])
        cs = np.clip(qc - 8, 0, 64 - 16)
        cmask = ((kc >= cs) & (kc < cs + 16)).astype(np.float32)
        shared["na_cmask"] = np.ascontiguousarray(
            np.broadcast_to(cmask[None, :, None, :], (2, 64, 4, 64)).reshape(128, 4, 64))
        shared["lamT"] = np.ascontiguousarray(np.swapaxes(f(inputs["diff_lambda"][:n_even]), 1, 2))
        shared["subg"] = pmajor(f(inputs["diff_subln_g"][:n_even]))
        quarter = 32
        inv = (1.0 / (10000.0 ** (np.arange(quarter, dtype=np.float32) / quarter))).astype(np.float32)
        t = np.arange(cfg.S)
        row = (t // GRID_W).astype(np.float32)[:, None] * inv
        col = (t % GRID_W).astype(np.float32)[:, None] * inv
        ang = np.concatenate([row, row, col, col], axis=-1).astype(np.float32)
        cosT = np.ones((128, cfg.T), np.float32)
        sinT = np.zeros((128, cfg.T), np.float32)
        cosT[:, :cfg.S] = np.cos(ang).T
        sinT[:, :cfg.S] = np.sin(ang).T * sign[:, None]
        shared["ropec"] = cosT
        shared["ropes"] = sinT
    n_odd = depth // 2
    if n_odd:
        rep = lambda a: np.ascontiguousarray(np.broadcast_to(a.reshape(n_odd, 1, 128), (n_odd, 128, 128)))
        shared["ssm_w_in"] = f(inputs["ssm_w_in"][:n_odd])
        shared["ssm_w_out"] = f(inputs["ssm_w_out"][:n_odd])
        cwt = f(inputs["ssm_conv_w"][:n_odd])
        shared["conv_wT"] = np.ascontiguousarray(cwt.reshape(n_odd, 4, 48, 128).transpose(0, 3, 2, 1))
        shared["conv_bT"] = pmajor(f(inputs["ssm_conv_b"][:n_odd]))
        shared["ssm_a_log"] = rep(f(inputs["ssm_a_log"][:n_odd]))
        shared["ssm_dt_bias"] = rep(f(inputs["ssm_dt_bias"][:n_odd]))
        shared["ssm_d"] = rep(f(inputs["ssm_d"][:n_odd]))
        shared["ssm_norm_gT"] = pmajor(f(inputs["ssm_norm_g"][:n_odd]))
        u = np.arange(128)[:, None]
        l = np.arange(128)[None, :]
        shared["ssm_masks"] = np.ascontiguousarray(
            np.stack([u <= l, u > l, u >= l, u < l], axis=1).astype(np.float32))
    cc = pmajor(f(inputs["c_ctx"]))
    maps = []
    for b in range(n_cores):
        m = dict(shared)
        m["x"] = f(inputs["x"][b])
        m["ctx"] = f(inputs["ctx"][b])
        cb = pmajor(f(inputs["c"][b]))
        m["cT"] = np.ascontiguousarray(np.stack([cb, cc], axis=-1))
        maps.append(m)
    return maps


_CACHE = {}


def run(cfg, inputs, n_cores, mixers=True, trace=False):
    import time as _t
    t0 = _t.time()
    b = Builder(cfg, mixers)
    nc = b.build()
    print("[kernel] build %.1fs n_ins=%d n_wait=%d" % (_t.time() - t0, b.P.n_ins, b.P.n_wait), flush=True)
    t0 = _t.time()
    maps = make_in_maps(cfg, inputs, n_cores)
    print("[kernel] layout %.1fs" % (_t.time() - t0), flush=True)
    t0 = _t.time()
    if trace:
        res = run_bass_kernel_spmd(nc, maps, core_ids=list(range(n_cores)), trace=True)
        print("[kernel] exec_time_ns", res.exec_time_ns, flush=True)
        if res.per_core_scope_times:
            for k in sorted(res.per_core_scope_times, key=lambda k: k.split("_")[-1]):
                print("[scope] %-28s %s" % (k, res.per_core_scope_times[k]), flush=True)
    else:
        res = run_bass_kernel_spmd(nc, maps, core_ids=list(range(n_cores)))
    print("[kernel] run %.1fs" % (_t.time() - t0), flush=True)
    return np.stack([np.asarray(r["out"]) for r in res.results], axis=0)


def kernel(**inputs):
    cfg = Cfg(seq=4096, depth=4)
    return run(cfg, inputs, 8).astype(np.float32)
```

```python
import math
from contextlib import ExitStack
import numpy as np
import concourse.bass as bass
import concourse.mybir as mybir
from concourse.bass_utils import run_bass_kernel_spmd

F32 = mybir.dt.float32
BF16 = mybir.dt.bfloat16
AF = mybir.ActivationFunctionType
ALU = mybir.AluOpType
AX = mybir.AxisListType

D = 2048
DC = D // 128
CTX = 256
GRID_W = 64
N_MOD = 9
FFN = 5632
FC = FFN // 128
HD = 128
NA_H = 8
DF_H = 4
ATT_W = 6144
SSM_INNER = 4096
SSM_H = 64
SSM_P = 64
SSM_N = 128
SSM_G = 8
SSM_CONV_CH = SSM_INNER + 2 * SSM_G * SSM_N
SSM_IN_W = SSM_INNER + SSM_CONV_CH + 2 * SSM_H
EPS = 1e-6


class Res:
    __slots__ = ("name", "w", "r", "dsem", "dkey")

    def __init__(self, name):
        self.name = name
        self.w = {}
        self.r = {}
        self.dsem = None
        self.dkey = None


class Prog:
    ENG = ("pe", "act", "dve", "pool", "sp")

    def __init__(self, n_dma_sems=80):
        self.nc = bass.Bass("TRN2", target_bir_lowering=False)
        nc = self.nc
        self.es = ExitStack()
        self.eng = {"pe": nc.tensor, "act": nc.scalar, "dve": nc.vector, "pool": nc.gpsimd, "sp": nc.sync}
        self.sems = {}
        self.cnt = {}
        for e in self.ENG:
            self.sems["E" + e] = self.es.enter_context(nc.semaphore("E" + e))
            self.cnt["E" + e] = 0
        self.dfree = []
        for i in range(n_dma_sems):
            k = "D%d" % i
            self.sems[k] = self.es.enter_context(nc.semaphore(k))
            self.cnt[k] = 0
            self.dfree.append(k)
        self.seen = {e: {} for e in self.ENG}
        self.n_ins = 0
        self.n_wait = 0
        self.bg = set()

    def sb(self, stack, name, shape, dt):
        self.n_sb = getattr(self, "n_sb", 0) + 1
        name = "%s_%d" % (name, self.n_sb)
        t = stack.enter_context(self.nc.sbuf_tensor(name, list(shape), dt))
        r = Res(name)
        return t, r

    def dsem_for(self, res):
        if res.dkey is None:
            res.dkey = self.dfree.pop()
        return res.dkey

    def release(self, res_list):
        for r in res_list:
            if r.dkey is not None:
                self.dfree.append(r.dkey)
                r.dkey = None

    @staticmethod
    def _merge(dst, src):
        for k, v in src.items():
            if dst.get(k, 0) < v:
                dst[k] = v

    def _need(self, reads, writes):
        need = {}
        for r in reads:
            self._merge(need, r.w)
        for w in writes:
            self._merge(need, w.w)
            self._merge(need, w.r)
        return need

    def _waits(self, e, need):
        seen = self.seen[e]
        own = "E" + e
        lst = []
        for k, v in need.items():
            if e == "pe" and k == own:
                continue
            if seen.get(k, 0) < v:
                lst.append((k, v))
                seen[k] = v
        return lst

    def _commit(self, key, reads, writes):
        ev = {key: self.cnt[key]}
        for r in reads:
            self._merge(r.r, ev)
        for w in writes:
            w.w = dict(ev)
            w.r = {}

    def op(self, e, fn, reads=(), writes=()):
        need = self._need(reads, writes)
        lst = self._waits(e, need)
        eng = self.eng[e]
        for (k, v) in lst[1:]:
            eng.wait_ge(self.sems[k], v)
            self.n_wait += 1
        ins = fn()
        if lst:
            ins._wait_ge(self.sems[lst[0][0]], lst[0][1])
        key = "E" + e
        self.cnt[key] += 1
        ins.then_inc(self.sems[key], 1)
        self.n_ins += 1
        self._commit(key, reads, writes)
        return ins

    def group(self, e, fns, reads=(), writes=()):
        need = self._need(reads, writes)
        lst = self._waits(e, need)
        eng = self.eng[e]
        for (k, v) in lst[1:]:
            eng.wait_ge(self.sems[k], v)
            self.n_wait += 1
        ins = None
        for i, fn in enumerate(fns):
            ins = fn()
            if i == 0 and lst:
                ins._wait_ge(self.sems[lst[0][0]], lst[0][1])
            self.n_ins += 1
        key = "E" + e
        self.cnt[key] += 1
        ins.then_inc(self.sems[key], 1)
        self._commit(key, reads, writes)

    def dma(self, q, out, in_, reads, writes, owner, **kw):
        need = self._need(reads, writes)
        lst = self._waits(q, need)
        eng = self.eng[q]
        for (k, v) in lst:
            eng.wait_ge(self.sems[k], v)
            self.n_wait += 1
        key = self.dsem_for(owner)
        ins = eng.dma_start(out=out, in_=in_, **kw)
        self.cnt[key] += 16
        ins.then_inc(self.sems[key], 16)
        self.n_ins += 1
        self._commit(key, reads, writes)

    def barrier(self):
        for e in self.ENG:
            seen = self.seen[e]
            for k, v in self.cnt.items():
                if v == 0 or k == "E" + e or k in self.bg:
                    continue
                if seen.get(k, 0) < v:
                    self.eng[e].wait_ge(self.sems[k], v)
                    seen[k] = v
                    self.n_wait += 1

    def finish(self):
        for k, v in self.cnt.items():
            if v and self.seen["sp"].get(k, 0) < v and k != "Esp":
                self.eng["sp"].wait_ge(self.sems[k], v)
                self.seen["sp"][k] = v


class Cfg:
    def __init__(self, seq=4096, depth=4):
        self.S = seq
        self.L = CTX
        self.T = seq + CTX
        self.NT = self.T // 128
        self.depth = depth
        self.rows = seq // GRID_W


def blocks_of(cfg, tb):
    out = []
    t = 0
    while t < cfg.S:
        n = min(tb, cfg.S - t)
        out.append((t, n, 0))
        t += n
    t = cfg.S
    while t < cfg.T:
        n = min(tb, cfg.T - t)
        out.append((t, n, 1))
        t += n
    return out


def _scoped(fn):
    def w(self, *a, **k):
        self._scn = getattr(self, "_scn", 0) + 1
        with self.nc.named_scope("%s_%03d" % (fn.__name__, self._scn)):
            return fn(self, *a, **k)
    return w


class Builder:
    def __init__(self, cfg, mixers=True):
        self.cfg = cfg
        self.mixers = mixers
        self.P = Prog()
        self.nc = self.P.nc
        self.dram_in = {}
        self.dres = {}

    def din(self, name, shape, dt=F32):
        t = self.nc.dram_tensor(name, list(shape), dt, kind="ExternalInput").ap()
        self.dram_in[name] = t
        self.dres[name] = [Res(name)]
        return t

    def dscr(self, name, shape, dt, ntiles=1, kind="Internal"):
        t = self.nc.dram_tensor(name, list(shape), dt, kind=kind).ap()
        self.dres[name] = [Res("%s.%d" % (name, i)) for i in range(ntiles)]
        return t

    def tr(self, name, t0=None, n=None):
        rs = self.dres[name]
        if t0 is None or len(rs) == 1:
            return rs
        return rs[t0 // 128:(t0 + n + 127) // 128]

    def build(self):
        cfg, P, nc = self.cfg, self.P, self.nc
        S, L, T, NT = cfg.S, cfg.L, cfg.T, cfg.NT
        depth = cfg.depth
        n_even = (depth + 1) // 2
        n_odd = depth // 2
        x = self.din("x", [S, D])
        ctx = self.din("ctx", [L, D])
        cT = self.din("cT", [128, DC, 2])
        mod_w = self.din("mod_w", [depth, D, N_MOD * D])
        mod_bT = self.din("mod_bT", [depth, 128, N_MOD * DC])
        norm_gT = self.din("norm_gT", [depth, 128, 3 * DC])
        ffn_w1 = self.din("ffn_w1", [depth, 2, FC * 128, D])
        ffn_w3 = self.din("ffn_w3", [depth, 2, FC * 128, D])
        ffn_w2 = self.din("ffn_w2", [depth, 2, DC * 128, FFN])
        fin_g = self.din("fin_g", [128, D])
        if n_even:
            attn_w_in = self.din("attn_w_in", [n_even, D, ATT_W])
            attn_w_rot = self.din("attn_w_rot", [n_even, D, 2048])
            attn_w_out = self.din("attn_w_out", [n_even, D, D])
            na_bias = self.din("na_bias", [n_even, NA_H, 15, 64, 64])
            na_cmask = self.din("na_cmask", [128, 4, 64])
            lamT = self.din("lamT", [n_even, 128, 4])
            subg = self.din("subg", [n_even, 128, 2])
            ropec = self.din("ropec", [128, T])
            ropes = self.din("ropes", [128, T])
            self.ain = dict(w_in=attn_w_in, w_rot=attn_w_rot, w_out=attn_w_out, na_bias=na_bias,
                            na_cmask=na_cmask, lamT=lamT, subg=subg, ropec=ropec, ropes=ropes)
            self.awb = self.dscr("awb", [D, ATT_W], BF16)
            self.arb = self.dscr("arb", [D, 2048], BF16)
            self.aob = self.dscr("aob", [D, D], BF16)
            self.QKT = self.dscr("QKT", [32, 128, T], BF16)
            self.VTM = self.dscr("VTM", [T, 2048], BF16)
            self.CAT = self.dscr("CAT", [D, T], BF16)
        ident = self.din("ident", [128, 128])
        if n_odd:
            self.sin = dict(
                w_in=self.din("ssm_w_in", [n_odd, D, SSM_IN_W]),
                w_out=self.din("ssm_w_out", [n_odd, SSM_INNER, D]),
                conv_wT=self.din("conv_wT", [n_odd, 128, 48, 4]),
                conv_bT=self.din("conv_bT", [n_odd, 128, 48]),
                a_log=self.din("ssm_a_log", [n_odd, 128, 128]),
                dt_bias=self.din("ssm_dt_bias", [n_odd, 128, 128]),
                d_skip=self.din("ssm_d", [n_odd, 128, 128]),
                norm_gT=self.din("ssm_norm_gT", [n_odd, 128, 32]),
                masks=self.din("ssm_masks", [128, 4, 128]))
            self.swb = self.dscr("swb", [D, SSM_IN_W], BF16)
            self.sob = self.dscr("sob", [SSM_INNER, D], BF16)
            self.SZ = self.dscr("SZ", [T, SSM_INNER], F32)
            self.XBC = self.dscr("XBC", [48, 128, T], F32)
            self.DTR = self.dscr("DTR", [T, 128], F32)
            self.DTA = self.dscr("DTA", [T, 2, 128], F32)
            self.XS = self.dscr("XS", [T, SSM_INNER], F32)
            self.BCT = self.dscr("BCT", [16, 128, T], BF16)
            self.BTM = self.dscr("BTM", [T, SSM_G * SSM_N], BF16)
            self.YF = self.dscr("YF", [T, SSM_INNER], F32)
            self.YT = self.dscr("YT", [SSM_INNER, T], BF16)
        out = self.dscr("out", [S, D], F32, ntiles=1, kind="ExternalOutput")
        self.out = out
        XT = self.dscr("XT", [D, T], F32, ntiles=NT)
        self.XT = XT
        w1b = self.dscr("w1b", [2, FC, 128, D], BF16, ntiles=2)
        w3b = self.dscr("w3b", [2, FC, 128, D], BF16, ntiles=2)
        w2b = self.dscr("w2b", [2, DC, 128, FFN], BF16, ntiles=2)

        es = P.es
        self.ps = []
        self.psr = []
        for i in range(8):
            t = es.enter_context(nc.psum_tensor("ps%d" % i, [128, 512], F32))
            self.ps.append(t)
            self.psr.append(Res("ps%d" % i))
        self.identf, self.identf_r = P.sb(es, "identf", [128, 128], F32)
        self.identb, self.identb_r = P.sb(es, "identb", [128, 128], BF16)
        self.onesf, self.onesf_r = P.sb(es, "onesf", [128, 128], F32)
        self.onesb, self.onesb_r = P.sb(es, "onesb", [128, 128], BF16)
        self.modt, self.modt_r = P.sb(es, "modt", [128, N_MOD * DC, 2], F32)
        self.gs, self.gs_r = P.sb(es, "gs", [128, 3, DC, 2], F32)
        self.gate, self.gate_r = P.sb(es, "gate", [128, 3, DC, 2], F32)
        self.sc, self.sc_r = P.sb(es, "sc", [128, DC, 2], F32)

        P.dma("sp", self.identf[:], ident[:, :], self.tr("ident"), [self.identf_r], self.identf_r)
        P.op("dve", lambda: nc.vector.tensor_copy(out=self.identb[:], in_=self.identf[:]),
             [self.identf_r], [self.identb_r])
        P.op("dve", lambda: nc.vector.memset(self.onesf[:], 1.0), [], [self.onesf_r])
        P.op("dve", lambda: nc.vector.memset(self.onesb[:], 1.0), [], [self.onesb_r])
        self.eps_t, self.eps_r = P.sb(es, "epsc", [128, 1], F32)
        P.op("dve", lambda: nc.vector.memset(self.eps_t[:], EPS), [], [self.eps_r])
        with ExitStack() as st:
            craw, craw_r = P.sb(st, "craw", [128, DC, 2], F32)
            P.dma("sp", craw[:], cT[:, :, :], self.tr("cT"), [craw_r], craw_r)
            P.op("act", lambda: nc.scalar.activation(out=self.sc[:], in_=craw[:], func=AF.Silu),
                 [craw_r], [self.sc_r])
            P.barrier()
            P.release([craw_r])

        self.stage_in_transpose(x, ctx)
        for layer in range(depth):
            self.stage_wcast_ffn(layer, ffn_w1, ffn_w3, ffn_w2, w1b, w3b, w2b)
            self.stage_mod(layer, mod_w, mod_bT, norm_gT)
            self.stage_ffn(layer, 0, w1b, w3b, w2b)
            if self.mixers:
                ctx_out = layer < depth - 1
                if layer % 2 == 0:
                    self.stage_attn(layer, ctx_out)
                else:
                    self.stage_ssd(layer, ctx_out)
            self.stage_ffn(layer, 1, w1b, w3b, w2b)
        self.stage_final(fin_g)
        P.barrier()
        P.finish()
        return nc

    @_scoped
    def stage_in_transpose(self, x, ctx):
        cfg, P, nc = self.cfg, self.P, self.nc
        XTv = self.XT.rearrange("(c p) t -> p c t", p=128)
        with ExitStack() as st:
            NB = 2
            xin = [P.sb(st, "xin%d" % i, [128, D], F32) for i in range(NB)]
            xo = [P.sb(st, "xo%d" % i, [128, DC, 128], F32) for i in range(NB)]
            for i in range(cfg.NT):
                t0 = i * 128
                xi, xi_r = xin[i % NB]
                xoT, xo_r = xo[i % NB]
                src = x[t0:t0 + 128, :] if t0 < cfg.S else ctx[t0 - cfg.S:t0 - cfg.S + 128, :]
                srcres = self.tr("x") if t0 < cfg.S else self.tr("ctx")
                P.dma("sp", xi[:], src, srcres, [xi_r], xi_r)
                for q in range(DC // 4):
                    pb = (i * (DC // 4) + q) % 8
                    pst, psr = self.ps[pb], self.psr[pb]
                    P.group("pe", [
                        (lambda c=c, pst=pst: nc.tensor.transpose(
                            out=pst[:, (c % 4) * 128:(c % 4 + 1) * 128],
                            in_=xi[:, c * 128:(c + 1) * 128], identity=self.identf[:]))
                        for c in range(q * 4, q * 4 + 4)],
                        [xi_r, self.identf_r], [psr])
                    eng = "dve" if q % 2 == 0 else "act"
                    dst = xoT[:, q * 4:(q + 1) * 4, :]
                    srcp = pst[:, :].rearrange("p (c t) -> p c t", c=4)
                    if eng == "dve":
                        P.op("dve", lambda dst=dst, srcp=srcp: nc.vector.tensor_copy(out=dst, in_=srcp),
                             [psr], [xo_r])
                    else:
                        P.op("act", lambda dst=dst, srcp=srcp: nc.scalar.copy(out=dst, in_=srcp),
                             [psr], [xo_r])
                P.dma("act", XTv[:, :, t0:t0 + 128], xoT[:], [xo_r], self.tr("XT", t0, 128), xo_r)
            P.barrier()
            P.release([r for _, r in xin] + [r for _, r in xo])

    def stage_wcast_ffn(self, layer, w1, w3, w2, w1b, w3b, w2b):
        for f in range(2):
            for (src, dst, K, name) in ((w1, w1b, FC * 128, "w1b"), (w3, w3b, FC * 128, "w3b"), (w2, w2b, DC * 128, "w2b")):
                self.wcast(src[layer, f], dst[f].rearrange("c p n -> (c p) n"), K, name, res=[self.dres[name][f]])

    @_scoped
    def stage_mod(self, layer, mod_w, mod_bT, norm_gT):
        P, nc = self.P, self.nc
        NB = 512
        nblk = N_MOD * D // NB
        mwv = mod_w[layer].rearrange("(kt p) n -> p kt n", p=128)
        with ExitStack() as st:
            wt = [P.sb(st, "modw%d" % i, [128, DC, NB], F32) for i in range(3)]
            mb, mb_r = P.sb(st, "modb", [128, N_MOD * DC], F32)
            ng, ng_r = P.sb(st, "normg", [128, 3 * DC], F32)
            P.dma("sp", mb[:], mod_bT[layer], self.tr("mod_bT"), [mb_r], mb_r)
            P.dma("sp", ng[:], norm_gT[layer], self.tr("norm_gT"), [ng_r], ng_r)
            for b in range(nblk):
                w, w_r = wt[b % 3]
                for h in range(4):
                    P.dma("sp" if h % 2 == 0 else "act", w[:, h * 4:(h + 1) * 4, :],
                          mwv[:, h * 4:(h + 1) * 4, b * NB:(b + 1) * NB],
                          self.tr("mod_w"), [w_r], w_r)
                pb = b % 2
                pst, psr = self.ps[pb], self.psr[pb]
                fns = []
                for j in range(NB // 128):
                    for kt in range(DC):
                        fns.append(lambda j=j, kt=kt, w=w, pst=pst: nc.tensor.matmul(
                            pst[:, 2 * j:2 * j + 2], w[:, kt, j * 128:(j + 1) * 128], self.sc[:, kt, :],
                            start=(kt == 0), stop=(kt == DC - 1)))
                P.group("pe", fns, [w_r, self.sc_r], [psr])
                nj = NB // 128
                P.op("dve", lambda b=b, pst=pst: nc.vector.tensor_tensor(
                    out=self.modt[:, b * nj:(b + 1) * nj, :],
                    in0=pst[:, 0:2 * nj].rearrange("p (j s) -> p j s", s=2),
                    in1=mb[:, b * nj:(b + 1) * nj].unsqueeze(2).to_broadcast([128, nj, 2]),
                    op=ALU.add), [psr, mb_r], [self.modt_r])
            for i in range(3):
                sh = self.modt[:, (3 * i + 0) * DC:(3 * i + 1) * DC, :]
                scl = self.modt[:, (3 * i + 1) * DC:(3 * i + 2) * DC, :]
                gt = self.modt[:, (3 * i + 2) * DC:(3 * i + 3) * DC, :]
                P.op("dve", lambda i=i, scl=scl: nc.vector.scalar_tensor_tensor(
                    out=self.gs[:, i, :, :], in0=scl, scalar=1.0,
                    in1=ng[:, i * DC:(i + 1) * DC].unsqueeze(2).to_broadcast([128, DC, 2]),
                    op0=ALU.add, op1=ALU.mult), [self.modt_r, ng_r], [self.gs_r])
                fac = 1.0 if i == 1 else 0.5
                P.op("dve", lambda i=i, gt=gt, fac=fac: nc.vector.tensor_scalar(
                    out=self.gate[:, i, :, :], in0=gt, scalar1=fac, scalar2=None, op0=ALU.mult),
                    [self.modt_r], [self.gate_r])
            P.barrier()
            P.release([r for _, r in wt] + [mb_r, ng_r])

    def shift_ap(self, i, c, s):
        return self.modt[:, 3 * i * DC + c, s:s + 1]

    def norm_mod(self, st_tiles, i, t0, n, s, hT, hT_r, off=0):
        P, nc = self.P, self.nc
        XTv = self.XT.rearrange("(c p) t -> p c t", p=128)
        xs, sq, rstd, tmp = st_tiles
        (rstd_t, rstd_r) = rstd
        pst, psr = self.ps[7], self.psr[7]
        xres = self.tr("XT", t0, n)
        fns = []
        for c in range(DC):
            xt, xt_r = xs[c % len(xs)]
            sqt, sq_r = sq[c % len(sq)]
            P.dma("sp", xt[:, :n], XTv[:, c, t0:t0 + n], xres, [xt_r], xt_r)
            P.op("act", lambda xt=xt, sqt=sqt: nc.scalar.activation(out=sqt[:, :n], in_=xt[:, :n], func=AF.Square),
                 [xt_r], [sq_r])
            P.group("pe", [lambda sqt=sqt, c=c: nc.tensor.matmul(
                pst[:, :n], self.onesf[:], sqt[:, :n], start=(c == 0), stop=(c == DC - 1))],
                [sq_r, self.onesf_r], [psr])
        P.op("act", lambda: nc.scalar.activation(out=rstd_t[:, :n], in_=pst[:, :n], func=AF.Sqrt,
                                                 bias=self.eps_t[:, 0:1], scale=1.0 / D), [psr, self.eps_r], [rstd_r])
        P.op("dve", lambda: nc.vector.reciprocal(out=rstd_t[:, :n], in_=rstd_t[:, :n]), [rstd_r], [rstd_r])
        for c in range(DC):
            xt, xt_r = xs[c % len(xs)]
            tt, tt_r = tmp[c % len(tmp)]
            P.dma("sp", xt[:, :n], XTv[:, c, t0:t0 + n], xres, [xt_r], xt_r)
            P.op("dve", lambda xt=xt, tt=tt, c=c: nc.vector.scalar_tensor_tensor(
                out=tt[:, :n], in0=xt[:, :n], scalar=self.gs[:, i, c, s:s + 1], in1=rstd_t[:, :n],
                op0=ALU.mult, op1=ALU.mult), [xt_r, rstd_r, self.gs_r], [tt_r])
            P.op("act", lambda tt=tt, c=c: nc.scalar.activation(
                out=hT[:, c, off:off + n], in_=tt[:, :n], func=AF.Identity,
                bias=self.shift_ap(i, c, s), scale=1.0), [tt_r, self.modt_r], [hT_r])

    def norm_tiles(self, st, tb):
        P = self.P
        xs = [P.sb(st, "nx%d" % i, [128, tb], F32) for i in range(4)]
        sq = [P.sb(st, "nsq%d" % i, [128, tb], F32) for i in range(2)]
        rstd = P.sb(st, "nrstd", [128, tb], F32)
        tmp = [P.sb(st, "ntmp%d" % i, [128, tb], F32) for i in range(2)]
        return (xs, sq, rstd, tmp), [r for _, r in xs] + [r for _, r in sq] + [rstd[1]] + [r for _, r in tmp]

    @_scoped
    def stage_ffn(self, layer, f, w1b, w3b, w2b):
        cfg, P, nc = self.cfg, self.P, self.nc
        TB = 1024
        slot = 0 if f == 0 else 2
        XTv = self.XT.rearrange("(c p) t -> p c t", p=128)
        with ExitStack() as st:
            ntl, ntl_res = self.norm_tiles(st, 512)
            hT, hT_r = P.sb(st, "hT", [128, DC, TB], BF16)
            gT, gT_r = P.sb(st, "gT", [128, FC, TB], BF16)
            wa = [P.sb(st, "wa%d" % i, [128, DC * 128], BF16) for i in range(3)]
            wb = [P.sb(st, "wb%d" % i, [128, DC * 128], BF16) for i in range(3)]
            wd = [P.sb(st, "wd%d" % i, [128, FC * 128], BF16) for i in range(2)]
            sil = [P.sb(st, "sil%d" % i, [128, 512], F32) for i in range(2)]
            xr = [P.sb(st, "xr%d" % i, [128, 512], F32) for i in range(2)]
            yo = [P.sb(st, "yo%d" % i, [128, 512], F32) for i in range(2)]
            w1r, w3r, w2r = [self.dres["w1b"][f]], [self.dres["w3b"][f]], [self.dres["w2b"][f]]
            it = 0
            it2 = 0
            for (t0, n, s) in blocks_of(cfg, TB):
                halves = [(h0, min(512, n - h0)) for h0 in range(0, n, 512)]
                for (h0, hn) in halves:
                    self.norm_mod(ntl, slot, t0 + h0, hn, s, hT, hT_r, off=h0)

                def load_up(fc):
                    a, a_r = wa[fc % 3]
                    b, b_r = wb[fc % 3]
                    P.dma("sp", a[:], w1b[f, fc], w1r, [a_r], a_r)
                    P.dma("sp", b[:], w3b[f, fc], w3r, [b_r], b_r)
                load_up(0)
                load_up(1)
                for fc in range(FC):
                    if fc + 2 < FC:
                        load_up(fc + 2)
                    a, a_r = wa[fc % 3]
                    b, b_r = wb[fc % 3]
                    for (h0, hn) in halves:
                        k = it % 3
                        pa, pa_r = self.ps[2 * k], self.psr[2 * k]
                        pb, pb_r = self.ps[2 * k + 1], self.psr[2 * k + 1]
                        sl, sl_r = sil[it % 2]
                        it += 1
                        P.group("pe", [lambda kt=kt, a=a, pa=pa, h0=h0, hn=hn: nc.tensor.matmul(
                            pa[:, :hn], a[:, kt * 128:(kt + 1) * 128], hT[:, kt, h0:h0 + hn],
                            start=(kt == 0), stop=(kt == DC - 1)) for kt in range(DC)],
                            [a_r, hT_r], [pa_r])
                        P.group("pe", [lambda kt=kt, b=b, pb=pb, h0=h0, hn=hn: nc.tensor.matmul(
                            pb[:, :hn], b[:, kt * 128:(kt + 1) * 128], hT[:, kt, h0:h0 + hn],
                            start=(kt == 0), stop=(kt == DC - 1)) for kt in range(DC)],
                            [b_r, hT_r], [pb_r])
                        P.op("act", lambda pa=pa, sl=sl, hn=hn: nc.scalar.activation(out=sl[:, :hn], in_=pa[:, :hn], func=AF.Silu),
                             [pa_r], [sl_r])
                        P.op("dve", lambda pb=pb, sl=sl, fc=fc, h0=h0, hn=hn: nc.vector.tensor_tensor(
                            out=gT[:, fc, h0:h0 + hn], in0=sl[:, :hn], in1=pb[:, :hn], op=ALU.mult),
                            [sl_r, pb_r], [gT_r])

                def load_dn(dc):
                    w, w_r = wd[dc % 2]
                    hk = (FC // 2) * 128
                    P.dma("sp", w[:, :hk], w2b[f, dc, :, :hk], w2r, [w_r], w_r)
                    P.dma("sp", w[:, hk:], w2b[f, dc, :, hk:], w2r, [w_r], w_r)
                load_dn(0)
                for dc in range(DC):
                    if dc + 1 < DC:
                        load_dn(dc + 1)
                    w, w_r = wd[dc % 2]
                    for (h0, hn) in halves:
                        py, py_r = self.ps[it2 % 6], self.psr[it2 % 6]
                        xrt, xr_r = xr[it2 % 2]
                        yot, yo_r = yo[it2 % 2]
                        it2 += 1
                        P.dma("sp", xrt[:, :hn], XTv[:, dc, t0 + h0:t0 + h0 + hn], self.tr("XT", t0 + h0, hn), [xr_r], xr_r)
                        P.group("pe", [lambda kt=kt, w=w, py=py, h0=h0, hn=hn: nc.tensor.matmul(
                            py[:, :hn], w[:, kt * 128:(kt + 1) * 128], gT[:, kt, h0:h0 + hn],
                            start=(kt == 0), stop=(kt == FC - 1)) for kt in range(FC)],
                            [w_r, gT_r], [py_r])
                        P.op("dve", lambda py=py, xrt=xrt, yot=yot, dc=dc, hn=hn: nc.vector.scalar_tensor_tensor(
                            out=yot[:, :hn], in0=py[:, :hn], scalar=self.gate[:, slot, dc, s:s + 1], in1=xrt[:, :hn],
                            op0=ALU.mult, op1=ALU.add), [py_r, xr_r, self.gate_r], [yo_r])
                        P.dma("act", XTv[:, dc, t0 + h0:t0 + h0 + hn], yot[:, :hn], [yo_r], self.tr("XT", t0 + h0, hn), yo_r)
            P.barrier()
            P.release(ntl_res + [r for _, r in wa + wb + wd + xr + yo])

    def wcast(self, src2d, dst2d, K, name, rb=256, res=None):
        P = self.P
        if not hasattr(self, "wc_r"):
            self.wc_r = Res("wcast")
        for k0 in range(0, K, rb):
            P.dma("pool", dst2d[k0:k0 + rb, :], src2d[k0:k0 + rb, :], [], res or self.tr(name), self.wc_r)
        P.bg.add(self.wc_r.dkey)

    def lin_tiles(self, st, KT, NW=256, tag="l"):
        return [self.P.sb(st, "%sw%d" % (tag, i), [128, KT, NW], BF16) for i in range(3)]

    def lin_fm(self, wt, hT, hT_r, KT, wv, wname, cols, n, epi, NW=256, pbanks=(0, 1)):
        P, nc = self.P, self.nc
        c0, c1 = cols
        blocks = list(range(c0, c1, NW))
        nb = len(wt)
        hk = KT // 2

        def load(bi):
            w, w_r = wt[bi % nb]
            b0 = blocks[bi]
            P.dma("sp", w[:, :hk, :], wv[:, :hk, b0:b0 + NW], self.tr(wname), [w_r], w_r)
            P.dma("sp", w[:, hk:, :], wv[:, hk:, b0:b0 + NW], self.tr(wname), [w_r], w_r)
        for bi in range(min(nb - 1, len(blocks))):
            load(bi)
        it = 0
        for bi, b0 in enumerate(blocks):
            if bi + nb - 1 < len(blocks):
                load(bi + nb - 1)
            w, w_r = wt[bi % nb]
            for j in range(NW // 128):
                pb = pbanks[it % len(pbanks)]
                it += 1
                ps, ps_r = self.ps[pb], self.psr[pb]
                P.group("pe", [lambda kt=kt, w=w, j=j, ps=ps: nc.tensor.matmul(
                    ps[:, :n], w[:, kt, j * 128:(j + 1) * 128], hT[:, kt, :n],
                    start=(kt == 0), stop=(kt == KT - 1)) for kt in range(KT)],
                    [w_r, hT_r], [ps_r])
                epi(ps, ps_r, (b0 - c0) // 128 + j)

    def stage_attn(self, layer, ctx_out):
        cfg, P, nc = self.cfg, self.P, self.nc
        i = layer // 2
        A = self.ain
        self.wcast(A["w_in"][i], self.awb, D, "awb")
        self.wcast(A["w_rot"][i], self.arb, D, "arb")
        self.wcast(A["w_out"][i], self.aob, D, "aob")
        self.stage_attn_inproj(layer, i)
        self.stage_na(layer, i, ctx_out)
        self.stage_diff(layer, i, ctx_out)
        self.stage_outproj(self.CAT, "CAT", DC, self.aob.rearrange("(kt p) n -> p kt n", p=128), "aob", ctx_out)

    @_scoped
    def stage_attn_inproj(self, layer, i):
        cfg, P, nc = self.cfg, self.P, self.nc
        A = self.ain
        TB = 512
        wv = self.awb.rearrange("(kt p) n -> p kt n", p=128)
        rv = self.arb.rearrange("(kt p) n -> p kt n", p=128)
        with ExitStack() as st:
            ntl, ntl_res = self.norm_tiles(st, TB)
            hT, hT_r = P.sb(st, "ahT", [128, DC, TB], BF16)
            cosb, cos_r = P.sb(st, "cosb", [128, TB], F32)
            sinb, sin_r = P.sb(st, "sinb", [128, TB], F32)
            qo = [P.sb(st, "qo%d" % k, [128, TB], BF16) for k in range(3)]
            t1 = [P.sb(st, "rt1%d" % k, [128, TB], F32) for k in range(2)]
            t2 = [P.sb(st, "rt2%d" % k, [128, TB], F32) for k in range(2)]
            vw = [P.sb(st, "vw%d" % k, [128, DC, 512], BF16) for k in range(2)]
            vo = [P.sb(st, "vo%d" % k, [128, 512], BF16) for k in range(3)]
            wrot = [P.sb(st, "wrot%d" % k, [128, DC, 256], BF16) for k in range(2)]
            wlin = self.lin_tiles(st, DC, tag="ap")
            rel = [r for _, r in wlin]
            cnt = [0, 0, 0]
            for (t0, n, s) in blocks_of(cfg, TB):
                self.norm_mod(ntl, 1, t0, n, s, hT, hT_r)
                P.dma("sp", cosb[:, :n], A["ropec"][:, t0:t0 + n], self.tr("ropec"), [cos_r], cos_r)
                P.dma("sp", sinb[:, :n], A["ropes"][:, t0:t0 + n], self.tr("ropes"), [sin_r], sin_r)

                def epi_plain(ps, ps_r, j, t0=t0, n=n):
                    q, q_r = qo[cnt[0] % 3]
                    cnt[0] += 1
                    P.op("act", lambda: nc.scalar.copy(out=q[:, :n], in_=ps[:, :n]), [ps_r], [q_r])
                    P.dma("act", self.QKT[j, :, t0:t0 + n], q[:, :n], [q_r], self.tr("QKT"), q_r)
                self.lin_fm(wlin, hT, hT_r, DC, wv, "awb", (0, 2048), n, epi_plain)

                for (c0, r0, ch0) in ((3072, 0, 16), (4096, 1024, 24)):
                    for bi in range(4):
                        w, w_r = wrot[cnt[1] % 2]
                        P.dma("sp", w[:], rv[:, :, r0 + bi * 256:r0 + (bi + 1) * 256], self.tr("arb"), [w_r], w_r)
                        wq, wq_r = vw[cnt[1] % 2]
                        cnt[1] += 1
                        P.dma("sp", wq[:, :, 0:256], wv[:, :, c0 + bi * 256:c0 + (bi + 1) * 256], self.tr("awb"), [wq_r], wq_r)
                        for j in range(2):
                            kk = 2 + 2 * (j % 2)
                            pa, pa_r = self.ps[kk], self.psr[kk]
                            pb, pb_r = self.ps[kk + 1], self.psr[kk + 1]
                            P.group("pe", [lambda kt=kt, wq=wq, j=j, pa=pa: nc.tensor.matmul(
                                pa[:, :n], wq[:, kt, j * 128:(j + 1) * 128], hT[:, kt, :n],
                                start=(kt == 0), stop=(kt == DC - 1)) for kt in range(DC)], [wq_r, hT_r], [pa_r])
                            P.group("pe", [lambda kt=kt, w=w, j=j, pb=pb: nc.tensor.matmul(
                                pb[:, :n], w[:, kt, j * 128:(j + 1) * 128], hT[:, kt, :n],
                                start=(kt == 0), stop=(kt == DC - 1)) for kt in range(DC)], [w_r, hT_r], [pb_r])
                            a, a_r = t1[cnt[2] % 2]
                            b, b_r = t2[cnt[2] % 2]
                            cnt[2] += 1
                            q, q_r = qo[cnt[0] % 3]
                            cnt[0] += 1
                            P.op("dve", lambda a=a, pa=pa: nc.vector.tensor_tensor(out=a[:, :n], in0=pa[:, :n], in1=cosb[:, :n], op=ALU.mult),
                                 [pa_r, cos_r], [a_r])
                            P.op("dve", lambda b=b, pb=pb: nc.vector.tensor_tensor(out=b[:, :n], in0=pb[:, :n], in1=sinb[:, :n], op=ALU.mult),
                                 [pb_r, sin_r], [b_r])
                            P.op("pool", lambda a=a, b=b, q=q: nc.gpsimd.tensor_tensor(out=q[:, :n], in0=a[:, :n], in1=b[:, :n], op=ALU.add),
                                 [a_r, b_r], [q_r])
                            ch = ch0 + bi * 2 + j
                            P.dma("act", self.QKT[ch, :, t0:t0 + n], q[:, :n], [q_r], self.tr("QKT"), q_r)

                for (c0, o0) in ((2048, 0), (2560, 512), (5120, 1024), (5632, 1536)):
                    w, w_r = vw[cnt[1] % 2]
                    cnt[1] += 1
                    P.dma("sp", w[:, :8, :], wv[:, :8, c0:c0 + 512], self.tr("awb"), [w_r], w_r)
                    P.dma("sp", w[:, 8:, :], wv[:, 8:, c0:c0 + 512], self.tr("awb"), [w_r], w_r)
                    for tt in range(n // 128):
                        pv, pv_r = self.ps[4 + tt % 2], self.psr[4 + tt % 2]
                        P.group("pe", [lambda kt=kt, w=w, tt=tt, pv=pv: nc.tensor.matmul(
                            pv[:, :], hT[:, kt, tt * 128:(tt + 1) * 128], w[:, kt, :],
                            start=(kt == 0), stop=(kt == DC - 1)) for kt in range(DC)], [w_r, hT_r], [pv_r])
                        v, v_r = vo[cnt[0] % 3]
                        cnt[0] += 1
                        P.op("act", lambda v=v, pv=pv: nc.scalar.copy(out=v[:, :], in_=pv[:, :]), [pv_r], [v_r])
                        P.dma("act", self.VTM[t0 + tt * 128:t0 + (tt + 1) * 128, o0:o0 + 512], v[:, :], [v_r],
                              self.tr("VTM"), v_r)
            P.barrier()
            P.release(ntl_res + rel + [cos_r, sin_r] + [r for _, r in qo + vw + vo + wrot])

    @_scoped
    def stage_na(self, layer, i, ctx_out):
        cfg, P, nc = self.cfg, self.P, self.nc
        A = self.ain
        S, T, NT, rows = cfg.S, cfg.T, cfg.NT, cfg.rows
        kr = 8
        scale = HD ** -0.5
        CATv = self.CAT.rearrange("(c p) t -> c p t", p=128)
        with ExitStack() as st:
            cm, cm_r = P.sb(st, "cmask", [128, 4, 64], F32)
            P.dma("sp", cm[:], A["na_cmask"][:, :, :], self.tr("na_cmask"), [cm_r], cm_r)
            qT, q_r = P.sb(st, "naq", [128, S], BF16)
            kT, k_r = P.sb(st, "nak", [128, T], BF16)
            ve, ve_r = P.sb(st, "nave", [128, NT, 128], BF16)
            vod, vo_r = P.sb(st, "navo", [128, NT - 1, 128], BF16)
            Wm = [P.sb(st, "naW%d" % k, [128, 4, 64], F32) for k in range(8)]
            oT, o_r = P.sb(st, "naoT", [128, T], BF16)
            E = [P.sb(st, "naE%d" % k, [128, 256], F32) for k in range(2)]
            Pt = [P.sb(st, "naP%d" % k, [128, 384], BF16) for k in range(2)]
            rc = [P.sb(st, "narc%d" % k, [128, 64], F32) for k in range(2)]
            Pc, Pc_r = P.sb(st, "naPc", [128, 512], BF16)
            rcc, rcc_r = P.sb(st, "narcc", [128, 256], F32)
            cqT, cqT_r = P.sb(st, "nacq", [128, 256], BF16)
            for h in range(NA_H):
                P.dma("sp", qT[:], self.QKT[h, :, 0:S], self.tr("QKT"), [q_r], q_r)
                P.dma("sp", kT[:], self.QKT[8 + h, :, :], self.tr("QKT"), [k_r], k_r)
                P.dma("sp", ve[:], self.VTM[:, h * 128:(h + 1) * 128].rearrange("(j p) e -> p j e", p=128),
                      self.tr("VTM"), [ve_r], ve_r)
                P.dma("sp", vod[:], self.VTM[64:T - 64, h * 128:(h + 1) * 128].rearrange("(j p) e -> p j e", p=128),
                      self.tr("VTM"), [vo_r], vo_r)
                for dl in range(8):
                    W, W_r = Wm[dl]
                    dr0 = 7 - dl
                    src = A["na_bias"][i, h, dr0:dr0 + 8].rearrange("(a i2) kc qc -> (i2 kc) a qc", i2=2)
                    P.dma("sp", W[:], src, self.tr("na_bias"), [W_r], W_r)
                    P.op("act", lambda W=W: nc.scalar.activation(out=W[:], in_=W[:], func=AF.Exp), [W_r], [W_r])
                    P.op("dve", lambda W=W: nc.vector.tensor_tensor(out=W[:], in0=W[:], in1=cm[:], op=ALU.mult),
                         [W_r, cm_r], [W_r])
                def na_s(r):
                    rs = min(max(r - kr // 2, 0), rows - kr)
                    dl = r - rs
                    W, W_r = Wm[dl]
                    ps, ps_r = self.ps[r % 2], self.psr[r % 2]
                    Et, E_r = E[r % 2]
                    Pp, Pp_r = Pt[r % 2]
                    qs = qT[:, r * 64:(r + 1) * 64]
                    k0 = rs * 64
                    fns = [lambda a=a: nc.tensor.matmul(ps[:, a * 64:(a + 1) * 64], kT[:, k0 + a * 128:k0 + (a + 1) * 128], qs,
                                                        start=True, stop=True) for a in range(4)]
                    fns += [lambda a=a: nc.tensor.matmul(ps[:, (4 + a) * 64:(5 + a) * 64], kT[:, S + a * 128:S + (a + 1) * 128], qs,
                                                         start=True, stop=True) for a in range(2)]
                    P.group("pe", fns, [k_r, q_r], [ps_r])
                    P.op("act", lambda: nc.scalar.activation(out=Et[:, :], in_=ps[:, 0:256], func=AF.Exp, scale=scale),
                         [ps_r], [E_r])
                    P.op("act", lambda: nc.scalar.activation(out=Pp[:, 256:384], in_=ps[:, 256:384], func=AF.Exp, scale=scale),
                         [ps_r], [Pp_r])
                    P.op("dve", lambda: nc.vector.tensor_tensor(out=Pp[:, 0:256], in0=Et[:, :],
                                                                in1=W[:].rearrange("p a q -> p (a q)"), op=ALU.mult),
                         [E_r, W_r], [Pp_r])

                def na_pv(r):
                    rs = min(max(r - kr // 2, 0), rows - kr)
                    po, po_r = self.ps[2 + r % 2], self.psr[2 + r % 2]
                    Pp, Pp_r = Pt[r % 2]
                    rct, rc_r = rc[r % 2]
                    if rs % 2 == 0:
                        vt = [ve[:, rs // 2 + a, :] for a in range(4)]
                    else:
                        vt = [vod[:, (rs - 1) // 2 + a, :] for a in range(4)]
                    vt += [ve[:, S // 128 + a, :] for a in range(2)]
                    fns = [lambda a=a: nc.tensor.matmul(po[:, 0:64], vt[a], Pp[:, a * 64:(a + 1) * 64],
                                                        start=(a == 0), stop=(a == 5)) for a in range(6)]
                    fns += [lambda a=a: nc.tensor.matmul(po[:, 64:128], self.onesb[:], Pp[:, a * 64:(a + 1) * 64],
                                                         start=(a == 0), stop=(a == 5)) for a in range(6)]
                    P.group("pe", fns, [Pp_r, ve_r, vo_r, self.onesb_r], [po_r])
                    P.op("dve", lambda: nc.vector.reciprocal(out=rct[:, :], in_=po[:, 64:128]), [po_r], [rc_r])
                    P.op("dve", lambda: nc.vector.tensor_tensor(out=oT[:, r * 64:(r + 1) * 64], in0=po[:, 0:64], in1=rct[:, :],
                                                                op=ALU.mult), [po_r, rc_r], [o_r])

                na_s(0)
                for r in range(rows):
                    if r + 1 < rows:
                        na_s(r + 1)
                    na_pv(r)
                if ctx_out:
                    ps, ps_r = self.ps[4], self.psr[4]
                    po, po_r = self.ps[5], self.psr[5]
                    P.dma("sp", cqT[:], self.QKT[h, :, S:T], self.tr("QKT"), [cqT_r], cqT_r)
                    P.group("pe", [lambda a=a: nc.tensor.matmul(ps[:, a * 256:(a + 1) * 256], kT[:, S + a * 128:S + (a + 1) * 128],
                                                                cqT[:], start=True, stop=True) for a in range(2)],
                            [k_r, cqT_r], [ps_r])
                    P.op("act", lambda: nc.scalar.activation(out=Pc[:, :], in_=ps[:, :], func=AF.Exp, scale=scale), [ps_r], [Pc_r])
                    fns = [lambda a=a: nc.tensor.matmul(po[:, 0:256], ve[:, S // 128 + a, :], Pc[:, a * 256:(a + 1) * 256],
                                                        start=(a == 0), stop=(a == 1)) for a in range(2)]
                    fns += [lambda a=a: nc.tensor.matmul(po[:, 256:512], self.onesb[:], Pc[:, a * 256:(a + 1) * 256],
                                                         start=(a == 0), stop=(a == 1)) for a in range(2)]
                    P.group("pe", fns, [Pc_r, ve_r, self.onesb_r], [po_r])
                    P.op("dve", lambda: nc.vector.reciprocal(out=rcc[:, :], in_=po[:, 256:512]), [po_r], [rcc_r])
                    P.op("dve", lambda: nc.vector.tensor_tensor(out=oT[:, S:T], in0=po[:, 0:256], in1=rcc[:, :], op=ALU.mult),
                         [po_r, rcc_r], [o_r])
                nst = T if ctx_out else S
                P.dma("act", CATv[h, :, 0:nst], oT[:, 0:nst], [o_r], self.tr("CAT"), o_r)
            P.barrier()
            P.release([cm_r, q_r, k_r, ve_r, vo_r, o_r, cqT_r] + [r for _, r in Wm])

    @_scoped
    def stage_diff(self, layer, i, ctx_out):
        cfg, P, nc = self.cfg, self.P, self.nc
        A = self.ain
        S, T, NT = cfg.S, cfg.T, cfg.NT
        scale = HD ** -0.5
        lam_init = 0.8 - 0.6 * math.exp(-0.3 * layer)
        CATv = self.CAT.rearrange("(c p) t -> c p t", p=128)
        with ExitStack() as st:
            lt, lt_r = P.sb(st, "lamt", [128, 4], F32)
            sg, sg_r = P.sb(st, "subg", [128, 2], F32)
            lp, lp_r = P.sb(st, "lamp", [128, 2], F32)
            nl, nl_r = P.sb(st, "neglam", [128, 1], F32)
            P.dma("sp", lt[:], A["lamT"][i], self.tr("lamT"), [lt_r], lt_r)
            P.dma("sp", sg[:], A["subg"][i], self.tr("subg"), [sg_r], sg_r)
            P.op("dve", lambda: nc.vector.tensor_tensor(out=lp[:, 0:1], in0=lt[:, 0:1], in1=lt[:, 1:2], op=ALU.mult), [lt_r], [lp_r])
            P.op("dve", lambda: nc.vector.tensor_tensor(out=lp[:, 1:2], in0=lt[:, 2:3], in1=lt[:, 3:4], op=ALU.mult), [lt_r], [lp_r])
            ps, ps_r = self.ps[6], self.psr[6]
            P.group("pe", [lambda: nc.tensor.matmul(ps[:, 0:2], self.onesf[:], lp[:, :], start=True, stop=True)],
                    [lp_r, self.onesf_r], [ps_r])
            P.op("act", lambda: nc.scalar.activation(out=lp[:, :], in_=ps[:, 0:2], func=AF.Exp), [ps_r], [lp_r])
            P.op("dve", lambda: nc.vector.scalar_tensor_tensor(out=nl[:, :], in0=lp[:, 1:2], scalar=-lam_init, in1=lp[:, 0:1],
                                                               op0=ALU.add, op1=ALU.subtract), [lp_r], [nl_r])
            P.op("dve", lambda: nc.vector.tensor_scalar(out=sg[:, :], in0=sg[:, :], scalar1=1.0 - lam_init, scalar2=None, op0=ALU.mult),
                 [sg_r], [sg_r])
            qT = [P.sb(st, "dq%d" % m, [128, T], BF16) for m in range(2)]
            kT = [P.sb(st, "dk%d" % m, [128, T], BF16) for m in range(2)]
            v, v_r = P.sb(st, "dv", [128, NT, 256], BF16)
            Ering = [P.sb(st, "dE%d" % k, [128, 512], BF16) for k in range(4)]
            rec = [P.sb(st, "drec%d" % m, [128, 512], F32) for m in range(2)]
            oa = [P.sb(st, "doa%d" % e, [128, 512], F32) for e in range(2)]
            ob = [P.sb(st, "dob%d" % e, [128, 512], F32) for e in range(2)]
            sq, sq_r = P.sb(st, "dsq", [128, 512], F32)
            rstd, rstd_r = P.sb(st, "drstd", [128, 512], F32)
            oo = [P.sb(st, "doo%d" % e, [128, 512], BF16) for e in range(2)]
            qblocks = [(t0, n, s) for (t0, n, s) in blocks_of(cfg, 512) if s == 0 or ctx_out]
            for h in range(DF_H):
                for m in range(2):
                    P.dma("sp", qT[m][0][:], self.QKT[16 + 2 * h + m, :, :], self.tr("QKT"), [qT[m][1]], qT[m][1])
                    P.dma("sp", kT[m][0][:], self.QKT[24 + 2 * h + m, :, :], self.tr("QKT"), [kT[m][1]], kT[m][1])
                P.dma("sp", v[:], self.VTM[:, 1024 + h * 256:1024 + (h + 1) * 256].rearrange("(j p) e -> p j e", p=128),
                      self.tr("VTM"), [v_r], v_r)
                for (q0, nq, s) in qblocks:
                    ktiles = list(range(NT)) if s == 0 else list(range(S // 128, NT))
                    its = [(ki, kt, m) for ki, kt in enumerate(ktiles) for m in range(2)]

                    def emit_s(idx):
                        ki, kt, m = its[idx]
                        pss, pss_r = self.ps[6 + idx % 2], self.psr[6 + idx % 2]
                        Et, E_r = Ering[idx % 4]
                        P.group("pe", [lambda: nc.tensor.matmul(
                            pss[:, :nq], kT[m][0][:, kt * 128:(kt + 1) * 128], qT[m][0][:, q0:q0 + nq], start=True, stop=True)],
                            [kT[m][1], qT[m][1]], [pss_r])
                        P.op("act", lambda: nc.scalar.activation(out=Et[:, :nq], in_=pss[:, :nq], func=AF.Exp, scale=scale),
                             [pss_r], [E_r])

                    def emit_pv(idx):
                        ki, kt, m = its[idx]
                        Et, E_r = Ering[idx % 4]
                        first, last = (ki == 0), (ki == len(ktiles) - 1)
                        P.group("pe", [
                            lambda: nc.tensor.matmul(self.ps[2 * m][:, :nq], v[:, kt, 0:128], Et[:, :nq], start=first, stop=last),
                            lambda: nc.tensor.matmul(self.ps[2 * m + 1][:, :nq], v[:, kt, 128:256], Et[:, :nq], start=first, stop=last),
                            lambda: nc.tensor.matmul(self.ps[4 + m][:, :nq], self.onesb[:], Et[:, :nq], start=first, stop=last)],
                            [E_r, v_r, self.onesb_r], [self.psr[2 * m], self.psr[2 * m + 1], self.psr[4 + m]])

                    emit_s(0)
                    for idx in range(len(its)):
                        if idx + 1 < len(its):
                            emit_s(idx + 1)
                        emit_pv(idx)
                    P.op("dve", lambda: nc.vector.reciprocal(out=rec[0][0][:, :nq], in_=self.ps[4][:, :nq]), [self.psr[4]], [rec[0][1]])
                    P.op("dve", lambda: nc.vector.reciprocal(out=rec[1][0][:, :nq], in_=self.ps[5][:, :nq]), [self.psr[5]], [rec[1][1]])
                    P.op("dve", lambda: nc.vector.tensor_scalar(out=rec[1][0][:, :nq], in0=rec[1][0][:, :nq], scalar1=nl[:, 0:1],
                                                                scalar2=None, op0=ALU.mult), [rec[1][1], nl_r], [rec[1][1]])
                    for e in range(2):
                        P.op("dve", lambda e=e: nc.vector.tensor_tensor(out=oa[e][0][:, :nq], in0=self.ps[e][:, :nq], in1=rec[0][0][:, :nq],
                                                                        op=ALU.mult), [self.psr[e], rec[0][1]], [oa[e][1]])
                        P.op("dve", lambda e=e: nc.vector.tensor_tensor(out=ob[e][0][:, :nq], in0=self.ps[2 + e][:, :nq], in1=rec[1][0][:, :nq],
                                                                        op=ALU.mult), [self.psr[2 + e], rec[1][1]], [ob[e][1]])
                        P.op("pool", lambda e=e: nc.gpsimd.tensor_tensor(out=oa[e][0][:, :nq], in0=oa[e][0][:, :nq], in1=ob[e][0][:, :nq],
                                                                         op=ALU.add), [oa[e][1], ob[e][1]], [oa[e][1]])
                    pst, pst_r = self.ps[6], self.psr[6]
                    for e in range(2):
                        P.op("act", lambda e=e: nc.scalar.activation(out=sq[:, :nq], in_=oa[e][0][:, :nq], func=AF.Square), [oa[e][1]], [sq_r])
                        P.group("pe", [lambda e=e: nc.tensor.matmul(pst[:, :nq], self.onesf[:], sq[:, :nq], start=(e == 0), stop=(e == 1))],
                                [sq_r, self.onesf_r], [pst_r])
                    P.op("act", lambda: nc.scalar.activation(out=rstd[:, :nq], in_=pst[:, :nq], func=AF.Sqrt, bias=self.eps_t[:, 0:1],
                                                             scale=1.0 / 256), [pst_r, self.eps_r], [rstd_r])
                    P.op("dve", lambda: nc.vector.reciprocal(out=rstd[:, :nq], in_=rstd[:, :nq]), [rstd_r], [rstd_r])
                    for e in range(2):
                        P.op("dve", lambda e=e: nc.vector.scalar_tensor_tensor(
                            out=oo[e][0][:, :nq], in0=oa[e][0][:, :nq], scalar=sg[:, e:e + 1], in1=rstd[:, :nq],
                            op0=ALU.mult, op1=ALU.mult), [oa[e][1], sg_r, rstd_r], [oo[e][1]])
                        P.dma("act", CATv[8 + 2 * h + e, :, q0:q0 + nq], oo[e][0][:, :nq], [oo[e][1]], self.tr("CAT"), oo[e][1])
            P.barrier()
            P.release([lt_r, sg_r, v_r] + [r for _, r in qT + kT + oo])

    @_scoped
    def stage_outproj(self, ACT_T, aname, KT, wv, wname, ctx_out):
        cfg, P, nc = self.cfg, self.P, self.nc
        TB = 512
        XTv = self.XT.rearrange("(c p) t -> p c t", p=128)
        av = ACT_T.rearrange("(c p) t -> p c t", p=128)
        with ExitStack() as st:
            aT, aT_r = P.sb(st, "opa", [128, KT, TB], BF16)
            xr = [P.sb(st, "opx%d" % k, [128, TB], F32) for k in range(2)]
            yo = [P.sb(st, "opy%d" % k, [128, TB], F32) for k in range(2)]
            wlin = self.lin_tiles(st, KT, tag="op")
            rel = [r for _, r in wlin]
            cnt = [0]
            for (t0, n, s) in blocks_of(cfg, TB):
                if s == 1 and not ctx_out:
                    continue
                hk = KT // 2
                P.dma("sp", aT[:, :hk, :n], av[:, :hk, t0:t0 + n], self.tr(aname), [aT_r], aT_r)
                P.dma("sp", aT[:, hk:, :n], av[:, hk:, t0:t0 + n], self.tr(aname), [aT_r], aT_r)

                def epi(ps, ps_r, j, t0=t0, n=n, s=s):
                    xrt, xr_r = xr[cnt[0] % 2]
                    yot, yo_r = yo[cnt[0] % 2]
                    cnt[0] += 1
                    P.dma("sp", xrt[:, :n], XTv[:, j, t0:t0 + n], self.tr("XT", t0, n), [xr_r], xr_r)
                    P.op("dve", lambda: nc.vector.scalar_tensor_tensor(
                        out=yot[:, :n], in0=ps[:, :n], scalar=self.gate[:, 1, j, s:s + 1], in1=xrt[:, :n],
                        op0=ALU.mult, op1=ALU.add), [ps_r, xr_r, self.gate_r], [yo_r])
                    P.dma("act", XTv[:, j, t0:t0 + n], yot[:, :n], [yo_r], self.tr("XT", t0, n), yo_r)
                self.lin_fm(wlin, aT, aT_r, KT, wv, wname, (0, D), n, epi)
            P.barrier()
            P.release(rel + [aT_r] + [r for _, r in xr + yo])

    def lin_tm(self, wt, hT, hT_r, KT, wv, wname, cols, ntt, epi, pbanks=(4, 5)):
        P, nc = self.P, self.nc
        c0, c1 = cols
        blocks = list(range(c0, c1, 512))
        nb = len(wt)
        hk = KT // 2

        def load(bi):
            w, w_r = wt[bi % nb]
            b0 = blocks[bi]
            P.dma("sp", w[:, :hk, :], wv[:, :hk, b0:b0 + 512], self.tr(wname), [w_r], w_r)
            P.dma("sp", w[:, hk:, :], wv[:, hk:, b0:b0 + 512], self.tr(wname), [w_r], w_r)
        for bi in range(min(nb - 1, len(blocks))):
            load(bi)
        it = 0
        for bi, b0 in enumerate(blocks):
            if bi + nb - 1 < len(blocks):
                load(bi + nb - 1)
            w, w_r = wt[bi % nb]
            for tt in range(ntt):
                pb = pbanks[it % len(pbanks)]
                it += 1
                ps, ps_r = self.ps[pb], self.psr[pb]
                P.group("pe", [lambda kt=kt, w=w, tt=tt, ps=ps: nc.tensor.matmul(
                    ps[:, :], hT[:, kt, tt * 128:(tt + 1) * 128], w[:, kt, :],
                    start=(kt == 0), stop=(kt == KT - 1)) for kt in range(KT)], [w_r, hT_r], [ps_r])
                epi(ps, ps_r, bi, tt)

    def stage_ssd(self, layer, ctx_out):
        i = layer // 2
        A = self.sin
        self.wcast(A["w_in"][i], self.swb, D, "swb")
        self.wcast(A["w_out"][i], self.sob, SSM_INNER, "sob")
        self.stage_ssd_inproj(layer, i)
        self.stage_ssd_conv(layer, i)
        self.stage_ssd_dt(layer, i)
        self.stage_ssd_scan(layer, i, 0, ctx_out)
        self.stage_ssd_scan(layer, i, 1, ctx_out)
        self.stage_outproj(self.YT, "YT", SSM_INNER // 128, self.sob.rearrange("(kt p) n -> p kt n", p=128), "sob", ctx_out)

    @_scoped
    def stage_ssd_inproj(self, layer, i):
        cfg, P, nc = self.cfg, self.P, self.nc
        TB = 512
        wv = self.swb.rearrange("(kt p) n -> p kt n", p=128)
        with ExitStack() as st:
            ntl, ntl_res = self.norm_tiles(st, TB)
            hT, hT_r = P.sb(st, "shT", [128, DC, TB], BF16)
            wlin = self.lin_tiles(st, DC, tag="si")
            wtm = [P.sb(st, "stm%d" % k, [128, DC, 512], BF16) for k in range(2)]
            wdt, wdt_r = P.sb(st, "swdt", [128, DC, 128], BF16)
            zo = [P.sb(st, "szo%d" % k, [128, 512], F32) for k in range(3)]
            xo = [P.sb(st, "sxo%d" % k, [128, TB], F32) for k in range(3)]
            do = [P.sb(st, "sdo%d" % k, [128, 128], F32) for k in range(2)]
            cnt = [0, 0, 0]
            for (t0, n, s) in blocks_of(cfg, TB):
                self.norm_mod(ntl, 1, t0, n, s, hT, hT_r)

                def epi_z(ps, ps_r, cb, tt, t0=t0):
                    z, z_r = zo[cnt[0] % 3]
                    cnt[0] += 1
                    P.op("act", lambda: nc.scalar.activation(out=z[:, :], in_=ps[:, :], func=AF.Silu), [ps_r], [z_r])
                    P.dma("act", self.SZ[t0 + tt * 128:t0 + (tt + 1) * 128, cb * 512:(cb + 1) * 512], z[:, :], [z_r],
                          self.tr("SZ"), z_r)
                self.lin_tm(wtm, hT, hT_r, DC, wv, "swb", (0, SSM_INNER), n // 128, epi_z)

                def epi_x(ps, ps_r, j, t0=t0, n=n):
                    xx, x_r = xo[cnt[1] % 3]
                    cnt[1] += 1
                    P.op("act", lambda: nc.scalar.copy(out=xx[:, :n], in_=ps[:, :n]), [ps_r], [x_r])
                    P.dma("act", self.XBC[j, :, t0:t0 + n], xx[:, :n], [x_r], self.tr("XBC"), x_r)
                self.lin_fm(wlin, hT, hT_r, DC, wv, "swb", (SSM_INNER, SSM_INNER + SSM_CONV_CH), n, epi_x)

                P.dma("sp", wdt[:], wv[:, :, SSM_INNER + SSM_CONV_CH:SSM_IN_W], self.tr("swb"), [wdt_r], wdt_r)
                for tt in range(n // 128):
                    ps, ps_r = self.ps[6], self.psr[6]
                    P.group("pe", [lambda kt=kt, tt=tt: nc.tensor.matmul(
                        ps[:, 0:128], hT[:, kt, tt * 128:(tt + 1) * 128], wdt[:, kt, :],
                        start=(kt == 0), stop=(kt == DC - 1)) for kt in range(DC)], [wdt_r, hT_r], [ps_r])
                    dd, d_r = do[cnt[2] % 2]
                    cnt[2] += 1
                    P.op("dve", lambda dd=dd: nc.vector.tensor_copy(out=dd[:, :], in_=ps[:, 0:128]), [ps_r], [d_r])
                    P.dma("act", self.DTR[t0 + tt * 128:t0 + (tt + 1) * 128, :], dd[:, :], [d_r], self.tr("DTR"), d_r)
            P.barrier()
            P.release(ntl_res + [r for _, r in wlin + wtm + zo + xo + do] + [wdt_r])

    @_scoped
    def stage_ssd_conv(self, layer, i):
        cfg, P, nc = self.cfg, self.P, self.nc
        A = self.sin
        S, T, NT = cfg.S, cfg.T, cfg.NT
        with ExitStack() as st:
            cw, cw_r = P.sb(st, "cw", [128, 48, 4], F32)
            cb, cb_r = P.sb(st, "cb", [128, 48], F32)
            P.dma("sp", cw[:], A["conv_wT"][i], self.tr("conv_wT"), [cw_r], cw_r)
            P.dma("sp", cb[:], A["conv_bT"][i], self.tr("conv_bT"), [cb_r], cb_r)
            xin = [P.sb(st, "cxin%d" % k, [128, T], F32) for k in range(2)]
            acc = [P.sb(st, "cacc%d" % k, [128, T], F32) for k in range(2)]
            sf = [P.sb(st, "csf%d" % k, [128, T], F32) for k in range(2)]
            sbf = [P.sb(st, "csb%d" % k, [128, T], BF16) for k in range(2)]
            tmf = [P.sb(st, "ctmf%d" % k, [128, NT, 128], F32) for k in range(2)]
            tmb = [P.sb(st, "ctmb%d" % k, [128, NT, 128], BF16) for k in range(2)]
            for c in range(48):
                x, x_r = xin[c % 2]
                a, a_r = acc[c % 2]
                P.dma("sp", x[:], self.XBC[c, :, :], self.tr("XBC"), [x_r], x_r)
                for (lo, hi) in ((0, S), (S, T)):
                    P.op("act", lambda lo=lo, hi=hi: nc.scalar.activation(
                        out=a[:, lo:hi], in_=x[:, lo:hi], func=AF.Identity, bias=cb[:, c:c + 1], scale=cw[:, c, 1:2]),
                        [x_r, cw_r, cb_r], [a_r])
                    for (k, dlo, dhi, slo, shi) in ((0, lo + 1, hi, lo, hi - 1), (2, lo, hi - 1, lo + 1, hi), (3, lo, hi - 2, lo + 2, hi)):
                        P.op("dve", lambda k=k, dlo=dlo, dhi=dhi, slo=slo, shi=shi: nc.vector.scalar_tensor_tensor(
                            out=a[:, dlo:dhi], in0=x[:, slo:shi], scalar=cw[:, c, k:k + 1], in1=a[:, dlo:dhi],
                            op0=ALU.mult, op1=ALU.add), [x_r, a_r, cw_r], [a_r])
                if c < 32:
                    s_, s_r = sf[c % 2]
                    tm, tm_r = tmf[c % 2]
                    P.op("act", lambda: nc.scalar.activation(out=s_[:, :], in_=a[:, :], func=AF.Silu), [a_r], [s_r])
                    for q in range((NT + 3) // 4):
                        js = list(range(q * 4, min(q * 4 + 4, NT)))
                        pst, ps_r = self.ps[q % 4], self.psr[q % 4]
                        P.group("pe", [lambda j=j, pst=pst: nc.tensor.transpose(
                            out=pst[:, (j % 4) * 128:(j % 4 + 1) * 128], in_=s_[:, j * 128:(j + 1) * 128], identity=self.identf[:])
                            for j in js], [s_r, self.identf_r], [ps_r])
                        dst = tm[:, js[0]:js[-1] + 1, :]
                        src = pst[:, 0:len(js) * 128].rearrange("p (j e) -> p j e", e=128)
                        if q % 2 == 0:
                            P.op("dve", lambda dst=dst, src=src: nc.vector.tensor_copy(out=dst, in_=src), [ps_r], [tm_r])
                        else:
                            P.op("act", lambda dst=dst, src=src: nc.scalar.copy(out=dst, in_=src), [ps_r], [tm_r])
                    P.dma("act", self.XS[:, c * 128:(c + 1) * 128].rearrange("(j p) e -> p j e", p=128), tm[:], [tm_r],
                          self.tr("XS"), tm_r)
                else:
                    s_, s_r = sbf[c % 2]
                    P.op("act", lambda: nc.scalar.activation(out=s_[:, :], in_=a[:, :], func=AF.Silu), [a_r], [s_r])
                    P.dma("act", self.BCT[c - 32, :, :], s_[:, :], [s_r], self.tr("BCT"), s_r)
                    if c < 40:
                        tm, tm_r = tmb[c % 2]
                        for q in range((NT + 3) // 4):
                            js = list(range(q * 4, min(q * 4 + 4, NT)))
                            pst, ps_r = self.ps[4 + q % 4], self.psr[4 + q % 4]
                            pv = pst[:, :].bitcast(BF16)
                            P.group("pe", [lambda j=j, pv=pv: nc.tensor.transpose(
                                out=pv[:, (j % 4) * 128:(j % 4 + 1) * 128], in_=s_[:, j * 128:(j + 1) * 128], identity=self.identb[:])
                                for j in js], [s_r, self.identb_r], [ps_r])
                            dst = tm[:, js[0]:js[-1] + 1, :]
                            src = pv[:, 0:len(js) * 128].rearrange("p (j e) -> p j e", e=128)
                            P.op("dve", lambda dst=dst, src=src: nc.vector.tensor_copy(out=dst, in_=src), [ps_r], [tm_r])
                        g = c - 32
                        P.dma("act", self.BTM[:, g * 128:(g + 1) * 128].rearrange("(j p) e -> p j e", p=128), tm[:], [tm_r],
                              self.tr("BTM"), tm_r)
            P.barrier()
            P.release([cw_r, cb_r] + [r for _, r in xin + acc + sf + sbf + tmf + tmb])

    @_scoped
    def stage_ssd_dt(self, layer, i):
        cfg, P, nc = self.cfg, self.P, self.nc
        A = self.sin
        NT = cfg.NT
        with ExitStack() as st:
            x, x_r = P.sb(st, "dtx", [128, NT, 128], F32)
            ax, ax_r = P.sb(st, "dtax", [128, NT, 128], F32)
            bi, bi_r = P.sb(st, "dtb", [128, 128], F32)
            al, al_r = P.sb(st, "dtal", [128, 128], F32)
            P.dma("sp", x[:], self.DTR.rearrange("(j p) h -> p j h", p=128), self.tr("DTR"), [x_r], x_r)
            P.dma("sp", bi[:], A["dt_bias"][i], self.tr("ssm_dt_bias"), [bi_r], bi_r)
            P.dma("sp", al[:], A["a_log"][i], self.tr("ssm_a_log"), [al_r], al_r)
            bb = bi[:, :].unsqueeze(1).to_broadcast([128, NT, 128])
            P.op("dve", lambda: nc.vector.tensor_tensor(out=x[:], in0=x[:], in1=bb, op=ALU.add), [x_r, bi_r], [x_r])
            P.op("dve", lambda: nc.vector.scalar_tensor_tensor(out=ax[:], in0=x[:], scalar=-1.0, in1=x[:], op0=ALU.mult, op1=ALU.min),
                 [x_r], [ax_r])
            P.op("act", lambda: nc.scalar.activation(out=ax[:], in_=ax[:], func=AF.Exp), [ax_r], [ax_r])
            P.op("act", lambda: nc.scalar.activation(out=ax[:], in_=ax[:], func=AF.Ln, bias=self.onesf[:, 0:1], scale=1.0),
                 [ax_r, self.onesf_r], [ax_r])
            P.op("dve", lambda: nc.vector.scalar_tensor_tensor(out=x[:], in0=x[:], scalar=0.0, in1=ax[:], op0=ALU.max, op1=ALU.add),
                 [x_r, ax_r], [x_r])
            P.op("act", lambda: nc.scalar.activation(out=al[:], in_=al[:], func=AF.Exp), [al_r], [al_r])
            P.op("dve", lambda: nc.vector.scalar_tensor_tensor(out=ax[:], in0=x[:], scalar=-1.0,
                                                               in1=al[:, :].unsqueeze(1).to_broadcast([128, NT, 128]),
                                                               op0=ALU.mult, op1=ALU.mult), [x_r, al_r], [ax_r])
            P.dma("act", self.DTA[:, 0, :].rearrange("(j p) h -> p j h", p=128), x[:], [x_r], self.tr("DTA"), x_r)
            P.dma("act", self.DTA[:, 1, :].rearrange("(j p) h -> p j h", p=128), ax[:], [ax_r], self.tr("DTA"), ax_r)
            P.barrier()
            P.release([x_r, ax_r, bi_r, al_r])

    @_scoped
    def stage_ssd_scan(self, layer, i, d, ctx_out):
        cfg, P, nc = self.cfg, self.P, self.nc
        A = self.sin
        S, T, NT = cfg.S, cfg.T, cfg.NT
        G = SSM_G
        nlat = S // 128
        if d == 0:
            order = list(range(nlat, NT)) + list(range(nlat))
        else:
            order = list(range(NT - 1, nlat - 1, -1)) + list(range(nlat - 1, -1, -1))
        with ExitStack() as st:
            mk, mk_r = P.sb(st, "smask", [128, 4, 128], F32)
            P.dma("sp", mk[:], A["masks"][:, :, :], self.tr("ssm_masks"), [mk_r], mk_r)
            m_le, m_gt = (mk[:, 0, :], mk[:, 1, :]) if d == 0 else (mk[:, 2, :], mk[:, 3, :])
            xsb = [P.sb(st, "sxs%d" % k, [128, 64, 64], F32) for k in range(2)]
            yac, yac_r = P.sb(st, "syac", [128, SSM_INNER], F32)
            xdt, xdt_r = P.sb(st, "sxdt", [128, 64, 64], BF16)
            xds, xds_r = P.sb(st, "sxds", [128, 64, 64], BF16)
            dta, dta_r = P.sb(st, "sdta", [128, 2, 128], F32)
            eq, eq_r = P.sb(st, "seq", [128, 192], F32)
            dd, dd_r = P.sb(st, "sdd", [128, 64], F32)
            btm, btm_r = P.sb(st, "sbtm", [128, G * 128], BF16)
            bct, bct_r = P.sb(st, "sbct", [128, 16, 128], BF16)
            cbm = [P.sb(st, "scbm%d" % k, [128, 128], F32) for k in range(2)]
            Rt = [P.sb(st, "sR%d" % k, [128, 8, 128], F32) for k in range(2)]
            Lh = [P.sb(st, "sLh%d" % k, [128, 8, 128], F32) for k in range(2)]
            Mt = [P.sb(st, "sM%d" % k, [128, 8, 128], BF16) for k in range(2)]
            tmp = [P.sb(st, "stmp%d" % k, [128, 8, 64], F32) for k in range(2)]
            hf, hf_r = P.sb(st, "shf", [128, G, 512], F32)
            hb = [P.sb(st, "shb%d" % g, [128, 512], BF16) for g in range(G)]
            P.op("dve", lambda: nc.vector.memset(hf[:], 0.0), [], [hf_r])
            for g in range(G):
                P.op("pool", lambda g=g: nc.gpsimd.memset(hb[g][0][:], 0.0), [], [hb[g][1]])
            if d == 1:
                yfb = [P.sb(st, "syf%d" % k, [128, SSM_INNER], F32) for k in range(2)]
                szb = [P.sb(st, "ssz%d" % k, [128, SSM_INNER], F32) for k in range(2)]
                dsk, dsk_r = P.sb(st, "sdsk", [128, 128], F32)
                ngt, ng_r = P.sb(st, "sng", [128, 32], F32)
                ss, ss_r = P.sb(st, "sss", [128, 8], F32)
                ytr, ytr_r = P.sb(st, "sytr", [128, 32, 128], BF16)
                P.dma("sp", dsk[:], A["d_skip"][i], self.tr("ssm_d"), [dsk_r], dsk_r)
                P.dma("sp", ngt[:], A["norm_gT"][i], self.tr("ssm_norm_gT"), [ng_r], ng_r)
                P.op("dve", lambda: nc.vector.tensor_tensor(out=dsk[:, 0:64], in0=dsk[:, 0:64], in1=dsk[:, 64:128], op=ALU.add),
                     [dsk_r], [dsk_r])
            YTv = self.YT.rearrange("(c p) t -> p c t", p=128)
            def scan_loads(oi):
                j = order[oi]
                t0 = j * 128
                want_y = (j < nlat) or ctx_out
                xs, xs_r = xsb[oi % 2]
                P.dma("sp", xs[:].rearrange("p h e -> p (h e)"), self.XS[t0:t0 + 128, :], self.tr("XS"), [xs_r], xs_r)
                if d == 1 and want_y:
                    yf, yf_r = yfb[oi % 2]
                    sz, sz_r = szb[oi % 2]
                    P.dma("sp", yf[:], self.YF[t0:t0 + 128, :], self.tr("YF"), [yf_r], yf_r)
                    P.dma("sp", sz[:], self.SZ[t0:t0 + 128, :], self.tr("SZ"), [sz_r], sz_r)
            scan_loads(0)
            for oi, j in enumerate(order):
                t0 = j * 128
                is_ctx = j >= nlat
                want_y = (not is_ctx) or ctx_out
                xs, xs_r = xsb[oi % 2]
                if d == 1:
                    yf, yf_r = yfb[oi % 2]
                    sz, sz_r = szb[oi % 2]
                if oi + 1 < len(order):
                    scan_loads(oi + 1)
                P.dma("sp", dta[:], self.DTA[t0:t0 + 128, :, :], self.tr("DTA"), [dta_r], dta_r)
                P.dma("sp", btm[:], self.BTM[t0:t0 + 128, :], self.tr("BTM"), [btm_r], btm_r)
                P.dma("sp", bct[:], self.BCT[:, :, t0:t0 + 128].rearrange("c n t -> n c t"), self.tr("BCT"), [bct_r], bct_r)
                dt_d = dta[:, 0, d * 64:(d + 1) * 64]
                a_d = dta[:, 1, d * 64:(d + 1) * 64]
                pc, pc_r = self.ps[7], self.psr[7]
                P.group("pe", [
                    lambda: nc.tensor.matmul(pc[:, 0:64], m_le, a_d, start=True, stop=True),
                    lambda: nc.tensor.matmul(pc[:, 64:128], m_gt, a_d, start=True, stop=True),
                    lambda: nc.tensor.matmul(pc[:, 128:192], self.onesf[:], a_d, start=True, stop=True)],
                    [mk_r, dta_r, self.onesf_r], [pc_r])
                P.op("act", lambda: nc.scalar.activation(out=eq[:, :], in_=pc[:, 0:192], func=AF.Exp), [pc_r], [eq_r])
                P.op("dve", lambda: nc.vector.tensor_tensor(out=dd[:, :], in0=dt_d, in1=eq[:, 64:128], op=ALU.mult), [dta_r, eq_r], [dd_r])
                if want_y:
                    P.op("dve", lambda: nc.vector.tensor_tensor(out=xdt[:], in0=xs[:], in1=dt_d.unsqueeze(2).to_broadcast([128, 64, 64]),
                                                                op=ALU.mult), [xs_r, dta_r], [xdt_r])
                P.op("pool", lambda: nc.gpsimd.tensor_tensor(out=xds[:], in0=xs[:], in1=dd[:, :].unsqueeze(2).to_broadcast([128, 64, 64]),
                                                             op=ALU.mult), [xs_r, dd_r], [xds_r])
                def stepA(g):
                    k2 = g % 2
                    pcb, pcb_r = self.ps[7], self.psr[7]
                    P.group("pe", [lambda: nc.tensor.matmul(pcb[:, 256:384], bct[:, g, :], bct[:, 8 + g, :], start=True, stop=True)],
                            [bct_r], [pcb_r])
                    cm, cm_r = cbm[k2]
                    P.op("dve", lambda: nc.vector.tensor_tensor(out=cm[:, :], in0=pcb[:, 256:384], in1=m_le, op=ALU.mult),
                         [pcb_r, mk_r], [cm_r])
                    R, R_r = Rt[k2]
                    P.op("pool", lambda: nc.gpsimd.tensor_tensor(
                        out=R[:], in0=a_d[:, g * 8:(g + 1) * 8].unsqueeze(2).to_broadcast([128, 8, 128]),
                        in1=m_le.unsqueeze(1).to_broadcast([128, 8, 128]), op=ALU.mult), [dta_r, mk_r], [R_r])
                    L, L_r = Lh[k2]
                    for hh in range(2):
                        pd, pd_r = self.ps[1 + hh], self.psr[1 + hh]
                        P.group("pe", [lambda hh=hh, pd=pd: nc.tensor.matmul(
                            pd[:, :], m_gt, R[:, hh * 4:(hh + 1) * 4, :].rearrange("p a l -> p (a l)"), start=True, stop=True)],
                            [mk_r, R_r], [pd_r])
                        P.op("act", lambda hh=hh, pd=pd: nc.scalar.activation(
                            out=L[:, hh * 4:(hh + 1) * 4, :].rearrange("p a l -> p (a l)"), in_=pd[:, :], func=AF.Exp), [pd_r], [L_r])
                    M, M_r = Mt[k2]
                    P.op("dve", lambda: nc.vector.tensor_tensor(
                        out=M[:], in0=L[:], in1=cm[:, :].unsqueeze(1).to_broadcast([128, 8, 128]), op=ALU.mult),
                        [L_r, cm_r], [M_r])

                def stepB(g):
                    k2 = g % 2
                    if want_y:
                        M, M_r = Mt[k2]
                        pyd, pyd_r = (self.ps[3], self.psr[3]) if k2 == 0 else (self.ps[0], self.psr[0])
                        P.group("pe", [lambda jh=jh: nc.tensor.matmul(
                            pyd[:, jh * 64:(jh + 1) * 64], M[:, jh, :], xdt[:, g * 8 + jh, :], start=True, stop=True) for jh in range(8)],
                            [M_r, xdt_r], [pyd_r])
                        pyo, pyo_r = self.ps[4], self.psr[4]
                        P.group("pe", [lambda: nc.tensor.matmul(pyo[:, :], bct[:, 8 + g, :], hb[g][0][:, :], start=True, stop=True)],
                                [bct_r, hb[g][1]], [pyo_r])
                        tp, tp_r = tmp[k2]
                        P.op("dve", lambda: nc.vector.tensor_tensor(
                            out=tp[:], in0=pyo[:, :].rearrange("p (a e) -> p a e", e=64),
                            in1=eq[:, g * 8:(g + 1) * 8].unsqueeze(2).to_broadcast([128, 8, 64]), op=ALU.mult), [pyo_r, eq_r], [tp_r])
                        P.op("dve", lambda: nc.vector.tensor_tensor(
                            out=yac[:, g * 512:(g + 1) * 512], in0=pyd[:, :], in1=tp[:].rearrange("p a e -> p (a e)"), op=ALU.add),
                            [pyd_r, tp_r], [yac_r])
                    pst, pst_r = self.ps[5 + g % 2], self.psr[5 + g % 2]
                    P.group("pe", [lambda: nc.tensor.matmul(
                        pst[:, :], btm[:, g * 128:(g + 1) * 128], xds[:, g * 8:(g + 1) * 8, :].rearrange("p a e -> p (a e)"),
                        start=True, stop=True)], [btm_r, xds_r], [pst_r])
                    hv = hf[:, g, :].rearrange("p (a e) -> p a e", e=64)
                    P.op("pool", lambda: nc.gpsimd.tensor_tensor(
                        out=hv, in0=hv, in1=eq[:, 128 + g * 8:128 + (g + 1) * 8].unsqueeze(2).to_broadcast([128, 8, 64]), op=ALU.mult),
                        [hf_r, eq_r], [hf_r])
                    P.op("dve", lambda: nc.vector.tensor_tensor(out=hf[:, g, :], in0=hf[:, g, :], in1=pst[:, :], op=ALU.add),
                         [hf_r, pst_r], [hf_r])
                    P.op("act", lambda: nc.scalar.copy(out=hb[g][0][:, :], in_=hf[:, g, :]), [hf_r], [hb[g][1]])

                if want_y:
                    stepA(0)
                for g in range(G):
                    if want_y and g + 1 < G:
                        stepA(g + 1)
                    stepB(g)
                if not want_y:
                    continue
                if d == 0:
                    P.dma("act", self.YF[t0:t0 + 128, :], yac[:, :], [yac_r], self.tr("YF"), yac_r)
                    continue
                P.op("dve", lambda: nc.vector.tensor_tensor(out=yac[:, :], in0=yac[:, :], in1=yf[:, :], op=ALU.add), [yac_r, yf_r], [yac_r])
                P.op("pool", lambda: nc.gpsimd.tensor_tensor(out=yf[:, :].rearrange("p (h e) -> p h e", e=64), in0=xs[:],
                                                             in1=dsk[:, 0:64].unsqueeze(2).to_broadcast([128, 64, 64]), op=ALU.mult),
                     [xs_r, dsk_r], [yf_r])
                P.op("dve", lambda: nc.vector.tensor_tensor(out=yac[:, :], in0=yac[:, :], in1=yf[:, :], op=ALU.add), [yac_r, yf_r], [yac_r])
                P.op("dve", lambda: nc.vector.tensor_tensor(out=yac[:, :], in0=yac[:, :], in1=sz[:, :], op=ALU.mult), [yac_r, sz_r], [yac_r])
                for g in range(G):
                    P.op("act", lambda g=g: nc.scalar.activation(out=yf[:, g * 512:(g + 1) * 512], in_=yac[:, g * 512:(g + 1) * 512],
                                                                 func=AF.Square, accum_out=ss[:, g:g + 1]), [yac_r], [yf_r, ss_r])
                P.op("act", lambda: nc.scalar.activation(out=ss[:, :], in_=ss[:, :], func=AF.Sqrt, bias=self.eps_t[:, 0:1], scale=1.0 / 512),
                     [ss_r, self.eps_r], [ss_r])
                P.op("dve", lambda: nc.vector.reciprocal(out=ss[:, :], in_=ss[:, :]), [ss_r], [ss_r])
                P.op("dve", lambda: nc.vector.tensor_tensor(out=yac[:, :].rearrange("p (g e) -> p g e", e=512),
                                                            in0=yac[:, :].rearrange("p (g e) -> p g e", e=512),
                                                            in1=ss[:, :].unsqueeze(2).to_broadcast([128, 8, 512]), op=ALU.mult),
                     [yac_r, ss_r], [yac_r])
                for q in range(8):
                    pst, ps_r = self.ps[q % 2], self.psr[q % 2]
                    P.group("pe", [lambda c=c, pst=pst: nc.tensor.transpose(
                        out=pst[:, (c % 4) * 128:(c % 4 + 1) * 128], in_=yac[:, c * 128:(c + 1) * 128], identity=self.identf[:])
                        for c in range(q * 4, q * 4 + 4)], [yac_r, self.identf_r], [ps_r])
                    for c in range(q * 4, q * 4 + 4):
                        P.op("act", lambda c=c, pst=pst: nc.scalar.activation(
                            out=ytr[:, c, :], in_=pst[:, (c % 4) * 128:(c % 4 + 1) * 128], func=AF.Copy, scale=ngt[:, c:c + 1]),
                            [ps_r, ng_r], [ytr_r])
                P.dma("act", YTv[:, :, t0:t0 + 128], ytr[:], [ytr_r], self.tr("YT"), ytr_r)
            P.barrier()
            rel = [mk_r, yac_r, dta_r, btm_r, bct_r] + [r for _, r in xsb]
            if d == 1:
                rel += [dsk_r, ng_r, ytr_r] + [r for _, r in yfb + szb]
            P.release(rel)

    @_scoped
    def stage_final(self, fin_g):
        cfg, P, nc = self.cfg, self.P, self.nc
        XTv = self.XT.rearrange("(c p) t -> p c t", p=128)
        with ExitStack() as st:
            g, g_r = P.sb(st, "fing", [128, D], F32)
            P.dma("sp", g[:], fin_g[:, :], self.tr("fin_g"), [g_r], g_r)
            xin = [P.sb(st, "fx%d" % i, [128, DC, 128], F32) for i in range(2)]
            xtm = [P.sb(st, "ft%d" % i, [128, D], F32) for i in range(2)]
            junk, junk_r = P.sb(st, "fjunk", [128, D], F32)
            ssq = [P.sb(st, "fss%d" % i, [128, 1], F32) for i in range(2)]
            for i in range(cfg.S // 128):
                t0 = i * 128
                xi, xi_r = xin[i % 2]
                xt, xt_r = xtm[i % 2]
                ss, ss_r = ssq[i % 2]
                P.dma("sp", xi[:], XTv[:, :, t0:t0 + 128], self.tr("XT", t0, 128), [xi_r], xi_r)
                for q in range(DC // 4):
                    pb = (i * (DC // 4) + q) % 8
                    pst, psr = self.ps[pb], self.psr[pb]
                    P.group("pe", [
                        (lambda c=c, pst=pst: nc.tensor.transpose(
                            out=pst[:, (c % 4) * 128:(c % 4 + 1) * 128],
                            in_=xi[:, c, :], identity=self.identf[:]))
                        for c in range(q * 4, q * 4 + 4)], [xi_r, self.identf_r], [psr])
                    P.op("dve", lambda q=q, pst=pst: nc.vector.tensor_copy(out=xt[:, q * 512:(q + 1) * 512], in_=pst[:, :]),
                         [psr], [xt_r])
                P.op("act", lambda: nc.scalar.activation(out=junk[:], in_=xt[:], func=AF.Square, accum_out=ss[:, 0:1]),
                     [xt_r], [junk_r, ss_r])
                P.op("act", lambda: nc.scalar.activation(out=ss[:], in_=ss[:], func=AF.Sqrt, bias=self.eps_t[:, 0:1], scale=1.0 / D),
                     [ss_r, self.eps_r], [ss_r])
                P.op("dve", lambda: nc.vector.reciprocal(out=ss[:], in_=ss[:]), [ss_r], [ss_r])
                P.op("dve", lambda: nc.vector.scalar_tensor_tensor(
                    out=xt[:], in0=xt[:], scalar=ss[:, 0:1], in1=g[:], op0=ALU.mult, op1=ALU.mult),
                    [xt_r, ss_r, g_r], [xt_r])
                P.dma("act", self.out[t0:t0 + 128, :], xt[:], [xt_r], self.tr("out"), xt_r)
            P.barrier()
            P.release([g_r] + [r for _, r in xin + xtm + ssq])


def pmajor(v):
    sh = v.shape[:-1]
    n = v.shape[-1] // 128
    return np.ascontiguousarray(np.swapaxes(v.reshape(sh + (n, 128)), -1, -2))


def pretile(w):
    lead = w.shape[:-2]
    K, N = w.shape[-2:]
    v = w.reshape(lead + (K // 128, 128, N // 128, 128))
    nl = len(lead)
    v = v.transpose(tuple(range(nl)) + (nl + 2, nl + 1, nl + 0, nl + 3))
    return np.ascontiguousarray(v).reshape(lead + (N, K))


def make_in_maps(cfg, inputs, n_cores):
    f = lambda a: np.ascontiguousarray(np.asarray(a, dtype=np.float32))
    depth = cfg.depth
    shared = {
        "mod_w": f(inputs["mod_w"][:depth]),
        "mod_bT": pmajor(f(inputs["mod_b"][:depth])),
        "norm_gT": pmajor(f(inputs["norm_g"][:depth]).reshape(depth, 3 * D)),
        "ffn_w1": pretile(f(inputs["ffn_w1"][:depth])),
        "ffn_w3": pretile(f(inputs["ffn_w3"][:depth])),
        "ffn_w2": pretile(f(inputs["ffn_w2"][:depth])),
        "fin_g": np.ascontiguousarray(np.broadcast_to(f(inputs["final_norm_g"])[None, :], (128, D))),
        "ident": np.eye(128, dtype=np.float32),
    }
    n_even = (depth + 1) // 2
    if n_even:
        w_in = f(inputs["attn_w_in"][:n_even])
        d = np.arange(128)
        perm = np.where(d % 64 < 32, d + 32, d - 32)
        sign = np.where(d % 64 < 32, -1.0, 1.0).astype(np.float32)
        cols = (np.arange(8)[:, None] * 128 + perm[None, :]).reshape(-1)
        shared["attn_w_in"] = w_in
        shared["attn_w_rot"] = np.ascontiguousarray(
            np.concatenate([w_in[:, :, 3072:4096][:, :, cols], w_in[:, :, 4096:5120][:, :, cols]], axis=-1))
        shared["attn_w_out"] = f(inputs["attn_w_out"][:n_even])
        rpb = f(inputs["na_rpb"][:n_even])
        kc = np.arange(64)[:, None]
        qc = np.arange(64)[None, :]
        coff = np.clip(kc - qc, -15, 15) + 15
        shared["na_bias"] = np.ascontiguousarray(rpb[:, :, :, coff])
        cs = np.clip(qc - 8, 0, 64 - 16)
        cmask = ((kc >= cs) & (kc < cs + 16)).astype(np.float32)
        shared["na_cmask"] = np.ascontiguousarray(
            np.broadcast_to(cmask[None, :, None, :], (2, 64, 4, 64)).reshape(128, 4, 64))
        shared["lamT"] = np.ascontiguousarray(np.swapaxes(f(inputs["diff_lambda"][:n_even]), 1, 2))
        shared["subg"] = pmajor(f(inputs["diff_subln_g"][:n_even]))
        quarter = 32
        inv = (1.0 / (10000.0 ** (np.arange(quarter, dtype=np.float32) / quarter))).astype(np.float32)
        t = np.arange(cfg.S)
        row = (t // GRID_W).astype(np.float32)[:, None] * inv
        col = (t % GRID_W).astype(np.float32)[:, None] * inv
        ang = np.concatenate([row, row, col, col], axis=-1).astype(np.float32)
        cosT = np.ones((128, cfg.T), np.float32)
        sinT = np.zeros((128, cfg.T), np.float32)
        cosT[:, :cfg.S] = np.cos(ang).T
        sinT[:, :cfg.S] = np.sin(ang).T * sign[:, None]
        shared["ropec"] = cosT
        shared["ropes"] = sinT
    n_odd = depth // 2
    if n_odd:
        rep = lambda a: np.ascontiguousarray(np.broadcast_to(a.reshape(n_odd, 1, 128), (n_odd, 128, 128)))
        shared["ssm_w_in"] = f(inputs["ssm_w_in"][:n_odd])
        shared["ssm_w_out"] = f(inputs["ssm_w_out"][:n_odd])
        cwt = f(inputs["ssm_conv_w"][:n_odd])
        shared["conv_wT"] = np.ascontiguousarray(cwt.reshape(n_odd, 4, 48, 128).transpose(0, 3, 2, 1))
        shared["conv_bT"] = pmajor(f(inputs["ssm_conv_b"][:n_odd]))
        shared["ssm_a_log"] = rep(f(inputs["ssm_a_log"][:n_odd]))
        shared["ssm_dt_bias"] = rep(f(inputs["ssm_dt_bias"][:n_odd]))
        shared["ssm_d"] = rep(f(inputs["ssm_d"][:n_odd]))
        shared["ssm_norm_gT"] = pmajor(f(inputs["ssm_norm_g"][:n_odd]))
        u = np.arange(128)[:, None]
        l = np.arange(128)[None, :]
        shared["ssm_masks"] = np.ascontiguousarray(
            np.stack([u <= l, u > l, u >= l, u < l], axis=1).astype(np.float32))
    cc = pmajor(f(inputs["c_ctx"]))
    maps = []
    for b in range(n_cores):
        m = dict(shared)
        m["x"] = f(inputs["x"][b])
        m["ctx"] = f(inputs["ctx"][b])
        cb = pmajor(f(inputs["c"][b]))
        m["cT"] = np.ascontiguousarray(np.stack([cb, cc], axis=-1))
        maps.append(m)
    return maps


_CACHE = {}


def run(cfg, inputs, n_cores, mixers=True, trace=False):
    import time as _t
    t0 = _t.time()
    b = Builder(cfg, mixers)
    nc = b.build()
    print("[kernel] build %.1fs n_ins=%d n_wait=%d" % (_t.time() - t0, b.P.n_ins, b.P.n_wait), flush=True)
    t0 = _t.time()
    maps = make_in_maps(cfg, inputs, n_cores)
    print("[kernel] layout %.1fs" % (_t.time() - t0), flush=True)
    t0 = _t.time()
    if trace:
        res = run_bass_kernel_spmd(nc, maps, core_ids=list(range(n_cores)), trace=True)
        print("[kernel] exec_time_ns", res.exec_time_ns, flush=True)
        if res.per_core_scope_times:
            for k in sorted(res.per_core_scope_times, key=lambda k: k.split("_")[-1]):
                print("[scope] %-28s %s" % (k, res.per_core_scope_times[k]), flush=True)
    else:
        res = run_bass_kernel_spmd(nc, maps, core_ids=list(range(n_cores)))
    print("[kernel] run %.1fs" % (_t.time() - t0), flush=True)
    return np.stack([np.asarray(r["out"]) for r in res.results], axis=0)


def kernel(**inputs):
    cfg = Cfg(seq=4096, depth=4)
    return run(cfg, inputs, 8).astype(np.float32)
```
